# Optimizing a Trainium2 kernel written in Bass

```python
import math
import jax, jax.numpy as jnp
from jax import lax
import numpy as np

D_MODEL = 1024
BATCH = 4
SEQ = 8192
DEPTH = 2

EPS = 1e-6

SSD_EXPAND = 2
D_INNER = SSD_EXPAND * D_MODEL
SSD_HEAD_DIM = 64
SSD_HEADS = D_INNER // SSD_HEAD_DIM
SSD_GROUPS = 8
SSD_HPG = SSD_HEADS // SSD_GROUPS
D_STATE = 128
CONV_K = 4
CONV_DIM = D_INNER + 2 * SSD_GROUPS * D_STATE
CHUNK = 128

ATT_PATTERNS = ((128, 1), (512, 4), (2048, 16))
ATT_GROUPS = 3
ATT_HEADS = 16
ATT_HEAD_DIM = 64
ATT_W = ATT_HEADS * ATT_HEAD_DIM
ATT_BLOCK = 128

MEM_LEN = 256
MEM_HEADS = 4
MEM_HEAD_DIM = D_MODEL // MEM_HEADS
MEM_W = MEM_HEADS * MEM_HEAD_DIM

N_BRANCH = 3

OFF_ZSSD = 0
OFF_XBC = OFF_ZSSD + D_INNER
OFF_DT = OFF_XBC + CONV_DIM
OFF_QKV = OFF_DT + SSD_HEADS
OFF_ZATT = OFF_QKV + 3 * ATT_GROUPS * ATT_W
OFF_QMEM = OFF_ZATT + ATT_W
OFF_ZMEM = OFF_QMEM + MEM_W
OFF_GATE = OFF_ZMEM + MEM_W
N_IN = OFF_GATE + N_BRANCH * D_MODEL

kernel_name = "hybrid_ssd_dilated_memory_block"


def _rmsnorm(x, g):
    xf = x.astype(jnp.float32)
    y = xf * lax.rsqrt(jnp.mean(xf * xf, axis=-1, keepdims=True) + EPS)
    return (y * g.astype(jnp.float32)).astype(x.dtype)


def _proj(h, w, start, size):
    return h @ w[:, start:start + size]


def _causal_depthwise_conv(u, w, b):
    c = u.shape[-1]
    y = lax.conv_general_dilated(u, w[:, None, :].astype(u.dtype), window_strides=(1,),
                                 padding=((CONV_K - 1, 0),),
                                 dimension_numbers=('NWC', 'WIO', 'NWC'),
                                 feature_group_count=c)
    return y + b.astype(u.dtype)


def _ssd_chunked(xs, dt, a, bm, cm):
    b, s, nh, p = xs.shape
    nc = s // CHUNK
    x = (xs * dt[..., None]).reshape(b, nc, CHUNK, SSD_GROUPS, SSD_HPG, p)
    la = (dt * a).reshape(b, nc, CHUNK, SSD_GROUPS, SSD_HPG)
    bc = bm.reshape(b, nc, CHUNK, SSD_GROUPS, D_STATE)
    cc = cm.reshape(b, nc, CHUNK, SSD_GROUPS, D_STATE)
    a_cum = jnp.cumsum(la, axis=2)
    ac = jnp.moveaxis(a_cum, 2, -1)
    causal = jnp.tril(jnp.ones((CHUNK, CHUNK), dtype=bool))
    seg = ac[..., :, None] - ac[..., None, :]
    decay = jnp.exp(jnp.where(causal, seg, -jnp.inf))
    cb = jnp.einsum('bclgn,bcsgn->bcgls', cc, bc)
    y_diag = jnp.einsum('bcgels,bcsgep->bclgep', cb[:, :, :, None] * decay, x)
    decay_states = jnp.exp(a_cum[:, :, -1:] - a_cum)
    states = jnp.einsum('bcsgn,bcsgep->bcgepn', bc, x * decay_states[..., None])
    chunk_decay = jnp.exp(a_cum[:, :, -1])

    def step(hst, inp):
        st, dec = inp
        return hst * dec[..., None, None] + st, hst

    h0 = jnp.zeros((b, SSD_GROUPS, SSD_HPG, p, D_STATE), jnp.float32)
    _, prev = lax.scan(step, h0, (jnp.swapaxes(states, 0, 1), jnp.swapaxes(chunk_decay, 0, 1)))
    prev = jnp.swapaxes(prev, 0, 1)
    y_off = jnp.einsum('bclgn,bcgepn->bclgep', cc, prev) * jnp.exp(a_cum)[..., None]
    return (y_diag + y_off).reshape(b, s, nh, p)


def _ssd_branch(h, w, conv_w, conv_b, dt_bias, a_log, d_skip, ssd_norm, w_ssd_out):
    bsz, s, _ = h.shape
    z = _proj(h, w, OFF_ZSSD, D_INNER)
    xbc = jax.nn.silu(_causal_depthwise_conv(_proj(h, w, OFF_XBC, CONV_DIM), conv_w, conv_b))
    dt_raw = _proj(h, w, OFF_DT, SSD_HEADS)
    xs = xbc[..., :D_INNER].reshape(bsz, s, SSD_HEADS, SSD_HEAD_DIM).astype(jnp.float32)
    bm = xbc[..., D_INNER:D_INNER + SSD_GROUPS * D_STATE].reshape(bsz, s, SSD_GROUPS, D_STATE).astype(jnp.float32)
    cm = xbc[..., D_INNER + SSD_GROUPS * D_STATE:].reshape(bsz, s, SSD_GROUPS, D_STATE).astype(jnp.float32)
    dt = jax.nn.softplus(dt_raw.astype(jnp.float32) + dt_bias.astype(jnp.float32))
    a = -jnp.exp(a_log.astype(jnp.float32))
    y = _ssd_chunked(xs, dt, a, bm, cm) + d_skip.astype(jnp.float32)[:, None] * xs
    y = y.reshape(bsz, s, D_INNER) * jax.nn.silu(z.astype(jnp.float32))
    yg = y.reshape(bsz, s, SSD_GROUPS, D_INNER // SSD_GROUPS)
    yg = yg * lax.rsqrt(jnp.mean(yg * yg, axis=-1, keepdims=True) + EPS)
    y = yg.reshape(bsz, s, D_INNER) * ssd_norm.astype(jnp.float32)
    return y.astype(h.dtype) @ w_ssd_out


def _dilated_window_attention(q, k, v, window, dilation):
    b, s, hh, e = q.shape
    n_off = window // dilation
    L = s // dilation
    Lp = -(-L // ATT_BLOCK) * ATT_BLOCK
    nb = Lp // ATT_BLOCK

    def to_streams(t):
        t = t.reshape(b, L, dilation, hh, e).transpose(0, 2, 1, 3, 4)
        t = jnp.pad(t, ((0, 0), (0, 0), (0, Lp - L), (0, 0), (0, 0)))
        return t.reshape(b, dilation, nb, ATT_BLOCK, hh, e)

    def with_prev(t):
        prev = jnp.pad(t, ((0, 0), (0, 0), (1, 0), (0, 0), (0, 0), (0, 0)))[:, :, :-1]
        return jnp.concatenate([prev, t], axis=3)

    qb = to_streams(q)
    kk = with_prev(to_streams(k))
    vv = with_prev(to_streams(v))
    scores = jnp.einsum('brnqhe,brnkhe->brnhqk', qb, kk).astype(jnp.float32) * (e ** -0.5)
    qi = jnp.arange(ATT_BLOCK)[:, None]
    kj = jnp.arange(2 * ATT_BLOCK)[None, :]
    dist = qi - kj + ATT_BLOCK
    kglob = jnp.arange(nb)[:, None, None] * ATT_BLOCK + kj[None] - ATT_BLOCK
    valid = (dist >= 0)[None] & (dist <= n_off)[None] & (kglob >= 0)
    scores = jnp.where(valid[None, None, :, None], scores, -jnp.inf)
    mx = jnp.max(scores, axis=-1, keepdims=True)
    pr = jnp.exp(scores - mx)
    den = jnp.sum(pr, axis=-1, keepdims=True)
    o = jnp.einsum('brnhqk,brnkhe->brnqhe', pr, vv.astype(jnp.float32)) / jnp.swapaxes(den, 3, 4)
    lse = jnp.swapaxes((mx + jnp.log(den))[..., 0], 3, 4)
    o = o.reshape(b, dilation, Lp, hh, e)[:, :, :L].transpose(0, 2, 1, 3, 4).reshape(b, s, hh, e)
    lse = lse.reshape(b, dilation, Lp, hh)[:, :, :L].transpose(0, 2, 1, 3).reshape(b, s, hh)
    return o, lse


def _dilated_branch(h, w, w_attn_out):
    bsz, s, _ = h.shape
    qkv = _proj(h, w, OFF_QKV, 3 * ATT_GROUPS * ATT_W).reshape(
        bsz, s, 3, ATT_GROUPS, ATT_HEADS, ATT_HEAD_DIM)
    outs, lses = [], []
    for g, (window, dilation) in enumerate(ATT_PATTERNS):
        o, lse = _dilated_window_attention(qkv[:, :, 0, g], qkv[:, :, 1, g], qkv[:, :, 2, g],
                                           window, dilation)
        outs.append(o)
        lses.append(lse)
    wts = jax.nn.softmax(jnp.stack(lses, axis=0), axis=0)
    o = jnp.sum(wts[..., None] * jnp.stack(outs, axis=0), axis=0)
    z = _proj(h, w, OFF_ZATT, ATT_W)
    y = o.reshape(bsz, s, ATT_W) * jax.nn.silu(z.astype(jnp.float32))
    return y.astype(h.dtype) @ w_attn_out


def _memory_branch(h, w, mem, mem_norm, w_mem_kv, w_mem_out):
    bsz, s, _ = h.shape
    q = _proj(h, w, OFF_QMEM, MEM_W).reshape(bsz, s, MEM_HEADS, MEM_HEAD_DIM)
    kv = (_rmsnorm(mem, mem_norm) @ w_mem_kv).reshape(bsz, MEM_LEN, 2, MEM_HEADS, MEM_HEAD_DIM)
    scores = jnp.einsum('bshe,bmhe->bhsm', q, kv[:, :, 0]).astype(jnp.float32) * (MEM_HEAD_DIM ** -0.5)
    pr = jax.nn.softmax(scores, axis=-1)
    o = jnp.einsum('bhsm,bmhe->bshe', pr, kv[:, :, 1].astype(jnp.float32)).reshape(bsz, s, MEM_W)
    z = _proj(h, w, OFF_ZMEM, MEM_W)
    y = o * jax.nn.silu(z.astype(jnp.float32))
    return y.astype(h.dtype) @ w_mem_out


def setup_inputs(seed: int = 0) -> dict:
    key = jax.random.key(seed)
    ks = jax.random.split(key, 20)
    f32 = jnp.float32
    x = jax.random.normal(ks[0], (BATCH, SEQ, D_MODEL), f32)
    mem = jax.random.normal(ks[1], (BATCH, MEM_LEN, D_MODEL), f32)
    norm_pre = 1.0 + 0.02 * jax.random.normal(ks[2], (DEPTH, D_MODEL), f32)
    norm_post = 1.0 + 0.02 * jax.random.normal(ks[3], (DEPTH, D_MODEL), f32)
    w_in = jax.random.normal(ks[4], (DEPTH, D_MODEL, N_IN), f32) * D_MODEL ** -0.5
    conv_w = jax.random.normal(ks[5], (DEPTH, CONV_K, CONV_DIM), f32) * CONV_K ** -0.5
    conv_b = 0.02 * jax.random.normal(ks[6], (DEPTH, CONV_DIM), f32)
    dt0 = jnp.exp(jax.random.uniform(ks[7], (DEPTH, SSD_HEADS), f32,
                                     math.log(1e-3), math.log(1e-1)))
    dt_bias = dt0 + jnp.log(-jnp.expm1(-dt0))
    a_log = jnp.log(jax.random.uniform(ks[8], (DEPTH, SSD_HEADS), f32, 1.0, 16.0))
    d_skip = 1.0 + 0.02 * jax.random.normal(ks[9], (DEPTH, SSD_HEADS), f32)
    ssd_norm = 1.0 + 0.02 * jax.random.normal(ks[10], (DEPTH, D_INNER), f32)
    w_ssd_out = jax.random.normal(ks[11], (DEPTH, D_INNER, D_MODEL), f32) * D_INNER ** -0.5
    w_attn_out = jax.random.normal(ks[12], (DEPTH, ATT_W, D_MODEL), f32) * ATT_W ** -0.5
    mem_norm = 1.0 + 0.02 * jax.random.normal(ks[13], (DEPTH, D_MODEL), f32)
    w_mem_kv = jax.random.normal(ks[14], (DEPTH, D_MODEL, 2 * MEM_W), f32) * D_MODEL ** -0.5
    w_mem_out = jax.random.normal(ks[15], (DEPTH, MEM_W, D_MODEL), f32) * MEM_W ** -0.5
    w_out = jax.random.normal(ks[16], (DEPTH, D_MODEL, D_MODEL), f32) * D_MODEL ** -0.5
    return {"x": x, "mem": mem, "norm_pre": norm_pre, "norm_post": norm_post, "w_in": w_in,
            "conv_w": conv_w, "conv_b": conv_b, "dt_bias": dt_bias, "a_log": a_log,
            "d_skip": d_skip, "ssd_norm": ssd_norm, "w_ssd_out": w_ssd_out,
            "w_attn_out": w_attn_out, "mem_norm": mem_norm, "w_mem_kv": w_mem_kv,
            "w_mem_out": w_mem_out, "w_out": w_out}


def reference(x, mem, norm_pre, norm_post, w_in, conv_w, conv_b, dt_bias, a_log, d_skip,
              ssd_norm, w_ssd_out, w_attn_out, mem_norm, w_mem_kv, w_mem_out, w_out):
    bsz, s, _ = x.shape
    for l in range(DEPTH):
        h = _rmsnorm(x, norm_pre[l])
        w = w_in[l]
        y_ssd = _ssd_branch(h, w, conv_w[l], conv_b[l], dt_bias[l], a_log[l], d_skip[l],
                            ssd_norm[l], w_ssd_out[l])
        y_att = _dilated_branch(h, w, w_attn_out[l])
        y_mem = _memory_branch(h, w, mem, mem_norm[l], w_mem_kv[l], w_mem_out[l])
        gates = jax.nn.sigmoid(_proj(h, w, OFF_GATE, N_BRANCH * D_MODEL)).reshape(
            bsz, s, N_BRANCH, D_MODEL)
        merged = gates[:, :, 0] * y_ssd + gates[:, :, 1] * y_att + gates[:, :, 2] * y_mem
        out = merged @ w_out[l]
        x = x + _rmsnorm(out, norm_post[l])
    return x
```

```python
import numpy as np
import concourse.bass as bass
import concourse.mybir as mybir
from concourse.bass_utils import run_bass_kernel_spmd

F32 = mybir.dt.float32
BF16 = mybir.dt.bfloat16
AF = mybir.ActivationFunctionType
ALU = mybir.AluOpType

D = 1024
DEPTH = 2
EPS = 1e-6
D_INNER = 2048
NH = 32
NG = 8
DS = 128
MEM_LEN = 256
OFF_ZSSD = 0
OFF_XBC = 2048
OFF_DT = 6144
OFF_QKV = 6176
OFF_ZATT = OFF_QKV + 9216
OFF_QMEM = OFF_ZATT + 1024
OFF_ZMEM = OFF_QMEM + 1024
OFF_GATE = OFF_ZMEM + 1024
N_IN = OFF_GATE + 3072
DIL = (1, 4, 16)
N_DMA_SEMS = 24


class Sched:
    ENGS = ("pe", "act", "dve", "pool", "sp")

    def __init__(self):
        self.ops = []

    def op(self, eng, instrs, r=(), w=()):
        if isinstance(instrs, tuple):
            instrs = [instrs]
        instrs = list(instrs)

        def fn(e, instrs=instrs):
            ins = None
            for m, kw in instrs:
                ins = getattr(e, m)(**kw)
            return ins
        self.ops.append(dict(eng=eng, fn=fn, r=tuple(r), w=tuple(w), dma=False))

    def dma(self, out, in_, r=(), w=(), eng="sp"):
        def fn(e, out=out, in_=in_):
            return e.dma_start(out=out, in_=in_)
        self.ops.append(dict(eng=eng, fn=fn, r=tuple(r), w=tuple(w), dma=True))

    def barrier(self):
        self.ops.append(dict(barrier=True))

    def emit(self, nc):
        ops = self.ops
        n = len(ops)
        last_w = {}
        readers = {}
        deps = [None] * n
        dma_sem_of = [None] * n
        dma_rr = 0
        dma_last_use = [None] * N_DMA_SEMS
        bar_pending = {}
        last_op_of = {}
        outstanding = set()
        for i, o in enumerate(ops):
            if o.get("barrier"):
                allprev = set(outstanding)
                for e in self.ENGS:
                    bar_pending[e] = bar_pending.get(e, set()) | allprev
                outstanding = set()
                last_w.clear()
                readers.clear()
                continue
            d = set()
            for k in o["r"]:
                if k in last_w:
                    d.add(last_w[k])
            for k in o["w"]:
                if k in last_w:
                    d.add(last_w[k])
                for rr in readers.get(k, ()):
                    d.add(rr)
            if o["eng"] in bar_pending and bar_pending[o["eng"]]:
                d |= bar_pending[o["eng"]]
                bar_pending[o["eng"]] = set()
            if o["dma"]:
                j = dma_rr % N_DMA_SEMS
                dma_rr += 1
                dma_sem_of[i] = j
                if dma_last_use[j] is not None:
                    d.add(dma_last_use[j])
                dma_last_use[j] = i
            d.discard(i)
            deps[i] = d
            for k in o["r"]:
                readers.setdefault(k, []).append(i)
            for k in o["w"]:
                last_w[k] = i
                readers[k] = []
            if o["dma"]:
                outstanding.add(i)
            else:
                prev = last_op_of.get(o["eng"])
                if prev is not None:
                    outstanding.discard(prev)
                last_op_of[o["eng"]] = i
                outstanding.add(i)
        self.final_wait = set(outstanding) | bar_pending.get("sp", set())
        needed = [False] * n
        for i, o in enumerate(ops):
            if o.get("barrier"):
                continue
            for dd in deps[i]:
                if ops[dd]["eng"] == "pe" and o["eng"] == "pe" and not ops[dd]["dma"]:
                    continue
                needed[dd] = True
        for dd in self.final_wait:
            needed[dd] = True
        cnt = {e: 0 for e in self.ENGS}
        dcnt = [0] * N_DMA_SEMS
        event = [None] * n
        for i, o in enumerate(ops):
            if o.get("barrier"):
                continue
            if o["dma"]:
                j = dma_sem_of[i]
                dcnt[j] += 16
                event[i] = (("d", j), dcnt[j])
            elif needed[i]:
                cnt[o["eng"]] += 1
                event[i] = (("e", o["eng"]), cnt[o["eng"]])
        import contextlib
        with contextlib.ExitStack() as st:
            esem = {e: st.enter_context(nc.semaphore("s_" + e)) for e in self.ENGS}
            dsem = [st.enter_context(nc.semaphore("d_%d" % j)) for j in range(N_DMA_SEMS)]
            block = st.enter_context(nc.Block())

            def sem_of(key):
                return esem[key[1]] if key[0] == "e" else dsem[key[1]]

            def run(engname, eng):
                known = {}
                for i, o in enumerate(ops):
                    if o.get("barrier") or o["eng"] != engname:
                        continue
                    waits = {}
                    for dd in deps[i]:
                        if ops[dd]["eng"] == "pe" and engname == "pe" and not ops[dd]["dma"]:
                            continue
                        key, val = event[dd]
                        if known.get(key, 0) >= val:
                            continue
                        waits[key] = max(waits.get(key, 0), val)
                    for key, val in waits.items():
                        eng.wait_ge(sem_of(key), val)
                        known[key] = val
                    ins = o["fn"](eng)
                    if event[i] is not None:
                        key, val = event[i]
                        ins.then_inc(sem_of(key), 16 if key[0] == "d" else 1)
                if engname == "sp":
                    waits = {}
                    for dd in self.final_wait:
                        key, val = event[dd]
                        if known.get(key, 0) >= val:
                            continue
                        waits[key] = max(waits.get(key, 0), val)
                    for key, val in waits.items():
                        eng.wait_ge(sem_of(key), val)

            @block.tensor
            def _(e):
                run("pe", e)

            @block.scalar
            def _(e):
                run("act", e)

            @block.vector
            def _(e):
                run("dve", e)

            @block.gpsimd
            def _(e):
                run("pool", e)

            @block.sync
            def _(e):
                run("sp", e)


SB_BASE = 16512
SB_LIMIT = 229376 - 256


class SBAlloc:
    def __init__(self, nc):
        self.nc = nc
        self.cur = SB_BASE
        self.n = 0

    def alloc(self, shape, dt):
        nb = 1
        for s in shape[1:]:
            nb *= s
        nb *= 4 if dt == F32 else 2
        off = self.cur
        self.cur += (nb + 63) // 64 * 64
        assert self.cur <= SB_LIMIT, ("SBUF overflow", self.cur)
        self.n += 1
        return self.nc.alloc_sbuf_tensor_at("sb%d" % self.n, list(shape), dt, offset=off).ap()


def I(m, **kw):
    return (m, kw)


def make_consts():
    p = np.arange(128)[:, None]
    j = np.arange(128)[None, :]
    ident = (p == j)
    U = (p <= j)
    Ls = (p > j)
    ones = np.ones((128, 128), bool)
    Ge = (p >= j)
    return np.concatenate([ident, U, Ls, ones, Ge], axis=1).astype(np.float32)


def build_program(T, NL, dbg=()):
    nc = bass.Bass("TRN2", target_bir_lowering=False)
    S = Sched()
    NT = T // 128
    HALF = min(T, 4096)
    NHALF = T // HALF
    NTT = HALF // 512

    def dram(name, shape, dt, kind="Internal"):
        if name in dbg:
            kind = "ExternalOutput"
        return nc.dram_tensor(name, list(shape), dt, kind=kind).ap()

    x_in = dram("x", [T, D], F32, "ExternalInput")
    mem_in = dram("mem", [MEM_LEN, D], F32, "ExternalInput")
    consts_in = dram("consts", [128, 640], F32, "ExternalInput")
    P = {}
    for name, shp in [("norm_pre", [NL, D]), ("norm_post", [NL, D]), ("w_in", [NL, D, N_IN]),
                      ("conv_w", [NL, 128, 128]), ("conv_b", [NL, 128, 32]), ("dt_bias", [NL, NH]),
                      ("a_log", [NL, NH]), ("d_skip", [NL, NH]), ("ssd_norm", [NL, D_INNER]),
                      ("w_ssd_out", [NL, D_INNER, D]), ("w_attn_out", [NL, D, D]),
                      ("mem_norm", [NL, D]), ("w_mem_kv", [NL, D, 2 * D]),
                      ("w_mem_out", [NL, D, D]), ("w_out", [NL, D, D])]:
        P[name] = dram(name, shp, F32, "ExternalInput")
    out = dram("out", [T, D], F32, "ExternalOutput")
    xmid = [dram("xmid%d" % i, [T, D], F32) for i in range(NL - 1)]
    XS = dram("XS", [T, 2048], BF16)
    BTOK = dram("BTOK", [T, 1024], BF16)
    BCT = dram("BCT", [2048, T], BF16)
    Z1 = dram("Z1", [T, 2048], BF16)
    QT = [dram("QT%d" % g, [1024, T], BF16) for g in range(3)]
    KT = [dram("KT%d" % g, [1024, T], BF16) for g in range(3)]
    V = [dram("V%d" % g, [T, 1024], BF16) for g in range(3)]
    ZA = dram("ZA", [T, 1024], BF16)
    QMT = dram("QMT", [1024, T], BF16)
    ZM = dram("ZM", [T, 1024], BF16)
    G = dram("G", [T, 3072], BF16)
    YS = dram("YS", [T, 2048], BF16)
    OA = [dram("OA%d" % g, [T, 1040], F32) for g in range(3)]

    sb = SBAlloc(nc)
    PS = [nc.alloc_psum_tensor("ps%d" % i, [128, 1024], F32).ap() for i in range(4)]

    def bank(i):
        return PS[i // 2][:, (i % 2) * 512:(i % 2) * 512 + 512]

    def bank_bf(i):
        return bank(i).bitcast(BF16)

    cst = sb.alloc([128, 640], F32)
    ident_bf = sb.alloc([128, 128], BF16)
    mask2 = sb.alloc([128, 2, 128], BF16)
    ones_bf = sb.alloc([128, 128], BF16)
    persist_mark0 = sb.cur
    DTs = sb.alloc([128, NT, 32], F32)
    LAs = sb.alloc([128, NT, 32], F32)
    halo = sb.alloc([128, 32, 3], F32)
    ident_f = cst[:, 0:128]
    U_f = cst[:, 128:256]
    Ls_f = cst[:, 256:384]
    ones_f = cst[:, 384:512]
    Ge_f = cst[:, 512:640]
    S.dma(cst, consts_in, w=["cst"])
    S.op("dve", I("tensor_copy", out=ident_bf, in_=ident_f), r=["cst"], w=["ident"])
    S.op("dve", I("tensor_copy", out=mask2[:, 0, :], in_=Ge_f), r=["cst"], w=["mask2a"])
    S.op("dve", I("tensor_copy", out=mask2[:, 1, :], in_=U_f), r=["cst"], w=["mask2b"])
    S.op("dve", I("tensor_copy", out=ones_bf, in_=ones_f), r=["cst"], w=["onesbf"])
    persist_mark = sb.cur

    def bcast_load(dst, src_row, key):
        S.dma(dst, src_row.partition_broadcast(128), w=[key])

    def rstd_ops(ss, rstd, n, rkeys, wkey):
        S.op("dve", I("tensor_scalar", out=rstd, in0=ss, scalar1=1.0 / n, scalar2=EPS,
                      op0=ALU.mult, op1=ALU.add), r=rkeys, w=[wkey])
        S.op("act", I("activation", out=rstd, in_=rstd, func=AF.Ln), r=[wkey], w=[wkey])
        S.op("act", I("activation", out=rstd, in_=rstd, func=AF.Exp, scale=-0.5), r=[wkey], w=[wkey])

    C = dict(locals())
    for l in range(NL):
        x_src = x_in if l == 0 else xmid[l - 1]
        x_dst = out if l == NL - 1 else xmid[l]
        S.barrier()
        sb.cur = persist_mark
        gpre = sb.alloc([128, D], F32)
        convw = sb.alloc([128, 32, 4], F32)
        convb = sb.alloc([128, 32], F32)
        dtb = sb.alloc([128, 32], F32)
        abc = sb.alloc([128, 32], F32)
        bcast_load(gpre, P["norm_pre"][l:l + 1, :], "gpre")
        S.dma(convw, P["conv_w"][l].rearrange("p (b k) -> p b k", k=4), w=["convw"])
        S.dma(convb, P["conv_b"][l], w=["convb"])
        bcast_load(dtb, P["dt_bias"][l:l + 1, :], "dtb")
        bcast_load(abc, P["a_log"][l:l + 1, :], "abc0")
        S.op("act", I("activation", out=abc, in_=abc, func=AF.Exp), r=["abc0"], w=["abc0"])
        S.op("act", I("mul", out=abc, in_=abc, mul=-1.0), r=["abc0"], w=["abc"])
        S.op("pool", I("memset", ap=halo, constant=0.0), w=["halo"])
        projmark = sb.cur
        for hf in range(NHALF):
            sb.cur = projmark
            t0h = hf * HALF
            hT = sb.alloc([128, 8, HALF], BF16)
            nmark = sb.cur
            xin = [sb.alloc([128, D], F32) for _ in range(2)]
            xn = [sb.alloc([128, D], BF16) for _ in range(2)]
            junk = sb.alloc([128, D], BF16)
            ss = [sb.alloc([128, 1], F32) for _ in range(2)]
            rs = [sb.alloc([128, 1], F32) for _ in range(2)]
            for i in range(HALF // 128):
                s_ = i % 2
                tok = t0h + i * 128
                S.dma(xin[s_], x_src[tok:tok + 128, :], w=[("xin", s_)])
                S.op("act", I("activation", out=junk, in_=xin[s_], func=AF.Square,
                              accum_out=ss[s_][:, 0:1]),
                     r=[("xin", s_)], w=[("ss", s_), "junk"])
                rstd_ops(ss[s_], rs[s_], D, [("ss", s_)], ("rs", s_))
                S.op("dve", I("scalar_tensor_tensor", out=xn[s_], in0=xin[s_], scalar=rs[s_][:, 0:1],
                              in1=gpre, op0=ALU.mult, op1=ALU.mult),
                     r=[("xin", s_), ("rs", s_), "gpre"], w=[("xn", s_)])
                pb = 6 + s_
                S.op("pe", [I("transpose", out=bank_bf(pb)[:, k * 128:(k + 1) * 128],
                              in_=xn[s_][:, k * 128:(k + 1) * 128], identity=ident_bf)
                            for k in range(8)],
                     r=[("xn", s_), "ident"], w=[("bank", pb)])
                S.op("act", I("copy", out=hT[:, :, i * 128:(i + 1) * 128],
                              in_=bank_bf(pb).rearrange("p (k t) -> p k t", k=8)),
                     r=[("bank", pb)], w=[("hT", i // 4)])
            S.barrier()
            sb.cur = nmark
            C.update(locals())
            phaseP(C)
        S.barrier()
        sb.cur = persist_mark
        C.update(locals())
        phaseA(C)
        S.barrier()
        sb.cur = persist_mark
        phaseS(C)
        S.barrier()
        sb.cur = persist_mark
        phaseF(C)
    S.emit(nc)
    return nc


def phaseP(C):
    S, sb, l, hT = C["S"], C["sb"], C["l"], C["hT"]
    NTT, t0h, hf, NHALF, HALF = C["NTT"], C["t0h"], C["hf"], C["NHALF"], C["HALF"]
    bank, bank_bf, ident_bf = C["bank"], C["bank_bf"], C["ident_bf"]
    convw, convb, halo, dtb, abc = C["convw"], C["convb"], C["halo"], C["dtb"], C["abc"]
    DTs, LAs = C["DTs"], C["LAs"]
    W = C["P"]["w_in"][l]
    wst = [sb.alloc([128, 8, 512], F32) for _ in range(2)]
    wbf = [sb.alloc([128, 8, 512], BF16) for _ in range(2)]
    stage = [sb.alloc([128, 4, 512], BF16) for _ in range(2)]
    U = [[sb.alloc([128, 515], F32) for _ in range(2)] for _ in range(4)]
    acc = [sb.alloc([128, 512], F32) for _ in range(2)]
    xc = [sb.alloc([128, 4, 512], BF16) for _ in range(2)]
    stT = [sb.alloc([128, 4, 512], BF16) for _ in range(2)]
    wdt = sb.alloc([128, 8, 32], F32)
    wdtb = sb.alloc([128, 8, 32], BF16)
    dtmp = sb.alloc([128, 16, 32], F32)

    blocks = []
    for j in range(4):
        blocks.append((OFF_ZSSD + j * 512, "tm", C["Z1"], j * 512, AF.Silu))
    for j in range(8):
        blocks.append((OFF_XBC + j * 512, "xbc", None, j * 4, None))
    for g in range(3):
        for j in range(2):
            blocks.append((OFF_QKV + (0 * 3 + g) * 1024 + j * 512, "fm", C["QT"][g], j * 512, None))
        for j in range(2):
            blocks.append((OFF_QKV + (1 * 3 + g) * 1024 + j * 512, "fm", C["KT"][g], j * 512, None))
        for j in range(2):
            blocks.append((OFF_QKV + (2 * 3 + g) * 1024 + j * 512, "tm", C["V"][g], j * 512, AF.Copy))
    for j in range(2):
        blocks.append((OFF_ZATT + j * 512, "tm", C["ZA"], j * 512, AF.Silu))
    for j in range(2):
        blocks.append((OFF_QMEM + j * 512, "fm", C["QMT"], j * 512, None))
    for j in range(2):
        blocks.append((OFF_ZMEM + j * 512, "tm", C["ZM"], j * 512, AF.Silu))
    for j in range(6):
        blocks.append((OFF_GATE + j * 512, "tm", C["G"], j * 512, AF.Sigmoid))

    S.dma(wdt, W[:, OFF_DT:OFF_DT + 32].rearrange("(kc p) n -> p kc n", p=128), w=["wdt"])
    S.op("dve", I("tensor_copy", out=wdtb, in_=wdt), r=["wdt"], w=["wdtb"])
    ntile = HALF // 128
    grp = min(16, ntile)
    for g0 in range(0, ntile, grp):
        bk = 0
        for i in range(g0, g0 + grp):
            S.op("pe", [I("matmul", out=bank(bk)[:, (i - g0) * 32:(i - g0) * 32 + 32],
                          lhsT=hT[:, kc, i * 128:(i + 1) * 128], rhs=wdtb[:, kc, :],
                          start=(kc == 0), stop=(kc == 7)) for kc in range(8)],
                 r=[("hT", i // 4), "wdtb"], w=[("bank", bk)])
        c0 = t0h // 128 + g0
        S.op("dve", I("tensor_tensor", out=dtmp[:, 0:grp, :],
                      in0=bank(bk)[:, 0:grp * 32].rearrange("p (n e) -> p n e", e=32),
                      in1=dtb.unsqueeze(1).to_broadcast([128, grp, 32]), op=ALU.add),
             r=[("bank", bk), "dtb"], w=["dtmp"])
        S.op("act", I("activation", out=dtmp[:, 0:grp, :], in_=dtmp[:, 0:grp, :], func=AF.Exp),
             r=["dtmp"], w=["dtmp"])
        S.op("act", I("activation", out=DTs[:, c0:c0 + grp, :], in_=dtmp[:, 0:grp, :], func=AF.Ln,
                      bias=1.0, scale=1.0),
             r=["dtmp"], w=[("DTs", c0)])
        S.op("dve", I("tensor_tensor", out=LAs[:, c0:c0 + grp, :], in0=DTs[:, c0:c0 + grp, :],
                      in1=abc.unsqueeze(1).to_broadcast([128, grp, 32]), op=ALU.mult),
             r=[("DTs", c0), "abc"], w=[("LAs", c0)])

    def load_w(bi):
        c0 = blocks[bi][0]
        sl = bi % 2
        S.dma(wst[sl], W[:, c0:c0 + 512].rearrange("(kc p) n -> p kc n", p=128), w=[("wst", sl)])
        S.op("pool", I("tensor_copy", out=wbf[sl], in_=wst[sl]), r=[("wst", sl)], w=[("wbf", sl)])

    load_w(0)
    bkrr = [1]
    cnt = [0]
    for bi, (c0, kind, dst, d0, func) in enumerate(blocks):
        if bi + 1 < len(blocks):
            load_w(bi + 1)
        sl = bi % 2
        for tt in range(NTT):
            tok0 = t0h + tt * 512
            st = cnt[0] % 2
            cnt[0] += 1
            for sub in range(4):
                bk = bkrr[0]
                bkrr[0] = bkrr[0] % 5 + 1
                if kind == "tm":
                    mm = [I("matmul", out=bank(bk), lhsT=hT[:, kc, tt * 512 + sub * 128:tt * 512 + sub * 128 + 128],
                            rhs=wbf[sl][:, kc, :], start=(kc == 0), stop=(kc == 7)) for kc in range(8)]
                else:
                    mm = [I("matmul", out=bank(bk), lhsT=wbf[sl][:, kc, sub * 128:(sub + 1) * 128],
                            rhs=hT[:, kc, tt * 512:(tt + 1) * 512], start=(kc == 0), stop=(kc == 7))
                          for kc in range(8)]
                S.op("pe", mm, r=[("wbf", sl), ("hT", tt)], w=[("bank", bk)])
                if kind == "tm":
                    S.op("act", I("activation", out=stage[st][:, sub, :], in_=bank(bk), func=func),
                         r=[("bank", bk)], w=[("stage", st, sub)])
                elif kind == "fm":
                    eng = "act" if sub % 2 == 0 else "dve"
                    S.op(eng, I("copy" if eng == "act" else "tensor_copy", out=stage[st][:, sub, :], in_=bank(bk)),
                         r=[("bank", bk)], w=[("stage", st, sub)])
                else:
                    chb = d0 + sub
                    par = tt % 2
                    Uc, Up = U[sub][par], U[sub][1 - par]
                    a = sub % 2
                    S.op("act", I("copy", out=Uc[:, 3:515], in_=bank(bk)),
                         r=[("bank", bk)], w=[("U", sub, par, "m")])
                    if tt == 0:
                        S.op("pool", I("tensor_copy", out=Uc[:, 0:3], in_=halo[:, chb, :]),
                             r=[("halo", chb)], w=[("U", sub, par, "h")])
                    else:
                        S.op("pool", I("tensor_copy", out=Uc[:, 0:3], in_=Up[:, 512:515]),
                             r=[("U", sub, 1 - par, "m")], w=[("U", sub, par, "h")])
                    if tt == NTT - 1 and hf + 1 < NHALF:
                        S.op("pool", I("tensor_copy", out=halo[:, chb, :], in_=Uc[:, 512:515]),
                             r=[("U", sub, par, "m")], w=[("halo", chb)])
                    ukeys = [("U", sub, par, "m"), ("U", sub, par, "h"), "convw", "convb"]
                    S.op("dve", I("tensor_scalar", out=acc[a], in0=Uc[:, 3:515], scalar1=convw[:, chb, 3:4],
                                  scalar2=convb[:, chb:chb + 1], op0=ALU.mult, op1=ALU.add),
                         r=ukeys, w=[("acc", a)])
                    for kk, eng_ in ((2, "dve"), (1, "dve"), (0, "dve")):
                        S.op(eng_, I("scalar_tensor_tensor", out=acc[a], in0=Uc[:, kk:kk + 512],
                                     scalar=convw[:, chb, kk:kk + 1], in1=acc[a], op0=ALU.mult, op1=ALU.add),
                             r=ukeys + [("acc", a)], w=[("acc", a)])
                    S.op("act", I("activation", out=xc[st][:, sub, :], in_=acc[a], func=AF.Silu),
                         r=[("acc", a)], w=[("xc", st, sub)])
                    if chb < 24:
                        tb = 6 + (sub % 2)
                        S.op("pe", [I("transpose", out=bank_bf(tb)[:, j * 128:(j + 1) * 128],
                                      in_=xc[st][:, sub, j * 128:(j + 1) * 128], identity=ident_bf)
                                    for j in range(4)],
                             r=[("xc", st, sub), "ident"], w=[("bank", tb)])
                        eng = "act" if sub % 2 == 0 else "dve"
                        S.op(eng, I("copy" if eng == "act" else "tensor_copy",
                                    out=stT[st][:, :, sub * 128:(sub + 1) * 128],
                                    in_=bank_bf(tb)[:, 0:512].rearrange("p (j c) -> p j c", j=4)),
                             r=[("bank", tb)], w=[("stT", st, sub)])
            if kind == "tm":
                S.dma(dst[tok0:tok0 + 512, d0:d0 + 512].rearrange("(s p) c -> p s c", p=128), stage[st],
                      r=[("stage", st, s_) for s_ in range(4)])
            elif kind == "fm":
                S.dma(dst[d0:d0 + 512, tok0:tok0 + 512].rearrange("(s p) t -> p s t", p=128), stage[st],
                      r=[("stage", st, s_) for s_ in range(4)])
            else:
                chb0 = d0
                if chb0 < 16:
                    S.dma(C["XS"][tok0:tok0 + 512, chb0 * 128:chb0 * 128 + 512].rearrange("(j p) c -> p j c", p=128),
                          stT[st], r=[("stT", st, s_) for s_ in range(4)])
                elif chb0 < 24:
                    cc = (chb0 - 16) * 128
                    S.dma(C["BTOK"][tok0:tok0 + 512, cc:cc + 512].rearrange("(j p) c -> p j c", p=128),
                          stT[st], r=[("stT", st, s_) for s_ in range(4)])
                if chb0 >= 16:
                    r0 = (chb0 - 16) * 128
                    S.dma(C["BCT"][r0:r0 + 512, tok0:tok0 + 512].rearrange("(s p) t -> p s t", p=128),
                          xc[st], r=[("xc", st, s_) for s_ in range(4)])


def phaseA(C):
    S, sb, T = C["S"], C["sb"], C["T"]
    bank, PS, mask2, ones_bf = C["bank"], C["PS"], C["mask2"], C["ones_bf"]
    NBLK = T // 128
    q2 = [sb.alloc([128, T], BF16) for _ in range(2)]
    k2 = [sb.alloc([128, T], BF16) for _ in range(2)]
    v2 = [sb.alloc([128, NBLK, 128], BF16) for _ in range(2)]
    PT = [sb.alloc([128, 2, 2, 128], BF16) for _ in range(2)]
    ost = [sb.alloc([128, 8, 130], F32) for _ in range(2)]
    combo = 0
    ostc = 0
    for g in range(3):
        d = DIL[g]
        nb = T // d // 128
        OB = min(nb, 8)
        for hp in range(8):
            sl = (g * 8 + hp) % 2
            rows = slice(hp * 128, (hp + 1) * 128)
            S.dma(q2[sl], C["QT"][g][rows, :], w=[("q2", sl)])
            S.dma(k2[sl], C["KT"][g][rows, :], w=[("k2", sl)])
            vsrc = C["V"][g][:, rows].rearrange("(b i r) c -> r i b c", i=128, r=d)
            for r in range(d):
                S.dma(v2[sl][:, r * nb:(r + 1) * nb, :], vsrc[r], w=[("v2", sl, r)])
            qS = q2[sl].rearrange("p (m r) -> p r m", r=d)
            kS = k2[sl].rearrange("p (m r) -> p r m", r=d)
            odst = C["OA"][g][:, hp * 130:(hp + 1) * 130].rearrange("(b i r) c -> r i b c", i=128, r=d)
            for r in range(d):
                for b in range(nb):
                    blk = r * nb + b
                    x = combo % 2
                    combo += 1
                    mm = []
                    for hh in range(2):
                        pr = slice(hh * 64, (hh + 1) * 64)
                        bk = bank(2 * x + hh)
                        if b > 0:
                            mm.append(I("matmul", out=bk[:, 0:128], lhsT=kS[pr, r, (b - 1) * 128:b * 128],
                                        rhs=qS[pr, r, b * 128:(b + 1) * 128], start=True, stop=True))
                        mm.append(I("matmul", out=bk[:, 128:256], lhsT=kS[pr, r, b * 128:(b + 1) * 128],
                                    rhs=qS[pr, r, b * 128:(b + 1) * 128], start=True, stop=True))
                    S.op("pe", mm, r=[("q2", sl), ("k2", sl)], w=[("psS", x)])
                    lo = 0 if b > 0 else 128
                    pin = PS[x].rearrange("p (h c) -> p h c", h=2)[:, :, lo:256]
                    pout = PT[x].rearrange("p h t q -> p h (t q)")[:, :, lo:256]
                    S.op("act", I("activation", out=pout, in_=pin, func=AF.Exp, scale=0.125),
                         r=[("psS", x)], w=[("PT", x)])
                    mk = mask2.rearrange("p t q -> p (t q)")[:, lo:256].unsqueeze(1).to_broadcast([128, 2, 256 - lo])
                    S.op("dve" if combo % 2 == 0 else "pool",
                         I("tensor_tensor", out=pout, in0=pout, in1=mk, op=ALU.mult),
                         r=[("PT", x), "mask2a", "mask2b"], w=[("PT", x)])
                    ob = 4 + combo % 4
                    mm = []
                    for hh in range(2):
                        o_ = bank(ob)[:, hh * 65:hh * 65 + 64]
                        dn = bank(ob)[:, hh * 65 + 64:hh * 65 + 65]
                        hc = slice(hh * 64, (hh + 1) * 64)
                        if b > 0:
                            mm.append(I("matmul", out=o_, lhsT=PT[x][:, hh, 0, :], rhs=v2[sl][:, blk - 1, hc],
                                        start=True, stop=False))
                        mm.append(I("matmul", out=o_, lhsT=PT[x][:, hh, 1, :], rhs=v2[sl][:, blk, hc],
                                    start=(b == 0), stop=True))
                        if b > 0:
                            mm.append(I("matmul", out=dn, lhsT=PT[x][:, hh, 0, :], rhs=ones_bf[:, 0:1],
                                        start=True, stop=False))
                        mm.append(I("matmul", out=dn, lhsT=PT[x][:, hh, 1, :], rhs=ones_bf[:, 0:1],
                                    start=(b == 0), stop=True))
                    S.op("pe", mm, r=[("PT", x), ("v2", sl, r), "onesbf"], w=[("bank", ob)])
                    os_ = ostc % 2
                    eng = "act" if combo % 2 == 0 else "dve"
                    S.op(eng, I("copy" if eng == "act" else "tensor_copy", out=ost[os_][:, b % OB, :],
                                in_=bank(ob)[:, 0:130]),
                         r=[("bank", ob)], w=[("ost", os_, b % OB)])
                    if b % OB == OB - 1:
                        b0 = b - (OB - 1)
                        S.dma(odst[r][:, b0:b0 + OB, :], ost[os_][:, 0:OB, :],
                              r=[("ost", os_, j) for j in range(OB)])
                        ostc += 1


def phaseS(C):
    S, sb, T, l, NT = C["S"], C["sb"], C["T"], C["l"], C["NT"]
    bank, LAs, DTs = C["bank"], C["LAs"], C["DTs"]
    U_f, Ls_f, ones_f = C["U_f"], C["Ls_f"], C["ones_f"]
    H = sb.alloc([128, 2048], F32)
    Hbf = sb.alloc([128, 2048], BF16)
    dbc = sb.alloc([128, 32], F32)
    xs_t = [sb.alloc([128, 2048], BF16) for _ in range(2)]
    b_t = [sb.alloc([128, 1024], BF16) for _ in range(2)]
    bc4 = [sb.alloc([128, 16, 512], BF16) for _ in range(2)]
    ex = [sb.alloc([128, 96], F32) for _ in range(2)]
    xds = [sb.alloc([128, 32, 64], BF16) for _ in range(2)]
    xdt = [sb.alloc([128, 32, 64], BF16) for _ in range(2)]
    dx = [sb.alloc([128, 32, 64], F32) for _ in range(2)]
    ybf = [sb.alloc([128, 2048], BF16) for _ in range(2)]
    cbm = [sb.alloc([128, 128], F32) for _ in range(2)]
    lseg = [sb.alloc([128, 4, 128], F32) for _ in range(2)]
    dec = [sb.alloc([128, 4, 128], F32) for _ in range(2)]
    MT = [sb.alloc([128, 4, 128], BF16) for _ in range(2)]
    tt_ = [sb.alloc([128, 4, 64], F32) for _ in range(2)]
    S.dma(dbc, C["P"]["d_skip"][l:l + 1, :].partition_broadcast(128), w=["dbc"])
    S.op("pool", I("memset", ap=H, constant=0.0), w=[("H", g) for g in range(8)])
    S.op("pool", I("memset", ap=Hbf, constant=0.0), w=[("Hbf", g) for g in range(8)])

    def loads(c):
        s = c % 2
        S.dma(xs_t[s], C["XS"][c * 128:(c + 1) * 128, :], w=[("xs_t", s)])
        S.dma(b_t[s], C["BTOK"][c * 128:(c + 1) * 128, :], w=[("b_t", s)])
        if c % 4 == 0:
            s4 = (c // 4) % 2
            S.dma(bc4[s4], C["BCT"][:, c * 128:c * 128 + 512].rearrange("(j p) t -> p j t", p=128),
                  w=[("bc4", s4)])

    loads(0)
    xi = 0
    for c in range(NT):
        if c + 1 < NT:
            loads(c + 1)
        s = c % 2
        s4 = (c // 4) % 2
        la = LAs[:, c, :]
        S.op("pe", [I("matmul", out=bank(0)[:, 0:32], lhsT=U_f, rhs=la, start=True, stop=True),
                    I("matmul", out=bank(0)[:, 32:64], lhsT=Ls_f, rhs=la, start=True, stop=True),
                    I("matmul", out=bank(0)[:, 64:96], lhsT=ones_f, rhs=la, start=True, stop=True)],
             r=["cst", ("LAs", c)], w=[("bank", 0)])
        S.op("act", I("activation", out=ex[s], in_=bank(0)[:, 0:96], func=AF.Exp),
             r=[("bank", 0)], w=[("ex", s)])
        S.op("pool", I("tensor_tensor", out=xdt[s], in0=xs_t[s].rearrange("p (h e) -> p h e", e=64),
                       in1=DTs[:, c, :].unsqueeze(2).to_broadcast([128, 32, 64]), op=ALU.mult),
             r=[("xs_t", s), ("DTs", c)], w=[("xdt", s)])
        S.op("dve", I("tensor_tensor", out=xds[s], in0=xdt[s],
                      in1=ex[s][:, 32:64].unsqueeze(2).to_broadcast([128, 32, 64]), op=ALU.mult),
             r=[("xdt", s), ("ex", s)], w=[("xds", s)])
        S.op("pool", I("tensor_tensor", out=dx[s], in0=xs_t[s].rearrange("p (h e) -> p h e", e=64),
                       in1=dbc.unsqueeze(2).to_broadcast([128, 32, 64]), op=ALU.mult),
             r=[("xs_t", s), "dbc"], w=[("dx", s)])
        tk = slice((c % 4) * 128, (c % 4 + 1) * 128)
        for g in range(8):
            x = xi % 2
            xi += 1
            BT = bc4[s4][:, g, tk]
            CT = bc4[s4][:, 8 + g, tk]
            hs = slice(g * 4, (g + 1) * 4)
            cs = slice(g * 256, (g + 1) * 256)
            S.op("pe", I("matmul", out=bank(1)[:, 0:128], lhsT=BT, rhs=CT, start=True, stop=True),
                 r=[("bc4", s4)], w=[("bank", 1)])
            S.op("dve", I("tensor_tensor", out=cbm[x], in0=bank(1)[:, 0:128], in1=U_f, op=ALU.mult),
                 r=[("bank", 1), "cst"], w=[("cbm", x)])
            S.op("pool", I("tensor_tensor", out=lseg[x], in0=Ls_f.unsqueeze(1).to_broadcast([128, 4, 128]),
                           in1=la[:, hs].unsqueeze(2).to_broadcast([128, 4, 128]), op=ALU.mult),
                 r=["cst", ("LAs", c)], w=[("lseg", x)])
            S.op("pe", [I("matmul", out=bank(2 + x)[:, e * 128:(e + 1) * 128], lhsT=lseg[x][:, e, :], rhs=U_f,
                          start=True, stop=True) for e in range(4)],
                 r=[("lseg", x), "cst"], w=[("bank", 2 + x)])
            S.op("act", I("activation", out=dec[x], in_=bank(2 + x).rearrange("p (e l) -> p e l", e=4),
                          func=AF.Exp),
                 r=[("bank", 2 + x)], w=[("dec", x)])
            S.op("dve", I("tensor_tensor", out=MT[x], in0=dec[x],
                          in1=cbm[x].unsqueeze(1).to_broadcast([128, 4, 128]), op=ALU.mult),
                 r=[("dec", x), ("cbm", x)], w=[("MT", x)])
            mm = [I("matmul", out=bank(4 + x)[:, e * 64:(e + 1) * 64], lhsT=MT[x][:, e, :],
                    rhs=xdt[s][:, g * 4 + e, :], start=True, stop=True)
                  for e in range(4)]
            mm.append(I("matmul", out=bank(4 + x)[:, 256:512], lhsT=CT, rhs=Hbf[:, cs], start=True, stop=True))
            S.op("pe", mm, r=[("MT", x), ("xdt", s), ("bc4", s4), ("Hbf", g)], w=[("bank", 4 + x)])
            S.op("dve", I("tensor_tensor", out=tt_[x],
                          in0=bank(4 + x)[:, 256:512].rearrange("p (e q) -> p e q", e=4),
                          in1=ex[s][:, hs].unsqueeze(2).to_broadcast([128, 4, 64]), op=ALU.mult),
                 r=[("bank", 4 + x), ("ex", s)], w=[("tt", x)])
            S.op("pool", I("tensor_tensor", out=tt_[x], in0=tt_[x], in1=dx[s][:, hs, :], op=ALU.add),
                 r=[("tt", x), ("dx", s)], w=[("tt", x)])
            S.op("dve", I("tensor_tensor", out=ybf[s][:, cs].rearrange("p (e q) -> p e q", e=4),
                          in0=bank(4 + x)[:, 0:256].rearrange("p (e q) -> p e q", e=4), in1=tt_[x], op=ALU.add),
                 r=[("bank", 4 + x), ("tt", x)], w=[("ybf", s, g)])
            S.op("pe", I("matmul", out=bank(6 + x)[:, 0:256], lhsT=b_t[s][:, g * 128:(g + 1) * 128],
                         rhs=xds[s][:, hs, :].rearrange("p h e -> p (h e)"), start=True, stop=True),
                 r=[("b_t", s), ("xds", s)], w=[("bank", 6 + x)])
            Hg = H[:, cs].rearrange("p (e q) -> p e q", e=4)
            S.op("pool", I("tensor_tensor", out=Hg, in0=Hg,
                           in1=ex[s][:, 64 + g * 4:64 + (g + 1) * 4].unsqueeze(2).to_broadcast([128, 4, 64]),
                           op=ALU.mult),
                 r=[("H", g), ("ex", s)], w=[("H", g)])
            S.op("dve", I("tensor_tensor", out=H[:, cs], in0=H[:, cs], in1=bank(6 + x)[:, 0:256], op=ALU.add),
                 r=[("H", g), ("bank", 6 + x)], w=[("H", g)])
            S.op("act", I("copy", out=Hbf[:, cs], in_=H[:, cs]), r=[("H", g)], w=[("Hbf", g)])
        S.dma(C["YS"][c * 128:(c + 1) * 128, :], ybf[s], r=[("ybf", s, g) for g in range(8)])


def phaseF(C):
    S, sb, T, l, NT = C["S"], C["sb"], C["T"], C["l"], C["NT"]
    bank, bank_bf, PS, ident_bf, rstd_ops = C["bank"], C["bank_bf"], C["PS"], C["ident_bf"], C["rstd_ops"]
    P = C["P"]
    x_src, x_dst = C["x_src"], C["x_dst"]
    sb.cur = C["persist_mark0"]
    wso = sb.alloc([128, 16, 1024], BF16)
    wao = sb.alloc([128, 8, 1024], BF16)
    wmo = sb.alloc([128, 8, 1024], BF16)
    wo = sb.alloc([128, 8, 1024], BF16)
    ssdn = sb.alloc([128, 2048], F32)
    npost = sb.alloc([128, 1024], F32)
    KmT = sb.alloc([128, 8, 256], BF16)
    Vm1 = sb.alloc([128, 2, 4, 257], BF16)
    fmark = sb.cur
    wtmp = [sb.alloc([128, 4, 1024], F32) for _ in range(2)]
    wi = 0
    for (dstw, src, nkc) in ((wso, P["w_ssd_out"][l], 16), (wao, P["w_attn_out"][l], 8),
                             (wmo, P["w_mem_out"][l], 8), (wo, P["w_out"][l], 8)):
        srcv = src.rearrange("(kc p) n -> p kc n", p=128)
        for k0 in range(0, nkc, 4):
            s = wi % 2
            wi += 1
            S.dma(wtmp[s], srcv[:, k0:k0 + 4, :], w=[("wtmp", s)])
            S.op("pool" if wi % 2 else "dve", I("tensor_copy", out=dstw[:, k0:k0 + 4, :], in_=wtmp[s]),
                 r=[("wtmp", s)], w=[("wres", wi)])
    S.dma(ssdn, P["ssd_norm"][l:l + 1, :].partition_broadcast(128), w=["ssdn"])
    S.dma(npost, P["norm_post"][l:l + 1, :].partition_broadcast(128), w=["npost"])
    mnorm = sb.alloc([128, 1024], F32)
    S.dma(mnorm, P["mem_norm"][l:l + 1, :].partition_broadcast(128), w=["mnorm"])
    memT = sb.alloc([128, 8, 256], BF16)
    mx = sb.alloc([128, 1024], F32)
    mxn = sb.alloc([128, 1024], BF16)
    mss = sb.alloc([128, 1], F32)
    mrs = sb.alloc([128, 1], F32)
    for mb in range(2):
        S.dma(mx, C["mem_in"][mb * 128:(mb + 1) * 128, :], w=["mx"])
        S.op("act", I("activation", out=mxn, in_=mx, func=AF.Square, accum_out=mss[:, 0:1]),
             r=["mx"], w=["mss", "mxn"])
        rstd_ops(mss, mrs, D, ["mss"], "mrs")
        S.op("dve", I("scalar_tensor_tensor", out=mxn, in0=mx, scalar=mrs[:, 0:1], in1=mnorm,
                      op0=ALU.mult, op1=ALU.mult), r=["mx", "mrs", "mnorm"], w=["mxn"])
        S.op("pe", [I("transpose", out=bank_bf(4)[:, k * 128:(k + 1) * 128], in_=mxn[:, k * 128:(k + 1) * 128],
                      identity=ident_bf) for k in range(8)], r=["mxn", "ident"], w=[("bank", 4)])
        S.op("act", I("copy", out=memT[:, :, mb * 128:(mb + 1) * 128],
                      in_=bank_bf(4).rearrange("p (k t) -> p k t", k=8)), r=[("bank", 4)], w=[("memT", mb)])
    wkv = sb.alloc([128, 8, 512], BF16)
    wkvf = sb.alloc([128, 8, 512], F32)
    S.op("pool", I("memset", ap=Vm1, constant=1.0), w=["Vm1"])
    for cb in range(4):
        S.dma(wkvf, P["w_mem_kv"][l][:, cb * 512:(cb + 1) * 512].rearrange("(kc p) n -> p kc n", p=128),
              w=["wkvf"])
        S.op("dve", I("tensor_copy", out=wkv, in_=wkvf), r=["wkvf"], w=["wkv"])
        if cb < 2:
            for sub in range(4):
                j = cb * 4 + sub
                S.op("pe", [I("matmul", out=bank(5)[:, 0:256], lhsT=wkv[:, kc, sub * 128:(sub + 1) * 128],
                              rhs=memT[:, kc, :], start=(kc == 0), stop=(kc == 7)) for kc in range(8)],
                     r=["wkv", ("memT", 0), ("memT", 1)], w=[("bank", 5)])
                S.op("act", I("copy", out=KmT[:, j, :], in_=bank(5)[:, 0:256]), r=[("bank", 5)], w=[("KmT", j)])
        else:
            for mb in range(2):
                S.op("pe", [I("matmul", out=bank(5), lhsT=memT[:, kc, mb * 128:(mb + 1) * 128],
                              rhs=wkv[:, kc, :], start=(kc == 0), stop=(kc == 7)) for kc in range(8)],
                     r=["wkv", ("memT", 0), ("memT", 1)], w=[("bank", 5)])
                h0 = (cb - 2) * 2
                S.op("act", I("copy", out=Vm1[:, mb, h0:h0 + 2, 0:256],
                              in_=bank(5).rearrange("p (h e) -> p h e", h=2)),
                     r=[("bank", 5), "Vm1"], w=[("Vm1w", cb, mb)])
    S.barrier()
    sb.cur = fmark
    ys = [sb.alloc([128, 2048], BF16) for _ in range(2)]
    z1 = [sb.alloc([128, 2048], BF16) for _ in range(2)]
    oa = sb.alloc([128, 3, 1040], F32)
    za = [sb.alloc([128, 1024], BF16) for _ in range(2)]
    qmt = [sb.alloc([128, 8, 128], BF16) for _ in range(2)]
    zm = [sb.alloc([128, 1024], BF16) for _ in range(2)]
    gt = [sb.alloc([128, 3072], BF16) for _ in range(2)]
    xr = [sb.alloc([128, 1024], F32) for _ in range(2)]
    t1 = sb.alloc([128, 2048], F32)
    abf = sb.alloc([128, 2048], BF16)
    actT = sb.alloc([128, 16, 128], BF16)
    merged = sb.alloc([128, 1024], F32)
    num = sb.alloc([128, 1040], F32)
    tmp = sb.alloc([128, 1024], F32)
    PmT = [sb.alloc([128, 4, 128], BF16) for _ in range(2)]
    ssg = sb.alloc([128, 8], F32)
    rs8 = sb.alloc([128, 8], F32)
    rden = sb.alloc([128, 16], F32)
    rdm = sb.alloc([128, 4], F32)
    ssf = sb.alloc([128, 1], F32)
    rsf = sb.alloc([128, 1], F32)
    om = t1[:, 0:1024]
    xo = t1[:, 1024:2048]
    junk = abf

    def loads(i):
        s = i % 2
        tk = slice(i * 128, (i + 1) * 128)
        S.dma(ys[s], C["YS"][tk, :], w=[("ys", s)])
        S.dma(z1[s], C["Z1"][tk, :], w=[("z1", s)])
        S.dma(za[s], C["ZA"][tk, :], w=[("za", s)])
        S.dma(qmt[s], C["QMT"][:, tk].rearrange("(j p) t -> p j t", p=128), w=[("qmt", s)])
        S.dma(zm[s], C["ZM"][tk, :], w=[("zm", s)])
        S.dma(gt[s], C["G"][tk, :], w=[("gt", s)])
        S.dma(xr[s], x_src[tk, :], w=[("xr", s)])

    def transposes(src, nk, banks, key_r):
        for b0 in range(0, nk, 8):
            bk = banks[b0 // 8]
            S.op("pe", [I("transpose", out=bank_bf(bk)[:, k * 128:(k + 1) * 128],
                          in_=src[:, (b0 + k) * 128:(b0 + k + 1) * 128], identity=ident_bf) for k in range(8)],
                 r=[key_r, "ident"], w=[("bank", bk)])
            S.op("act", I("copy", out=actT[:, b0:b0 + 8, :], in_=bank_bf(bk).rearrange("p (k t) -> p k t", k=8)),
                 r=[("bank", bk)], w=[("actT", b0 // 8)])

    def outproj(wt, nk, pso, okey):
        for hh in range(2):
            S.op("pe", [I("matmul", out=pso[:, hh * 512:(hh + 1) * 512], lhsT=actT[:, kc, :],
                          rhs=wt[:, kc, hh * 512:(hh + 1) * 512], start=(kc == 0), stop=(kc == nk - 1))
                        for kc in range(nk)],
                 r=[("actT", 0), ("actT", 1)], w=[(okey, hh)])

    loads(0)
    for i in range(NT):
        s = i % 2
        tk = slice(i * 128, (i + 1) * 128)
        for g in range(3):
            S.dma(oa[:, g, :], C["OA"][g][tk, :], w=[("oa", g)])
        if i + 1 < NT:
            loads(i + 1)
        S.op("pool", I("tensor_tensor", out=num, in0=oa[:, 0, :], in1=oa[:, 1, :], op=ALU.add),
             r=[("oa", 0), ("oa", 1)], w=["num"])
        S.op("pool", I("tensor_tensor", out=num, in0=num, in1=oa[:, 2, :], op=ALU.add),
             r=[("oa", 2), "num"], w=["num"])
        numv = num.rearrange("p (h e) -> p h e", e=65)
        S.op("dve", I("reciprocal", out=rden, in_=numv[:, :, 64]), r=["num"], w=["rden"])
        S.op("dve", I("tensor_tensor", out=tmp.rearrange("p (h e) -> p h e", e=64), in0=numv[:, :, 0:64],
                      in1=rden.unsqueeze(2).to_broadcast([128, 16, 64]), op=ALU.mult),
             r=["num", "rden"], w=["tmp"])
        S.op("pool", I("tensor_tensor", out=abf[:, 0:1024], in0=tmp, in1=za[s], op=ALU.mult),
             r=["tmp", ("za", s)], w=["abf"])
        transposes(abf, 8, [4], "abf")
        outproj(wao, 8, PS[3], "psB")
        S.op("dve", I("tensor_tensor", out=merged, in0=PS[3], in1=gt[s][:, 1024:2048], op=ALU.mult),
             r=[("psB", 0), ("psB", 1), ("gt", s)], w=["merged"])
        S.op("dve", I("tensor_tensor", out=t1, in0=ys[s], in1=z1[s], op=ALU.mult),
             r=[("ys", s), ("z1", s)], w=["t1a", "t1b"])
        S.op("act", [I("activation", out=junk[:, 0:256], in_=t1[:, g * 256:(g + 1) * 256], func=AF.Square,
                       accum_out=ssg[:, g:g + 1]) for g in range(8)], r=["t1a", "t1b"], w=["ssg", "abf"])
        rstd_ops(ssg, rs8, 256, ["ssg"], "rs8")
        S.op("dve", I("tensor_tensor", out=t1.rearrange("p (g e) -> p g e", g=8),
                      in0=t1.rearrange("p (g e) -> p g e", g=8),
                      in1=rs8.unsqueeze(2).to_broadcast([128, 8, 256]), op=ALU.mult),
             r=["t1a", "t1b", "rs8"], w=["t1a", "t1b"])
        S.op("pool", I("tensor_tensor", out=abf, in0=t1, in1=ssdn, op=ALU.mult), r=["t1a", "t1b", "ssdn"], w=["abf"])
        transposes(abf, 16, [0, 1], "abf")
        outproj(wso, 16, PS[1], "psA")
        S.op("dve", I("tensor_tensor", out=tmp, in0=PS[1], in1=gt[s][:, 0:1024], op=ALU.mult),
             r=[("psA", 0), ("psA", 1), ("gt", s)], w=["tmp"])
        S.op("pool", I("tensor_tensor", out=merged, in0=merged, in1=tmp, op=ALU.add),
             r=["tmp", "merged"], w=["merged"])
        for hp in range(2):
            mm = []
            for hh in range(2):
                h = hp * 2 + hh
                for mb in range(2):
                    for ec in range(2):
                        mm.append(I("matmul", out=bank(5)[:, (hh * 2 + mb) * 128:(hh * 2 + mb + 1) * 128],
                                    lhsT=KmT[:, h * 2 + ec, mb * 128:(mb + 1) * 128], rhs=qmt[s][:, h * 2 + ec, :],
                                    start=(ec == 0), stop=(ec == 1)))
            S.op("pe", mm, r=[("qmt", s)], w=[("bank", 5)])
            S.op("act", I("activation", out=PmT[hp], in_=bank(5).rearrange("p (j t) -> p j t", j=4),
                          func=AF.Exp, scale=1.0 / 16.0), r=[("bank", 5)], w=[("PmT", hp)])
            for hh in range(2):
                h = hp * 2 + hh
                S.op("pe", [I("matmul", out=bank(hh)[:, 0:257], lhsT=PmT[hp][:, hh * 2 + mb, :],
                              rhs=Vm1[:, mb, h, :], start=(mb == 0), stop=(mb == 1)) for mb in range(2)],
                     r=[("PmT", hp)], w=[("bank", hh)])
                S.op("dve", I("reciprocal", out=rdm[:, h:h + 1], in_=bank(hh)[:, 256:257]),
                     r=[("bank", hh)], w=[("rdm", h)])
                S.op("dve", I("tensor_scalar", out=om[:, h * 256:(h + 1) * 256], in0=bank(hh)[:, 0:256],
                              scalar1=rdm[:, h:h + 1], scalar2=None, op0=ALU.mult),
                     r=[("bank", hh), ("rdm", h)], w=["t1a"])
        S.op("pool", I("tensor_tensor", out=abf[:, 0:1024], in0=om, in1=zm[s], op=ALU.mult),
             r=["t1a", ("zm", s)], w=["abf"])
        transposes(abf, 8, [4], "abf")
        outproj(wmo, 8, PS[1], "psA")
        S.op("dve", I("tensor_tensor", out=tmp, in0=PS[1], in1=gt[s][:, 2048:3072], op=ALU.mult),
             r=[("psA", 0), ("psA", 1), ("gt", s)], w=["tmp"])
        S.op("pool", I("tensor_tensor", out=merged, in0=merged, in1=tmp, op=ALU.add),
             r=["tmp", "merged"], w=["merged"])
        S.op("act", I("copy", out=abf[:, 0:1024], in_=merged), r=["merged"], w=["abf"])
        transposes(abf, 8, [4], "abf")
        outproj(wo, 8, PS[3], "psB")
        S.op("act", I("activation", out=junk[:, 0:1024], in_=PS[3], func=AF.Square, accum_out=ssf[:, 0:1]),
             r=[("psB", 0), ("psB", 1)], w=["ssf", "abf"])
        rstd_ops(ssf, rsf, D, ["ssf"], "rsf")
        S.op("dve", I("scalar_tensor_tensor", out=xo, in0=PS[3], scalar=rsf[:, 0:1], in1=npost,
                      op0=ALU.mult, op1=ALU.mult),
             r=[("psB", 0), ("psB", 1), "rsf", "npost"], w=["t1b"])
        S.op("pool", I("tensor_tensor", out=xo, in0=xo, in1=xr[s], op=ALU.add), r=["t1b", ("xr", s)], w=["t1b"])
        S.dma(x_dst[tk, :], xo, r=["t1b"])


WNAMES = ["norm_pre", "norm_post", "w_in", "conv_w", "conv_b", "dt_bias", "a_log", "d_skip", "ssd_norm",
          "w_ssd_out", "w_attn_out", "mem_norm", "w_mem_kv", "w_mem_out", "w_out"]
_PROG = {}
FUSED = True


def _get_prog(T, NL):
    key = (T, NL)
    if key not in _PROG:
        _PROG[key] = build_program(T, NL)
    return _PROG[key]


def prep_weights(inputs):
    w = {k: np.ascontiguousarray(np.asarray(inputs[k], dtype=np.float32)) for k in WNAMES}
    nl = w["conv_w"].shape[0]
    cw = w["conv_w"].reshape(nl, 4, 32, 128).transpose(0, 3, 2, 1)
    w["conv_w"] = np.ascontiguousarray(cw.reshape(nl, 128, 128))
    cb = w["conv_b"].reshape(nl, 32, 128).transpose(0, 2, 1)
    w["conv_b"] = np.ascontiguousarray(cb)
    return w


def kernel(**inputs):
    x = np.ascontiguousarray(np.asarray(inputs["x"], dtype=np.float32))
    mem = np.ascontiguousarray(np.asarray(inputs["mem"], dtype=np.float32))
    B, T, _ = x.shape
    consts = make_consts()
    w = prep_weights(inputs)
    depth = w["w_in"].shape[0]
    if FUSED:
        nc = _get_prog(T, depth)
        in_maps = []
        for b in range(B):
            m = {"x": x[b], "mem": mem[b], "consts": consts}
            m.update(w)
            in_maps.append(m)
        res = run_bass_kernel_spmd(nc, in_maps, core_ids=list(range(B)))
        return np.stack([np.asarray(r["out"]) for r in res.results], axis=0).astype(np.float32)
    cur = [x[b] for b in range(B)]
    nc = _get_prog(T, 1)
    for l in range(depth):
        in_maps = []
        for b in range(B):
            m = {"x": cur[b], "mem": mem[b], "consts": consts}
            m.update({k: w[k][l:l + 1] for k in WNAMES})
            in_maps.append(m)
        res = run_bass_kernel_spmd(nc, in_maps, core_ids=list(range(B)))
        cur = [np.ascontiguousarray(np.asarray(r["out"], dtype=np.float32)) for r in res.results]
    return np.stack(cur, axis=0).astype(np.float32)
```

```python
import numpy as np
import concourse.bass as bass
import concourse.mybir as mybir
from concourse.bass_utils import run_bass_kernel_spmd

F32 = mybir.dt.float32
BF16 = mybir.dt.bfloat16
AF = mybir.ActivationFunctionType
ALU = mybir.AluOpType

D = 1024
DEPTH = 2
EPS = 1e-6
D_INNER = 2048
NH = 32
NG = 8
DS = 128
MEM_LEN = 256
OFF_ZSSD = 0
OFF_XBC = 2048
OFF_DT = 6144
OFF_QKV = 6176
OFF_ZATT = OFF_QKV + 9216
OFF_QMEM = OFF_ZATT + 1024
OFF_ZMEM = OFF_QMEM + 1024
OFF_GATE = OFF_ZMEM + 1024
N_IN = OFF_GATE + 3072
DIL = (1, 4, 16)
N_DMA_SEMS = 24


class Sched:
    ENGS = ("pe", "act", "dve", "pool", "sp")

    def __init__(self):
        self.ops = []

    def op(self, eng, instrs, r=(), w=()):
        if isinstance(instrs, tuple):
            instrs = [instrs]
        instrs = list(instrs)

        def fn(e, instrs=instrs):
            ins = None
            for m, kw in instrs:
                ins = getattr(e, m)(**kw)
            return ins
        r, w = list(r), list(w)
        for k in r:
            if isinstance(k, tuple) and k and k[0] == "bank" and k not in w:
                w.append(k)
        self.ops.append(dict(eng=eng, fn=fn, r=tuple(r), w=tuple(w), dma=False))

    def dma(self, out, in_, r=(), w=(), eng="sp"):
        def fn(e, out=out, in_=in_):
            return e.dma_start(out=out, in_=in_)
        self.ops.append(dict(eng=eng, fn=fn, r=tuple(r), w=tuple(w), dma=True))

    def barrier(self):
        self.ops.append(dict(barrier=True))

    def emit(self, nc):
        ops = self.ops
        n = len(ops)
        last_w = {}
        readers = {}
        deps = [None] * n
        dma_sem_of = [None] * n
        dma_rr = 0
        dma_last_use = [None] * N_DMA_SEMS
        bar_pending = {}
        last_op_of = {}
        outstanding = set()
        for i, o in enumerate(ops):
            if o.get("barrier"):
                allprev = set(outstanding)
                for e in self.ENGS:
                    bar_pending[e] = bar_pending.get(e, set()) | allprev
                outstanding = set()
                last_w.clear()
                readers.clear()
                continue
            d = set()
            for k in o["r"]:
                if k in last_w:
                    d.add(last_w[k])
            for k in o["w"]:
                if k in last_w:
                    d.add(last_w[k])
                for rr in readers.get(k, ()):
                    d.add(rr)
            if o["eng"] in bar_pending and bar_pending[o["eng"]]:
                d |= bar_pending[o["eng"]]
                bar_pending[o["eng"]] = set()
            if o["dma"]:
                j = dma_rr % N_DMA_SEMS
                dma_rr += 1
                dma_sem_of[i] = j
                if dma_last_use[j] is not None:
                    d.add(dma_last_use[j])
                dma_last_use[j] = i
            d.discard(i)
            deps[i] = d
            for k in o["r"]:
                readers.setdefault(k, []).append(i)
            for k in o["w"]:
                last_w[k] = i
                readers[k] = []
            if o["dma"]:
                outstanding.add(i)
            else:
                prev = last_op_of.get(o["eng"])
                if prev is not None:
                    outstanding.discard(prev)
                last_op_of[o["eng"]] = i
                outstanding.add(i)
        self.final_wait = set(outstanding) | bar_pending.get("sp", set())
        needed = [False] * n
        for i, o in enumerate(ops):
            if o.get("barrier"):
                continue
            for dd in deps[i]:
                if ops[dd]["eng"] == "pe" and o["eng"] == "pe" and not ops[dd]["dma"]:
                    continue
                needed[dd] = True
        for dd in self.final_wait:
            needed[dd] = True
        cnt = {e: 0 for e in self.ENGS}
        dcnt = [0] * N_DMA_SEMS
        event = [None] * n
        for i, o in enumerate(ops):
            if o.get("barrier"):
                continue
            if o["dma"]:
                j = dma_sem_of[i]
                dcnt[j] += 16
                event[i] = (("d", j), dcnt[j])
            elif needed[i]:
                cnt[o["eng"]] += 1
                event[i] = (("e", o["eng"]), cnt[o["eng"]])
        self.stats = (dict(cnt), max(dcnt), n)
        import contextlib
        with contextlib.ExitStack() as st:
            esem = {e: st.enter_context(nc.semaphore("s_" + e)) for e in self.ENGS}
            dsem = [st.enter_context(nc.semaphore("d_%d" % j)) for j in range(N_DMA_SEMS)]
            block = st.enter_context(nc.Block())

            def sem_of(key):
                return esem[key[1]] if key[0] == "e" else dsem[key[1]]

            def run(engname, eng):
                known = {}
                for i, o in enumerate(ops):
                    if o.get("barrier") or o["eng"] != engname:
                        continue
                    waits = {}
                    for dd in deps[i]:
                        if ops[dd]["eng"] == "pe" and engname == "pe" and not ops[dd]["dma"]:
                            continue
                        key, val = event[dd]
                        if known.get(key, 0) >= val:
                            continue
                        waits[key] = max(waits.get(key, 0), val)
                    for key, val in waits.items():
                        eng.wait_ge(sem_of(key), val)
                        known[key] = val
                    ins = o["fn"](eng)
                    if event[i] is not None:
                        key, val = event[i]
                        ins.then_inc(sem_of(key), 16 if key[0] == "d" else 1)
                if engname == "sp":
                    waits = {}
                    for dd in self.final_wait:
                        key, val = event[dd]
                        if known.get(key, 0) >= val:
                            continue
                        waits[key] = max(waits.get(key, 0), val)
                    for key, val in waits.items():
                        eng.wait_ge(sem_of(key), val)

            @block.tensor
            def _(e):
                run("pe", e)

            @block.scalar
            def _(e):
                run("act", e)

            @block.vector
            def _(e):
                run("dve", e)

            @block.gpsimd
            def _(e):
                run("pool", e)

            @block.sync
            def _(e):
                run("sp", e)


SB_BASE = 16512
SB_LIMIT = 229376 - 256


class SBAlloc:
    def __init__(self, nc):
        self.nc = nc
        self.cur = SB_BASE
        self.n = 0

    def alloc(self, shape, dt):
        nb = 1
        for s in shape[1:]:
            nb *= s
        nb *= 4 if dt == F32 else 2
        off = self.cur
        self.cur += (nb + 63) // 64 * 64
        assert self.cur <= SB_LIMIT, ("SBUF overflow", self.cur)
        self.n += 1
        return self.nc.alloc_sbuf_tensor_at("sb%d" % self.n, list(shape), dt, offset=off).ap()


def I(m, **kw):
    return (m, kw)


def make_consts():
    p = np.arange(128)[:, None]
    j = np.arange(128)[None, :]
    ident = (p == j)
    U = (p <= j)
    Ls = (p > j)
    ones = np.ones((128, 128), bool)
    Ge = (p >= j)
    return np.concatenate([ident, U, Ls, ones, Ge], axis=1).astype(np.float32)


def build_program(T, NL, dbg=()):
    nc = bass.Bass("TRN2", target_bir_lowering=False)
    S = Sched()
    NT = T // 128
    HALF = min(T, 4096)
    NHALF = T // HALF
    NTT = HALF // 512

    def dram(name, shape, dt, kind="Internal"):
        if name in dbg:
            kind = "ExternalOutput"
        return nc.dram_tensor(name, list(shape), dt, kind=kind).ap()

    x_in = dram("x", [T, D], F32, "ExternalInput")
    mem_in = dram("mem", [MEM_LEN, D], F32, "ExternalInput")
    consts_in = dram("consts", [128, 640], F32, "ExternalInput")
    P = {}
    for name, shp in [("norm_pre", [NL, D]), ("norm_post", [NL, D]), ("w_in", [NL, D, N_IN]),
                      ("conv_w", [NL, 128, 128]), ("conv_b", [NL, 128, 32]), ("dt_bias", [NL, NH]),
                      ("a_log", [NL, NH]), ("d_skip", [NL, NH]), ("ssd_norm", [NL, D_INNER]),
                      ("w_ssd_out", [NL, D_INNER, D]), ("w_attn_out", [NL, D, D]),
                      ("mem_norm", [NL, D]), ("w_mem_kv", [NL, D, 2 * D]),
                      ("w_mem_out", [NL, D, D]), ("w_out", [NL, D, D])]:
        P[name] = dram(name, shp, F32, "ExternalInput")
    out = dram("out", [T, D], F32, "ExternalOutput")
    xmid = [dram("xmid%d" % i, [T, D], F32) for i in range(NL - 1)]
    XS = dram("XS", [T, 2048], BF16)
    BTOK = dram("BTOK", [T, 1024], BF16)
    BCT = dram("BCT", [2048, T], BF16)
    Z1 = dram("Z1", [T, 2048], BF16)
    QT = [dram("QT%d" % g, [1024, T], BF16) for g in range(3)]
    KT = [dram("KT%d" % g, [1024, T], BF16) for g in range(3)]
    V = [dram("V%d" % g, [T, 1024], BF16) for g in range(3)]
    ZA = dram("ZA", [T, 1024], BF16)
    QMT = dram("QMT", [1024, T], BF16)
    ZM = dram("ZM", [T, 1024], BF16)
    G = dram("G", [T, 3072], BF16)
    YS = dram("YS", [T, 2048], BF16)
    OA = [dram("OA%d" % g, [T, 1040], F32) for g in range(3)]

    sb = SBAlloc(nc)
    PS = [nc.alloc_psum_tensor("ps%d" % i, [128, 1024], F32).ap() for i in range(4)]

    def bank(i):
        return PS[i // 2][:, (i % 2) * 512:(i % 2) * 512 + 512]

    def bank_bf(i):
        return bank(i).bitcast(BF16)

    cst = sb.alloc([128, 640], F32)
    ident_bf = sb.alloc([128, 128], BF16)
    mask2 = sb.alloc([128, 2, 128], BF16)
    ones_bf = sb.alloc([128, 128], BF16)
    persist_mark0 = sb.cur
    DTs = sb.alloc([128, NT, 32], F32)
    LAs = sb.alloc([128, NT, 32], F32)
    halo = sb.alloc([128, 32, 3], F32)
    ident_f = cst[:, 0:128]
    U_f = cst[:, 128:256]
    Ls_f = cst[:, 256:384]
    ones_f = cst[:, 384:512]
    Ge_f = cst[:, 512:640]
    S.dma(cst, consts_in, w=["cst"])
    S.op("dve", I("tensor_copy", out=ident_bf, in_=ident_f), r=["cst"], w=["ident"])
    S.op("dve", I("tensor_copy", out=mask2[:, 0, :], in_=Ge_f), r=["cst"], w=["mask2a"])
    S.op("dve", I("tensor_copy", out=mask2[:, 1, :], in_=U_f), r=["cst"], w=["mask2b"])
    S.op("dve", I("tensor_copy", out=ones_bf, in_=ones_f), r=["cst"], w=["onesbf"])
    persist_mark = sb.cur

    def bcast_load(dst, src_row, key):
        S.dma(dst, src_row.partition_broadcast(128), w=[key])

    def rstd_ops(ss, rstd, n, rkeys, wkey):
        S.op("dve", I("tensor_scalar", out=rstd, in0=ss, scalar1=1.0 / n, scalar2=EPS,
                      op0=ALU.mult, op1=ALU.add), r=rkeys, w=[wkey])
        S.op("act", I("activation", out=rstd, in_=rstd, func=AF.Ln), r=[wkey], w=[wkey])
        S.op("act", I("activation", out=rstd, in_=rstd, func=AF.Exp, scale=-0.5), r=[wkey], w=[wkey])

    C = dict(locals())
    for l in range(NL):
        x_src = x_in if l == 0 else xmid[l - 1]
        x_dst = out if l == NL - 1 else xmid[l]
        S.barrier()
        sb.cur = persist_mark
        gpre = sb.alloc([128, D], F32)
        convw = sb.alloc([128, 32, 4], F32)
        convb = sb.alloc([128, 32], F32)
        dtb = sb.alloc([128, 32], F32)
        abc = sb.alloc([128, 32], F32)
        bcast_load(gpre, P["norm_pre"][l:l + 1, :], "gpre")
        S.dma(convw, P["conv_w"][l].rearrange("p (b k) -> p b k", k=4), w=["convw"])
        S.dma(convb, P["conv_b"][l], w=["convb"])
        bcast_load(dtb, P["dt_bias"][l:l + 1, :], "dtb")
        bcast_load(abc, P["a_log"][l:l + 1, :], "abc0")
        S.op("act", I("activation", out=abc, in_=abc, func=AF.Exp), r=["abc0"], w=["abc0"])
        S.op("act", I("mul", out=abc, in_=abc, mul=-1.0), r=["abc0"], w=["abc"])
        S.op("pool", I("memset", ap=halo, constant=0.0), w=["halo"])
        projmark = sb.cur
        for hf in range(NHALF):
            sb.cur = projmark
            t0h = hf * HALF
            hT = sb.alloc([128, 8, HALF], BF16)
            nmark = sb.cur
            xin = [sb.alloc([128, D], F32) for _ in range(2)]
            xn = [sb.alloc([128, D], BF16) for _ in range(2)]
            junk = sb.alloc([128, D], BF16)
            ss = [sb.alloc([128, 1], F32) for _ in range(2)]
            rs = [sb.alloc([128, 1], F32) for _ in range(2)]
            for i in range(HALF // 128):
                s_ = i % 2
                tok = t0h + i * 128
                S.dma(xin[s_], x_src[tok:tok + 128, :], w=[("xin", s_)])
                S.op("act", I("activation", out=junk, in_=xin[s_], func=AF.Square,
                              accum_out=ss[s_][:, 0:1]),
                     r=[("xin", s_)], w=[("ss", s_), "junk"])
                rstd_ops(ss[s_], rs[s_], D, [("ss", s_)], ("rs", s_))
                S.op("dve", I("scalar_tensor_tensor", out=xn[s_], in0=xin[s_], scalar=rs[s_][:, 0:1],
                              in1=gpre, op0=ALU.mult, op1=ALU.mult),
                     r=[("xin", s_), ("rs", s_), "gpre"], w=[("xn", s_)])
                pb = 6 + s_
                S.op("pe", [I("transpose", out=bank_bf(pb)[:, k * 128:(k + 1) * 128],
                              in_=xn[s_][:, k * 128:(k + 1) * 128], identity=ident_bf)
                            for k in range(8)],
                     r=[("xn", s_), "ident"], w=[("bank", pb)])
                S.op("act", I("copy", out=hT[:, :, i * 128:(i + 1) * 128],
                              in_=bank_bf(pb).rearrange("p (k t) -> p k t", k=8)),
                     r=[("bank", pb)], w=[("hT", i // 4)])
            S.barrier()
            sb.cur = nmark
            C.update(locals())
            phaseP(C)
        S.barrier()
        sb.cur = persist_mark
        C.update(locals())
        for _ in phaseA(C):
            pass
        S.barrier()
        sb.cur = persist_mark
        for _ in phaseS(C):
            pass
        S.barrier()
        sb.cur = persist_mark
        phaseF(C)
    S.emit(nc)
    return nc


def phaseP(C):
    S, sb, l, hT = C["S"], C["sb"], C["l"], C["hT"]
    NTT, t0h, hf, NHALF, HALF = C["NTT"], C["t0h"], C["hf"], C["NHALF"], C["HALF"]
    bank, bank_bf, ident_bf = C["bank"], C["bank_bf"], C["ident_bf"]
    convw, convb, halo, dtb, abc = C["convw"], C["convb"], C["halo"], C["dtb"], C["abc"]
    DTs, LAs = C["DTs"], C["LAs"]
    W = C["P"]["w_in"][l]
    wst = [sb.alloc([128, 8, 512], F32) for _ in range(2)]
    wbf = [sb.alloc([128, 8, 512], BF16) for _ in range(2)]
    stage = [sb.alloc([128, 4, 512], BF16) for _ in range(2)]
    U = [[sb.alloc([128, 515], F32) for _ in range(2)] for _ in range(4)]
    acc = [sb.alloc([128, 512], F32) for _ in range(4)]
    xc = [sb.alloc([128, 4, 512], BF16) for _ in range(2)]
    stT = [sb.alloc([128, 4, 512], BF16) for _ in range(2)]
    wdt = sb.alloc([128, 8, 32], F32)
    wdtb = sb.alloc([128, 8, 32], BF16)
    dtmp = sb.alloc([128, 16, 32], F32)

    blocks = []
    for j in range(4):
        blocks.append((OFF_ZSSD + j * 512, "tm", C["Z1"], j * 512, AF.Silu))
    for j in range(8):
        blocks.append((OFF_XBC + j * 512, "xbc", None, j * 4, None))
    for g in range(3):
        for j in range(2):
            blocks.append((OFF_QKV + (0 * 3 + g) * 1024 + j * 512, "fm", C["QT"][g], j * 512, None))
        for j in range(2):
            blocks.append((OFF_QKV + (1 * 3 + g) * 1024 + j * 512, "fm", C["KT"][g], j * 512, None))
        for j in range(2):
            blocks.append((OFF_QKV + (2 * 3 + g) * 1024 + j * 512, "tm", C["V"][g], j * 512, AF.Copy))
    for j in range(2):
        blocks.append((OFF_ZATT + j * 512, "tm", C["ZA"], j * 512, AF.Silu))
    for j in range(2):
        blocks.append((OFF_QMEM + j * 512, "fm", C["QMT"], j * 512, None))
    for j in range(2):
        blocks.append((OFF_ZMEM + j * 512, "tm", C["ZM"], j * 512, AF.Silu))
    for j in range(6):
        blocks.append((OFF_GATE + j * 512, "tm", C["G"], j * 512, AF.Sigmoid))

    S.dma(wdt, W[:, OFF_DT:OFF_DT + 32].rearrange("(kc p) n -> p kc n", p=128), w=["wdt"])
    S.op("dve", I("tensor_copy", out=wdtb, in_=wdt), r=["wdt"], w=["wdtb"])
    ntile = HALF // 128
    grp = min(16, ntile)
    for g0 in range(0, ntile, grp):
        bk = 0
        for i in range(g0, g0 + grp):
            S.op("pe", [I("matmul", out=bank(bk)[:, (i - g0) * 32:(i - g0) * 32 + 32],
                          lhsT=hT[:, kc, i * 128:(i + 1) * 128], rhs=wdtb[:, kc, :],
                          start=(kc == 0), stop=(kc == 7)) for kc in range(8)],
                 r=[("hT", i // 4), "wdtb"], w=[("bank", bk)])
        c0 = t0h // 128 + g0
        S.op("dve", I("tensor_tensor", out=dtmp[:, 0:grp, :],
                      in0=bank(bk)[:, 0:grp * 32].rearrange("p (n e) -> p n e", e=32),
                      in1=dtb.unsqueeze(1).to_broadcast([128, grp, 32]), op=ALU.add),
             r=[("bank", bk), "dtb"], w=["dtmp"])
        S.op("act", I("activation", out=dtmp[:, 0:grp, :], in_=dtmp[:, 0:grp, :], func=AF.Exp),
             r=["dtmp"], w=["dtmp"])
        S.op("act", I("activation", out=DTs[:, c0:c0 + grp, :], in_=dtmp[:, 0:grp, :], func=AF.Ln,
                      bias=1.0, scale=1.0),
             r=["dtmp"], w=[("DTs", c0)])
        S.op("dve", I("tensor_tensor", out=LAs[:, c0:c0 + grp, :], in0=DTs[:, c0:c0 + grp, :],
                      in1=abc.unsqueeze(1).to_broadcast([128, grp, 32]), op=ALU.mult),
             r=[("DTs", c0), "abc"], w=[("LAs", c0)])

    def load_w(bi):
        c0 = blocks[bi][0]
        sl = bi % 2
        S.dma(wst[sl], W[:, c0:c0 + 512].rearrange("(kc p) n -> p kc n", p=128), w=[("wst", sl)])
        S.op("pool", I("tensor_copy", out=wbf[sl], in_=wst[sl]), r=[("wst", sl)], w=[("wbf", sl)])

    load_w(0)
    bkrr = [1]
    cnt = [0]
    for bi, (c0, kind, dst, d0, func) in enumerate(blocks):
        if bi + 1 < len(blocks):
            load_w(bi + 1)
        sl = bi % 2
        for tt in range(NTT):
            tok0 = t0h + tt * 512
            st = cnt[0] % 2
            cnt[0] += 1
            for sub in range(4):
                bk = bkrr[0]
                bkrr[0] = bkrr[0] % 5 + 1
                if kind == "tm":
                    mm = [I("matmul", out=bank(bk), lhsT=hT[:, kc, tt * 512 + sub * 128:tt * 512 + sub * 128 + 128],
                            rhs=wbf[sl][:, kc, :], start=(kc == 0), stop=(kc == 7)) for kc in range(8)]
                else:
                    mm = [I("matmul", out=bank(bk), lhsT=wbf[sl][:, kc, sub * 128:(sub + 1) * 128],
                            rhs=hT[:, kc, tt * 512:(tt + 1) * 512], start=(kc == 0), stop=(kc == 7))
                          for kc in range(8)]
                S.op("pe", mm, r=[("wbf", sl), ("hT", tt)], w=[("bank", bk)])
                if kind == "tm":
                    S.op("act", I("activation", out=stage[st][:, sub, :], in_=bank(bk), func=func),
                         r=[("bank", bk)], w=[("stage", st, sub)])
                elif kind == "fm":
                    eng = "act" if sub % 2 == 0 else "dve"
                    S.op(eng, I("copy" if eng == "act" else "tensor_copy", out=stage[st][:, sub, :], in_=bank(bk)),
                         r=[("bank", bk)], w=[("stage", st, sub)])
                else:
                    chb = d0 + sub
                    par = tt % 2
                    Uc, Up = U[sub][par], U[sub][1 - par]
                    S.op("act", I("copy", out=Uc[:, 3:515], in_=bank(bk)),
                         r=[("bank", bk)], w=[("U", sub, par, "m")])
                    if tt == 0:
                        S.op("pool", I("tensor_copy", out=Uc[:, 0:3], in_=halo[:, chb, :]),
                             r=[("halo", chb)], w=[("U", sub, par, "h")])
                    else:
                        S.op("pool", I("tensor_copy", out=Uc[:, 0:3], in_=Up[:, 512:515]),
                             r=[("U", sub, 1 - par, "m")], w=[("U", sub, par, "h")])
                    if tt == NTT - 1 and hf + 1 < NHALF:
                        S.op("pool", I("tensor_copy", out=halo[:, chb, :], in_=Uc[:, 512:515]),
                             r=[("U", sub, par, "m")], w=[("halo", chb)])
            if kind == "xbc":
                par = tt % 2
                for kk in (3, 2, 1, 0):
                    for sub in range(4):
                        chb = d0 + sub
                        Uc = U[sub][par]
                        ukeys = [("U", sub, par, "m"), ("U", sub, par, "h"), "convw", "convb"]
                        if kk == 3:
                            S.op("dve", I("tensor_scalar", out=acc[sub], in0=Uc[:, 3:515], scalar1=convw[:, chb, 3:4],
                                          scalar2=convb[:, chb:chb + 1], op0=ALU.mult, op1=ALU.add),
                                 r=ukeys, w=[("acc", sub)])
                        else:
                            S.op("dve", I("scalar_tensor_tensor", out=acc[sub], in0=Uc[:, kk:kk + 512],
                                          scalar=convw[:, chb, kk:kk + 1], in1=acc[sub], op0=ALU.mult, op1=ALU.add),
                                 r=ukeys + [("acc", sub)], w=[("acc", sub)])
                for sub in range(4):
                    S.op("act", I("activation", out=xc[st][:, sub, :], in_=acc[sub], func=AF.Silu),
                         r=[("acc", sub)], w=[("xc", st, sub)])
                if d0 < 24:
                    for sub in range(4):
                        tb = 6 + (sub % 2)
                        S.op("pe", [I("transpose", out=bank_bf(tb)[:, j * 128:(j + 1) * 128],
                                      in_=xc[st][:, sub, j * 128:(j + 1) * 128], identity=ident_bf)
                                    for j in range(4)],
                             r=[("xc", st, sub), "ident"], w=[("bank", tb)])
                        eng = "act" if sub % 2 == 0 else "pool_no"
                        eng = "act" if sub % 2 == 0 else "dve"
                        S.op(eng, I("copy" if eng == "act" else "tensor_copy",
                                    out=stT[st][:, :, sub * 128:(sub + 1) * 128],
                                    in_=bank_bf(tb)[:, 0:512].rearrange("p (j c) -> p j c", j=4)),
                             r=[("bank", tb)], w=[("stT", st, sub)])
            if kind == "tm":
                S.dma(dst[tok0:tok0 + 512, d0:d0 + 512].rearrange("(s p) c -> p s c", p=128), stage[st],
                      r=[("stage", st, s_) for s_ in range(4)])
            elif kind == "fm":
                S.dma(dst[d0:d0 + 512, tok0:tok0 + 512].rearrange("(s p) t -> p s t", p=128), stage[st],
                      r=[("stage", st, s_) for s_ in range(4)])
            else:
                chb0 = d0
                if chb0 < 16:
                    S.dma(C["XS"][tok0:tok0 + 512, chb0 * 128:chb0 * 128 + 512].rearrange("(j p) c -> p j c", p=128),
                          stT[st], r=[("stT", st, s_) for s_ in range(4)])
                elif chb0 < 24:
                    cc = (chb0 - 16) * 128
                    S.dma(C["BTOK"][tok0:tok0 + 512, cc:cc + 512].rearrange("(j p) c -> p j c", p=128),
                          stT[st], r=[("stT", st, s_) for s_ in range(4)])
                if chb0 >= 16:
                    r0 = (chb0 - 16) * 128
                    S.dma(C["BCT"][r0:r0 + 512, tok0:tok0 + 512].rearrange("(s p) t -> p s t", p=128),
                          xc[st], r=[("xc", st, s_) for s_ in range(4)])


def phaseA(C):
    S, sb, T = C["S"], C["sb"], C["T"]
    bank, PS, mask2, ones_bf = C["bank"], C["PS"], C["mask2"], C["ones_bf"]
    NBLK = T // 128
    q2 = [sb.alloc([128, T], BF16) for _ in range(2)]
    k2 = [sb.alloc([128, T], BF16) for _ in range(2)]
    v2 = [sb.alloc([128, NBLK, 128], BF16) for _ in range(1)]
    PT = [sb.alloc([128, 2, 2, 128], BF16) for _ in range(2)]
    ost = [sb.alloc([128, 8, 130], F32) for _ in range(2)]

    def psS(x, hh):
        return bank(2 * x + hh)[:, 0:256]

    def psO(slot):
        return bank(4 + slot)[:, 0:130]
    combo = 0
    ostc = 0
    for g in range(3):
        d = DIL[g]
        nb = T // d // 128
        OB = min(nb, 8)
        for hp in range(8):
            sl = (g * 8 + hp) % 2
            rows = slice(hp * 128, (hp + 1) * 128)
            S.dma(q2[sl], C["QT"][g][rows, :], w=[("q2", sl)])
            S.dma(k2[sl], C["KT"][g][rows, :], w=[("k2", sl)])
            vsrc = C["V"][g][:, rows].rearrange("(b i r) c -> r i b c", i=128, r=d)
            for r in range(d):
                S.dma(v2[0][:, r * nb:(r + 1) * nb, :], vsrc[r], w=[("v2", r)])
            qS = q2[sl].rearrange("p (m r) -> p r m", r=d)
            kS = k2[sl].rearrange("p (m r) -> p r m", r=d)
            odst = C["OA"][g][:, hp * 130:(hp + 1) * 130].rearrange("(b i r) c -> r i b c", i=128, r=d)
            for r in range(d):
                for b in range(nb):
                    blk = r * nb + b
                    x = combo % 2
                    combo += 1
                    mm = []
                    for hh in range(2):
                        pr = slice(hh * 64, (hh + 1) * 64)
                        bk = psS(x, hh)
                        if b > 0:
                            mm.append(I("matmul", out=bk[:, 0:128], lhsT=kS[pr, r, (b - 1) * 128:b * 128],
                                        rhs=qS[pr, r, b * 128:(b + 1) * 128], start=True, stop=True))
                        mm.append(I("matmul", out=bk[:, 128:256], lhsT=kS[pr, r, b * 128:(b + 1) * 128],
                                    rhs=qS[pr, r, b * 128:(b + 1) * 128], start=True, stop=True))
                    S.op("pe", mm, r=[("q2", sl), ("k2", sl)], w=[("bank", 2 * x), ("bank", 2 * x + 1)])
                    lo = 0 if b > 0 else 128
                    pin = PS[x].rearrange("p (h c) -> p h c", h=2)[:, :, lo:256]
                    pout = PT[x].rearrange("p h t q -> p h (t q)")[:, :, lo:256]
                    S.op("act", I("activation", out=pout, in_=pin, func=AF.Exp, scale=0.125),
                         r=[("bank", 2 * x), ("bank", 2 * x + 1)], w=[("PT", x)])
                    mk = mask2.rearrange("p t q -> p (t q)")[:, lo:256].unsqueeze(1).to_broadcast([128, 2, 256 - lo])
                    S.op("dve" if combo % 2 == 0 else "pool",
                         I("tensor_tensor", out=pout, in0=pout, in1=mk, op=ALU.mult),
                         r=[("PT", x), "mask2a", "mask2b"], w=[("PT", x)])
                    ob = combo % 4
                    mm = []
                    for hh in range(2):
                        o_ = psO(ob)[:, hh * 65:hh * 65 + 64]
                        dn = psO(ob)[:, hh * 65 + 64:hh * 65 + 65]
                        hc = slice(hh * 64, (hh + 1) * 64)
                        if b > 0:
                            mm.append(I("matmul", out=o_, lhsT=PT[x][:, hh, 0, :], rhs=v2[0][:, blk - 1, hc],
                                        start=True, stop=False))
                        mm.append(I("matmul", out=o_, lhsT=PT[x][:, hh, 1, :], rhs=v2[0][:, blk, hc],
                                    start=(b == 0), stop=True))
                        if b > 0:
                            mm.append(I("matmul", out=dn, lhsT=PT[x][:, hh, 0, :], rhs=ones_bf[:, 0:1],
                                        start=True, stop=False))
                        mm.append(I("matmul", out=dn, lhsT=PT[x][:, hh, 1, :], rhs=ones_bf[:, 0:1],
                                    start=(b == 0), stop=True))
                    S.op("pe", mm, r=[("PT", x), ("v2", r), "onesbf"], w=[("bank", 4 + ob)])
                    os_ = ostc % 2
                    eng = "act" if combo % 2 == 0 else "dve"
                    S.op(eng, I("copy" if eng == "act" else "tensor_copy", out=ost[os_][:, b % OB, :],
                                in_=psO(ob)),
                         r=[("bank", 4 + ob)], w=[("ost", os_, b % OB)])
                    if b % OB == OB - 1:
                        b0 = b - (OB - 1)
                        S.dma(odst[r][:, b0:b0 + OB, :], ost[os_][:, 0:OB, :],
                              r=[("ost", os_, j) for j in range(OB)])
                        ostc += 1
                    if combo % 3 == 0:
                        yield


def phaseS(C):
    S, sb, T, l, NT = C["S"], C["sb"], C["T"], C["l"], C["NT"]
    bank, LAs, DTs = C["bank"], C["LAs"], C["DTs"]
    U_f, Ls_f, ones_f = C["U_f"], C["Ls_f"], C["ones_f"]
    H = sb.alloc([128, 2048], F32)
    Hbf = sb.alloc([128, 2048], BF16)
    dbc = sb.alloc([128, 32], F32)
    xs_t = [sb.alloc([128, 2048], BF16) for _ in range(2)]
    b_t = [sb.alloc([128, 1024], BF16) for _ in range(2)]
    bc4 = [sb.alloc([128, 16, 256], BF16) for _ in range(2)]
    ex = [sb.alloc([128, 96], F32) for _ in range(2)]
    xds = [sb.alloc([128, 32, 64], BF16) for _ in range(2)]
    xdt = [sb.alloc([128, 32, 64], BF16) for _ in range(2)]
    dx = [sb.alloc([128, 32, 64], BF16) for _ in range(2)]
    ybf = [sb.alloc([128, 2048], BF16) for _ in range(2)]
    cbm = [sb.alloc([128, 128], F32) for _ in range(2)]
    lseg = [sb.alloc([128, 4, 128], F32) for _ in range(2)]
    dec = [sb.alloc([128, 4, 128], F32) for _ in range(2)]
    MT = [sb.alloc([128, 4, 128], BF16) for _ in range(2)]
    tt_ = [sb.alloc([128, 4, 64], F32) for _ in range(2)]
    S.dma(dbc, C["P"]["d_skip"][l:l + 1, :].partition_broadcast(128), w=["dbc"])
    S.op("pool", I("memset", ap=H, constant=0.0), w=[("H", g) for g in range(8)])
    S.op("pool", I("memset", ap=Hbf, constant=0.0), w=[("Hbf", g) for g in range(8)])

    def loads(c):
        s = c % 2
        S.dma(xs_t[s], C["XS"][c * 128:(c + 1) * 128, :], w=[("xs_t", s)])
        S.dma(b_t[s], C["BTOK"][c * 128:(c + 1) * 128, :], w=[("b_t", s)])
        if c % 2 == 0:
            s4 = (c // 2) % 2
            S.dma(bc4[s4], C["BCT"][:, c * 128:c * 128 + 256].rearrange("(j p) t -> p j t", p=128),
                  w=[("bc4", s4)])

    loads(0)
    xi = 0
    for c in range(NT):
        if c + 1 < NT:
            loads(c + 1)
        s = c % 2
        s4 = (c // 2) % 2
        la = LAs[:, c, :]
        S.op("pe", [I("matmul", out=bank(0)[:, 0:32], lhsT=U_f, rhs=la, start=True, stop=True),
                    I("matmul", out=bank(0)[:, 32:64], lhsT=Ls_f, rhs=la, start=True, stop=True),
                    I("matmul", out=bank(0)[:, 64:96], lhsT=ones_f, rhs=la, start=True, stop=True)],
             r=["cst", ("LAs", c)], w=[("bank", 0)])
        S.op("act", I("activation", out=ex[s], in_=bank(0)[:, 0:96], func=AF.Exp),
             r=[("bank", 0)], w=[("ex", s)])
        S.op("pool", I("tensor_tensor", out=xdt[s], in0=xs_t[s].rearrange("p (h e) -> p h e", e=64),
                       in1=DTs[:, c, :].unsqueeze(2).to_broadcast([128, 32, 64]), op=ALU.mult),
             r=[("xs_t", s), ("DTs", c)], w=[("xdt", s)])
        S.op("dve", I("tensor_tensor", out=xds[s], in0=xdt[s],
                      in1=ex[s][:, 32:64].unsqueeze(2).to_broadcast([128, 32, 64]), op=ALU.mult),
             r=[("xdt", s), ("ex", s)], w=[("xds", s)])
        S.op("pool", I("tensor_tensor", out=dx[s], in0=xs_t[s].rearrange("p (h e) -> p h e", e=64),
                       in1=dbc.unsqueeze(2).to_broadcast([128, 32, 64]), op=ALU.mult),
             r=[("xs_t", s), "dbc"], w=[("dx", s)])
        tk = slice((c % 2) * 128, (c % 2 + 1) * 128)
        for g in range(8):
            x = xi % 2
            xi += 1
            BT = bc4[s4][:, g, tk]
            CT = bc4[s4][:, 8 + g, tk]
            hs = slice(g * 4, (g + 1) * 4)
            cs = slice(g * 256, (g + 1) * 256)
            cbp = bank(1)[:, 0:128]
            S.op("pe", I("matmul", out=cbp, lhsT=BT, rhs=CT, start=True, stop=True),
                 r=[("bc4", s4)], w=[("bank", 1)])
            S.op("dve", I("tensor_tensor", out=cbm[x], in0=cbp, in1=U_f, op=ALU.mult),
                 r=[("bank", 1), "cst"], w=[("cbm", x)])
            S.op("pool", I("tensor_tensor", out=lseg[x], in0=Ls_f.unsqueeze(1).to_broadcast([128, 4, 128]),
                           in1=la[:, hs].unsqueeze(2).to_broadcast([128, 4, 128]), op=ALU.mult),
                 r=["cst", ("LAs", c)], w=[("lseg", x)])
            S.op("pe", [I("matmul", out=bank(2 + x)[:, e * 128:(e + 1) * 128], lhsT=lseg[x][:, e, :], rhs=U_f,
                          start=True, stop=True) for e in range(4)],
                 r=[("lseg", x), "cst"], w=[("bank", 2 + x)])
            S.op("act", I("activation", out=dec[x], in_=bank(2 + x).rearrange("p (e l) -> p e l", e=4),
                          func=AF.Exp),
                 r=[("bank", 2 + x)], w=[("dec", x)])
            S.op("dve", I("tensor_tensor", out=MT[x], in0=dec[x],
                          in1=cbm[x].unsqueeze(1).to_broadcast([128, 4, 128]), op=ALU.mult),
                 r=[("dec", x), ("cbm", x)], w=[("MT", x)])
            mm = [I("matmul", out=bank(4 + x)[:, e * 64:(e + 1) * 64], lhsT=MT[x][:, e, :],
                    rhs=xdt[s][:, g * 4 + e, :], start=True, stop=True)
                  for e in range(4)]
            mm.append(I("matmul", out=bank(4 + x)[:, 256:512], lhsT=CT, rhs=Hbf[:, cs], start=True, stop=True))
            S.op("pe", mm, r=[("MT", x), ("xdt", s), ("bc4", s4), ("Hbf", g)], w=[("bank", 4 + x)])
            S.op("dve", I("tensor_tensor", out=tt_[x],
                          in0=bank(4 + x)[:, 256:512].rearrange("p (e q) -> p e q", e=4),
                          in1=ex[s][:, hs].unsqueeze(2).to_broadcast([128, 4, 64]), op=ALU.mult),
                 r=[("bank", 4 + x), ("ex", s)], w=[("tt", x)])
            S.op("pool", I("tensor_tensor", out=tt_[x], in0=tt_[x], in1=dx[s][:, hs, :], op=ALU.add),
                 r=[("tt", x), ("dx", s)], w=[("tt", x)])
            S.op("dve", I("tensor_tensor", out=ybf[s][:, cs].rearrange("p (e q) -> p e q", e=4),
                          in0=bank(4 + x)[:, 0:256].rearrange("p (e q) -> p e q", e=4), in1=tt_[x], op=ALU.add),
                 r=[("bank", 4 + x), ("tt", x)], w=[("ybf", s, g)])
            php = bank(6 + x)[:, 0:256]
            S.op("pe", I("matmul", out=php, lhsT=b_t[s][:, g * 128:(g + 1) * 128],
                         rhs=xds[s][:, hs, :].rearrange("p h e -> p (h e)"), start=True, stop=True),
                 r=[("b_t", s), ("xds", s)], w=[("bank", 6 + x)])
            Hg = H[:, cs].rearrange("p (e q) -> p e q", e=4)
            S.op("pool", I("tensor_tensor", out=Hg, in0=Hg,
                           in1=ex[s][:, 64 + g * 4:64 + (g + 1) * 4].unsqueeze(2).to_broadcast([128, 4, 64]),
                           op=ALU.mult),
                 r=[("H", g), ("ex", s)], w=[("H", g)])
            S.op("dve", I("tensor_tensor", out=H[:, cs], in0=H[:, cs], in1=php, op=ALU.add),
                 r=[("H", g), ("bank", 6 + x)], w=[("H", g)])
            S.op("act", I("copy", out=Hbf[:, cs], in_=H[:, cs]), r=[("H", g)], w=[("Hbf", g)])
            yield
        S.dma(C["YS"][c * 128:(c + 1) * 128, :], ybf[s], r=[("ybf", s, g) for g in range(8)])


def phaseF(C):
    S, sb, T, l, NT = C["S"], C["sb"], C["T"], C["l"], C["NT"]
    bank, bank_bf, PS, ident_bf, rstd_ops = C["bank"], C["bank_bf"], C["PS"], C["ident_bf"], C["rstd_ops"]
    P = C["P"]
    x_src, x_dst = C["x_src"], C["x_dst"]
    sb.cur = C["persist_mark0"]
    wso = sb.alloc([128, 16, 1024], BF16)
    wao = sb.alloc([128, 8, 1024], BF16)
    wmo = sb.alloc([128, 8, 1024], BF16)
    wo = sb.alloc([128, 8, 1024], BF16)
    ssdn = sb.alloc([128, 2048], F32)
    npost = sb.alloc([128, 1024], F32)
    KmT = sb.alloc([128, 8, 256], BF16)
    Vm1 = sb.alloc([128, 2, 4, 257], BF16)
    fmark = sb.cur
    wtmp = [sb.alloc([128, 4, 1024], F32) for _ in range(2)]
    wi = 0
    for (dstw, src, nkc) in ((wso, P["w_ssd_out"][l], 16), (wao, P["w_attn_out"][l], 8),
                             (wmo, P["w_mem_out"][l], 8), (wo, P["w_out"][l], 8)):
        srcv = src.rearrange("(kc p) n -> p kc n", p=128)
        for k0 in range(0, nkc, 4):
            s = wi % 2
            wi += 1
            S.dma(wtmp[s], srcv[:, k0:k0 + 4, :], w=[("wtmp", s)])
            S.op("pool" if wi % 2 else "dve", I("tensor_copy", out=dstw[:, k0:k0 + 4, :], in_=wtmp[s]),
                 r=[("wtmp", s)], w=[("wres", wi)])
    S.dma(ssdn, P["ssd_norm"][l:l + 1, :].partition_broadcast(128), w=["ssdn"])
    S.dma(npost, P["norm_post"][l:l + 1, :].partition_broadcast(128), w=["npost"])
    mnorm = sb.alloc([128, 1024], F32)
    S.dma(mnorm, P["mem_norm"][l:l + 1, :].partition_broadcast(128), w=["mnorm"])
    memT = sb.alloc([128, 8, 256], BF16)
    mx = sb.alloc([128, 1024], F32)
    mxn = sb.alloc([128, 1024], BF16)
    mss = sb.alloc([128, 1], F32)
    mrs = sb.alloc([128, 1], F32)
    for mb in range(2):
        S.dma(mx, C["mem_in"][mb * 128:(mb + 1) * 128, :], w=["mx"])
        S.op("act", I("activation", out=mxn, in_=mx, func=AF.Square, accum_out=mss[:, 0:1]),
             r=["mx"], w=["mss", "mxn"])
        rstd_ops(mss, mrs, D, ["mss"], "mrs")
        S.op("dve", I("scalar_tensor_tensor", out=mxn, in0=mx, scalar=mrs[:, 0:1], in1=mnorm,
                      op0=ALU.mult, op1=ALU.mult), r=["mx", "mrs", "mnorm"], w=["mxn"])
        S.op("pe", [I("transpose", out=bank_bf(4)[:, k * 128:(k + 1) * 128], in_=mxn[:, k * 128:(k + 1) * 128],
                      identity=ident_bf) for k in range(8)], r=["mxn", "ident"], w=[("bank", 4)])
        S.op("act", I("copy", out=memT[:, :, mb * 128:(mb + 1) * 128],
                      in_=bank_bf(4).rearrange("p (k t) -> p k t", k=8)), r=[("bank", 4)], w=[("memT", mb)])
    wkv = sb.alloc([128, 8, 512], BF16)
    wkvf = sb.alloc([128, 8, 512], F32)
    S.op("pool", I("memset", ap=Vm1, constant=1.0), w=["Vm1"])
    for cb in range(4):
        S.dma(wkvf, P["w_mem_kv"][l][:, cb * 512:(cb + 1) * 512].rearrange("(kc p) n -> p kc n", p=128),
              w=["wkvf"])
        S.op("dve", I("tensor_copy", out=wkv, in_=wkvf), r=["wkvf"], w=["wkv"])
        if cb < 2:
            for sub in range(4):
                j = cb * 4 + sub
                S.op("pe", [I("matmul", out=bank(5)[:, 0:256], lhsT=wkv[:, kc, sub * 128:(sub + 1) * 128],
                              rhs=memT[:, kc, :], start=(kc == 0), stop=(kc == 7)) for kc in range(8)],
                     r=["wkv", ("memT", 0), ("memT", 1)], w=[("bank", 5)])
                S.op("act", I("copy", out=KmT[:, j, :], in_=bank(5)[:, 0:256]), r=[("bank", 5)], w=[("KmT", j)])
        else:
            for mb in range(2):
                S.op("pe", [I("matmul", out=bank(5), lhsT=memT[:, kc, mb * 128:(mb + 1) * 128],
                              rhs=wkv[:, kc, :], start=(kc == 0), stop=(kc == 7)) for kc in range(8)],
                     r=["wkv", ("memT", 0), ("memT", 1)], w=[("bank", 5)])
                h0 = (cb - 2) * 2
                S.op("act", I("copy", out=Vm1[:, mb, h0:h0 + 2, 0:256],
                              in_=bank(5).rearrange("p (h e) -> p h e", h=2)),
                     r=[("bank", 5), "Vm1"], w=[("Vm1w", cb, mb)])
    S.barrier()
    sb.cur = fmark
    ys = [sb.alloc([128, 2048], BF16) for _ in range(2)]
    z1 = [sb.alloc([128, 2048], BF16) for _ in range(2)]
    oa = sb.alloc([128, 3, 1040], F32)
    za = [sb.alloc([128, 1024], BF16) for _ in range(2)]
    qmt = [sb.alloc([128, 8, 128], BF16) for _ in range(2)]
    zm = [sb.alloc([128, 1024], BF16) for _ in range(2)]
    gt = [sb.alloc([128, 3072], BF16) for _ in range(2)]
    xr = [sb.alloc([128, 1024], F32) for _ in range(2)]
    t1 = sb.alloc([128, 2048], F32)
    abfA = sb.alloc([128, 1024], BF16)
    abfS = sb.alloc([128, 2048], BF16)
    abfM = sb.alloc([128, 1024], BF16)
    actTA = sb.alloc([128, 8, 128], BF16)
    actTS = sb.alloc([128, 16, 128], BF16)
    actTM = sb.alloc([128, 8, 128], BF16)
    merged = sb.alloc([128, 1024], F32)
    num = sb.alloc([128, 1040], F32)
    ob_ = sb.alloc([128, 1024], F32)
    tmp = sb.alloc([128, 1024], F32)
    PmT = [sb.alloc([128, 4, 128], BF16) for _ in range(2)]
    ssg = sb.alloc([128, 8], F32)
    rs8 = sb.alloc([128, 8], F32)
    rden = sb.alloc([128, 16], F32)
    rdm = sb.alloc([128, 4], F32)
    ssf = sb.alloc([128, 1], F32)
    rsf = sb.alloc([128, 1], F32)
    om = t1[:, 0:1024]
    xo = t1[:, 1024:2048]

    def loads(i):
        s = i % 2
        tk = slice(i * 128, (i + 1) * 128)
        S.dma(ys[s], C["YS"][tk, :], w=[("ys", s)])
        S.dma(z1[s], C["Z1"][tk, :], w=[("z1", s)])
        S.dma(za[s], C["ZA"][tk, :], w=[("za", s)])
        S.dma(qmt[s], C["QMT"][:, tk].rearrange("(j p) t -> p j t", p=128), w=[("qmt", s)])
        S.dma(zm[s], C["ZM"][tk, :], w=[("zm", s)])
        S.dma(gt[s], C["G"][tk, :], w=[("gt", s)])
        S.dma(xr[s], x_src[tk, :], w=[("xr", s)])

    def transposes(src, skey, dstT, dkey, nk, banks):
        for b0 in range(0, nk, 8):
            bk = banks[b0 // 8]
            S.op("pe", [I("transpose", out=bank_bf(bk)[:, k * 128:(k + 1) * 128],
                          in_=src[:, (b0 + k) * 128:(b0 + k + 1) * 128], identity=ident_bf) for k in range(8)],
                 r=[skey, "ident"], w=[("bank", bk)])
            S.op("act", I("copy", out=dstT[:, b0:b0 + 8, :], in_=bank_bf(bk).rearrange("p (k t) -> p k t", k=8)),
                 r=[("bank", bk)], w=[(dkey, b0 // 8)])

    def outproj(wt, srcT, skey, nk, pso, okey):
        for hh in range(2):
            S.op("pe", [I("matmul", out=pso[:, hh * 512:(hh + 1) * 512], lhsT=srcT[:, kc, :],
                          rhs=wt[:, kc, hh * 512:(hh + 1) * 512], start=(kc == 0), stop=(kc == nk - 1))
                        for kc in range(nk)],
                 r=[(skey, j) for j in range((nk + 7) // 8)], w=[("bank", okey + hh)])

    loads(0)
    for i in range(NT):
        s = i % 2
        tk = slice(i * 128, (i + 1) * 128)
        for g in range(3):
            S.dma(oa[:, g, :], C["OA"][g][tk, :], w=[("oa", g)])
        if i + 1 < NT:
            loads(i + 1)
        S.op("pool", I("tensor_tensor", out=num, in0=oa[:, 0, :], in1=oa[:, 1, :], op=ALU.add),
             r=[("oa", 0), ("oa", 1)], w=["num"])
        S.op("pool", I("tensor_tensor", out=num, in0=num, in1=oa[:, 2, :], op=ALU.add),
             r=[("oa", 2), "num"], w=["num"])
        numv = num.rearrange("p (h e) -> p h e", e=65)
        S.op("dve", I("tensor_tensor", out=t1, in0=ys[s], in1=z1[s], op=ALU.mult),
             r=[("ys", s), ("z1", s)], w=["t1a", "t1b"])
        S.op("act", [I("activation", out=abfS[:, 0:256], in_=t1[:, g * 256:(g + 1) * 256], func=AF.Square,
                       accum_out=ssg[:, g:g + 1]) for g in range(8)], r=["t1a", "t1b"], w=["ssg", "abfS"])
        S.op("dve", I("reciprocal", out=rden, in_=numv[:, :, 64]), r=["num"], w=["rden"])
        rstd_ops(ssg, rs8, 256, ["ssg"], "rs8")
        S.op("dve", I("tensor_tensor", out=ob_.rearrange("p (h e) -> p h e", e=64), in0=numv[:, :, 0:64],
                      in1=rden.unsqueeze(2).to_broadcast([128, 16, 64]), op=ALU.mult),
             r=["num", "rden"], w=["ob"])
        S.op("pool", I("tensor_tensor", out=abfA, in0=ob_, in1=za[s], op=ALU.mult),
             r=["ob", ("za", s)], w=["abfA"])
        transposes(abfA, "abfA", actTA, "actTA", 8, [4])
        for hp in range(2):
            mm = []
            for hh in range(2):
                h = hp * 2 + hh
                for mb in range(2):
                    for ec in range(2):
                        mm.append(I("matmul", out=bank(5)[:, (hh * 2 + mb) * 128:(hh * 2 + mb + 1) * 128],
                                    lhsT=KmT[:, h * 2 + ec, mb * 128:(mb + 1) * 128], rhs=qmt[s][:, h * 2 + ec, :],
                                    start=(ec == 0), stop=(ec == 1)))
            S.op("pe", mm, r=[("qmt", s)], w=[("bank", 5)])
            S.op("act", I("activation", out=PmT[hp], in_=bank(5).rearrange("p (j t) -> p j t", j=4),
                          func=AF.Exp, scale=1.0 / 16.0), r=[("bank", 5)], w=[("PmT", hp)])
        outproj(wao, actTA, "actTA", 8, PS[3], 6)
        S.op("dve", I("tensor_tensor", out=t1.rearrange("p (g e) -> p g e", g=8),
                      in0=t1.rearrange("p (g e) -> p g e", g=8),
                      in1=rs8.unsqueeze(2).to_broadcast([128, 8, 256]), op=ALU.mult),
             r=["t1a", "t1b", "rs8"], w=["t1a", "t1b"])
        S.op("pool", I("tensor_tensor", out=abfS, in0=t1, in1=ssdn, op=ALU.mult),
             r=["t1a", "t1b", "ssdn"], w=["abfS"])
        S.op("dve", I("tensor_tensor", out=merged, in0=PS[3], in1=gt[s][:, 1024:2048], op=ALU.mult),
             r=[("bank", 6), ("bank", 7), ("gt", s)], w=["merged"])
        transposes(abfS, "abfS", actTS, "actTS", 16, [0, 1])
        outproj(wso, actTS, "actTS", 16, PS[1], 2)
        S.op("dve", I("tensor_tensor", out=tmp, in0=PS[1], in1=gt[s][:, 0:1024], op=ALU.mult),
             r=[("bank", 2), ("bank", 3), ("gt", s)], w=["tmp"])
        S.op("pool", I("tensor_tensor", out=merged, in0=merged, in1=tmp, op=ALU.add),
             r=["tmp", "merged"], w=["merged"])
        for hp in range(2):
            for hh in range(2):
                h = hp * 2 + hh
                S.op("pe", [I("matmul", out=bank(hh)[:, 0:257], lhsT=PmT[hp][:, hh * 2 + mb, :],
                              rhs=Vm1[:, mb, h, :], start=(mb == 0), stop=(mb == 1)) for mb in range(2)],
                     r=[("PmT", hp)], w=[("bank", hh)])
                S.op("dve", I("reciprocal", out=rdm[:, h:h + 1], in_=bank(hh)[:, 256:257]),
                     r=[("bank", hh)], w=[("rdm", h)])
                S.op("dve", I("tensor_scalar", out=om[:, h * 256:(h + 1) * 256], in0=bank(hh)[:, 0:256],
                              scalar1=rdm[:, h:h + 1], scalar2=None, op0=ALU.mult),
                     r=[("bank", hh), ("rdm", h)], w=["t1a"])
        S.op("pool", I("tensor_tensor", out=abfM, in0=om, in1=zm[s], op=ALU.mult),
             r=["t1a", ("zm", s)], w=["abfM"])
        transposes(abfM, "abfM", actTM, "actTM", 8, [4])
        outproj(wmo, actTM, "actTM", 8, PS[1], 2)
        S.op("dve", I("tensor_tensor", out=tmp, in0=PS[1], in1=gt[s][:, 2048:3072], op=ALU.mult),
             r=[("bank", 2), ("bank", 3), ("gt", s)], w=["tmp"])
        S.op("pool", I("tensor_tensor", out=merged, in0=merged, in1=tmp, op=ALU.add),
             r=["tmp", "merged"], w=["merged"])
        S.op("act", I("copy", out=abfA, in_=merged), r=["merged"], w=["abfA"])
        transposes(abfA, "abfA", actTA, "actTA", 8, [4])
        outproj(wo, actTA, "actTA", 8, PS[3], 6)
        S.op("act", I("activation", out=abfA, in_=PS[3], func=AF.Square, accum_out=ssf[:, 0:1]),
             r=[("bank", 6), ("bank", 7)], w=["ssf", "abfA"])
        rstd_ops(ssf, rsf, D, ["ssf"], "rsf")
        S.op("dve", I("scalar_tensor_tensor", out=xo, in0=PS[3], scalar=rsf[:, 0:1], in1=npost,
                      op0=ALU.mult, op1=ALU.mult),
             r=[("bank", 6), ("bank", 7), "rsf", "npost"], w=["t1b"])
        S.op("pool", I("tensor_tensor", out=xo, in0=xo, in1=xr[s], op=ALU.add), r=["t1b", ("xr", s)], w=["t1b"])
        S.dma(x_dst[tk, :], xo, r=["t1b"])


WNAMES = ["norm_pre", "norm_post", "w_in", "conv_w", "conv_b", "dt_bias", "a_log", "d_skip", "ssd_norm",
          "w_ssd_out", "w_attn_out", "mem_norm", "w_mem_kv", "w_mem_out", "w_out"]
_PROG = {}
FUSED = True


def _get_prog(T, NL):
    key = (T, NL)
    if key not in _PROG:
        _PROG[key] = build_program(T, NL)
    return _PROG[key]


def prep_weights(inputs):
    w = {k: np.ascontiguousarray(np.asarray(inputs[k], dtype=np.float32)) for k in WNAMES}
    nl = w["conv_w"].shape[0]
    cw = w["conv_w"].reshape(nl, 4, 32, 128).transpose(0, 3, 2, 1)
    w["conv_w"] = np.ascontiguousarray(cw.reshape(nl, 128, 128))
    cb = w["conv_b"].reshape(nl, 32, 128).transpose(0, 2, 1)
    w["conv_b"] = np.ascontiguousarray(cb)
    return w


def kernel(**inputs):
    x = np.ascontiguousarray(np.asarray(inputs["x"], dtype=np.float32))
    mem = np.ascontiguousarray(np.asarray(inputs["mem"], dtype=np.float32))
    B, T, _ = x.shape
    consts = make_consts()
    w = prep_weights(inputs)
    depth = w["w_in"].shape[0]
    if FUSED:
        nc = _get_prog(T, depth)
        in_maps = []
        for b in range(B):
            m = {"x": x[b], "mem": mem[b], "consts": consts}
            m.update(w)
            in_maps.append(m)
        res = run_bass_kernel_spmd(nc, in_maps, core_ids=list(range(B)))
        return np.stack([np.asarray(r["out"]) for r in res.results], axis=0).astype(np.float32)
    cur = [x[b] for b in range(B)]
    nc = _get_prog(T, 1)
    for l in range(depth):
        in_maps = []
        for b in range(B):
            m = {"x": cur[b], "mem": mem[b], "consts": consts}
            m.update({k: w[k][l:l + 1] for k in WNAMES})
            in_maps.append(m)
        res = run_bass_kernel_spmd(nc, in_maps, core_ids=list(range(B)))
        cur = [np.ascontiguousarray(np.asarray(r["out"], dtype=np.float32)) for r in res.results]
    return np.stack(cur, axis=0).astype(np.float32)
```

```python
import numpy as np
import concourse.bass as bass
import concourse.mybir as mybir
from concourse.bass_utils import run_bass_kernel_spmd

F32 = mybir.dt.float32
BF16 = mybir.dt.bfloat16
AF = mybir.ActivationFunctionType
ALU = mybir.AluOpType

D = 1024
DEPTH = 2
EPS = 1e-6
D_INNER = 2048
NH = 32
NG = 8
DS = 128
MEM_LEN = 256
OFF_ZSSD = 0
OFF_XBC = 2048
OFF_DT = 6144
OFF_QKV = 6176
OFF_ZATT = OFF_QKV + 9216
OFF_QMEM = OFF_ZATT + 1024
OFF_ZMEM = OFF_QMEM + 1024
OFF_GATE = OFF_ZMEM + 1024
N_IN = OFF_GATE + 3072
DIL = (1, 4, 16)
N_DMA_SEMS = 24


class Sched:
    ENGS = ("pe", "act", "dve", "pool", "sp")

    def __init__(self):
        self.ops = []

    def op(self, eng, instrs, r=(), w=()):
        if isinstance(instrs, tuple):
            instrs = [instrs]
        instrs = list(instrs)

        def fn(e, instrs=instrs):
            ins = None
            for m, kw in instrs:
                ins = getattr(e, m)(**kw)
            return ins
        r, w = list(r), list(w)
        for k in r:
            if isinstance(k, tuple) and k and k[0] == "bank" and k not in w:
                w.append(k)
        self.ops.append(dict(eng=eng, fn=fn, r=tuple(r), w=tuple(w), dma=False))

    def dma(self, out, in_, r=(), w=(), eng="sp"):
        def fn(e, out=out, in_=in_):
            return e.dma_start(out=out, in_=in_)
        self.ops.append(dict(eng=eng, fn=fn, r=tuple(r), w=tuple(w), dma=True))

    def barrier(self):
        self.ops.append(dict(barrier=True))

    def emit(self, nc):
        ops = self.ops
        n = len(ops)
        last_w = {}
        readers = {}
        deps = [None] * n
        dma_sem_of = [None] * n
        dma_rr = 0
        dma_last_use = [None] * N_DMA_SEMS
        bar_pending = {}
        last_op_of = {}
        outstanding = set()
        for i, o in enumerate(ops):
            if o.get("barrier"):
                allprev = set(outstanding)
                for e in self.ENGS:
                    bar_pending[e] = bar_pending.get(e, set()) | allprev
                outstanding = set()
                last_w.clear()
                readers.clear()
                continue
            d = set()
            for k in o["r"]:
                if k in last_w:
                    d.add(last_w[k])
            for k in o["w"]:
                if k in last_w:
                    d.add(last_w[k])
                for rr in readers.get(k, ()):
                    d.add(rr)
            if o["eng"] in bar_pending and bar_pending[o["eng"]]:
                d |= bar_pending[o["eng"]]
                bar_pending[o["eng"]] = set()
            if o["dma"]:
                j = dma_rr % N_DMA_SEMS
                dma_rr += 1
                dma_sem_of[i] = j
                if dma_last_use[j] is not None:
                    d.add(dma_last_use[j])
                dma_last_use[j] = i
            d.discard(i)
            deps[i] = d
            for k in o["r"]:
                readers.setdefault(k, []).append(i)
            for k in o["w"]:
                last_w[k] = i
                readers[k] = []
            if o["dma"]:
                outstanding.add(i)
            else:
                prev = last_op_of.get(o["eng"])
                if prev is not None:
                    outstanding.discard(prev)
                last_op_of[o["eng"]] = i
                outstanding.add(i)
        self.final_wait = set(outstanding) | bar_pending.get("sp", set())
        needed = [False] * n
        for i, o in enumerate(ops):
            if o.get("barrier"):
                continue
            for dd in deps[i]:
                if ops[dd]["eng"] == "pe" and o["eng"] == "pe" and not ops[dd]["dma"]:
                    continue
                needed[dd] = True
        for dd in self.final_wait:
            needed[dd] = True
        cnt = {e: 0 for e in self.ENGS}
        dcnt = [0] * N_DMA_SEMS
        event = [None] * n
        for i, o in enumerate(ops):
            if o.get("barrier"):
                continue
            if o["dma"]:
                j = dma_sem_of[i]
                dcnt[j] += 16
                event[i] = (("d", j), dcnt[j])
            elif needed[i]:
                cnt[o["eng"]] += 1
                event[i] = (("e", o["eng"]), cnt[o["eng"]])
        self.stats = (dict(cnt), max(dcnt), n)
        import contextlib
        with contextlib.ExitStack() as st:
            esem = {e: st.enter_context(nc.semaphore("s_" + e)) for e in self.ENGS}
            dsem = [st.enter_context(nc.semaphore("d_%d" % j)) for j in range(N_DMA_SEMS)]
            block = st.enter_context(nc.Block())

            def sem_of(key):
                return esem[key[1]] if key[0] == "e" else dsem[key[1]]

            def run(engname, eng):
                known = {}
                for i, o in enumerate(ops):
                    if o.get("barrier") or o["eng"] != engname:
                        continue
                    waits = {}
                    for dd in deps[i]:
                        if ops[dd]["eng"] == "pe" and engname == "pe" and not ops[dd]["dma"]:
                            continue
                        key, val = event[dd]
                        if known.get(key, 0) >= val:
                            continue
                        waits[key] = max(waits.get(key, 0), val)
                    for key, val in waits.items():
                        eng.wait_ge(sem_of(key), val)
                        known[key] = val
                    ins = o["fn"](eng)
                    if event[i] is not None:
                        key, val = event[i]
                        ins.then_inc(sem_of(key), 16 if key[0] == "d" else 1)
                if engname == "sp":
                    waits = {}
                    for dd in self.final_wait:
                        key, val = event[dd]
                        if known.get(key, 0) >= val:
                            continue
                        waits[key] = max(waits.get(key, 0), val)
                    for key, val in waits.items():
                        eng.wait_ge(sem_of(key), val)

            @block.tensor
            def _(e):
                run("pe", e)

            @block.scalar
            def _(e):
                run("act", e)

            @block.vector
            def _(e):
                run("dve", e)

            @block.gpsimd
            def _(e):
                run("pool", e)

            @block.sync
            def _(e):
                run("sp", e)


SB_BASE = 16512
SB_LIMIT = 229376 - 256


class SBAlloc:
    def __init__(self, nc):
        self.nc = nc
        self.cur = SB_BASE
        self.n = 0

    def alloc(self, shape, dt):
        nb = 1
        for s in shape[1:]:
            nb *= s
        nb *= 4 if dt == F32 else 2
        off = self.cur
        self.cur += (nb + 63) // 64 * 64
        assert self.cur <= SB_LIMIT, ("SBUF overflow", self.cur)
        self.n += 1
        return self.nc.alloc_sbuf_tensor_at("sb%d" % self.n, list(shape), dt, offset=off).ap()


def I(m, **kw):
    return (m, kw)


def make_consts():
    p = np.arange(128)[:, None]
    j = np.arange(128)[None, :]
    ident = (p == j)
    U = (p <= j)
    Ls = (p > j)
    ones = np.ones((128, 128), bool)
    Ge = (p >= j)
    return np.concatenate([ident, U, Ls, ones, Ge], axis=1).astype(np.float32)


def build_program(T, NL, dbg=()):
    nc = bass.Bass("TRN2", target_bir_lowering=False)
    S = Sched()
    NT = T // 128
    HALF = min(T, 4096)
    NHALF = T // HALF
    NTT = HALF // 512

    def dram(name, shape, dt, kind="Internal"):
        if name in dbg:
            kind = "ExternalOutput"
        return nc.dram_tensor(name, list(shape), dt, kind=kind).ap()

    x_in = dram("x", [T, D], F32, "ExternalInput")
    mem_in = dram("mem", [MEM_LEN, D], F32, "ExternalInput")
    consts_in = dram("consts", [128, 640], F32, "ExternalInput")
    P = {}
    for name, shp in [("norm_pre", [NL, D]), ("norm_post", [NL, D]), ("w_in", [NL, D, N_IN]),
                      ("conv_w", [NL, 128, 128]), ("conv_b", [NL, 128, 32]), ("dt_bias", [NL, NH]),
                      ("a_log", [NL, NH]), ("d_skip", [NL, NH]), ("ssd_norm", [NL, D_INNER]),
                      ("w_ssd_out", [NL, D_INNER, D]), ("w_attn_out", [NL, D, D]),
                      ("mem_norm", [NL, D]), ("w_mem_kv", [NL, D, 2 * D]),
                      ("w_mem_out", [NL, D, D]), ("w_out", [NL, D, D])]:
        P[name] = dram(name, shp, F32, "ExternalInput")
    out = dram("out", [T, D], F32, "ExternalOutput")
    xmid = [dram("xmid%d" % i, [T, D], F32) for i in range(NL - 1)]
    XS = dram("XS", [T, 2048], BF16)
    BTOK = dram("BTOK", [T, 1024], BF16)
    BCT = dram("BCT", [2048, T], BF16)
    Z1 = dram("Z1", [T, 2048], BF16)
    QT = [dram("QT%d" % g, [1024, T], BF16) for g in range(3)]
    KT = [dram("KT%d" % g, [1024, T], BF16) for g in range(3)]
    V = [dram("V%d" % g, [T, 1024], BF16) for g in range(3)]
    ZA = dram("ZA", [T, 1024], BF16)
    QMT = dram("QMT", [1024, T], BF16)
    ZM = dram("ZM", [T, 1024], BF16)
    G = dram("G", [T, 3072], BF16)
    YS = dram("YS", [T, 2048], BF16)
    OA = [dram("OA%d" % g, [T, 1040], F32) for g in range(3)]

    sb = SBAlloc(nc)
    PS = [nc.alloc_psum_tensor("ps%d" % i, [128, 1024], F32).ap() for i in range(4)]

    def bank(i):
        return PS[i // 2][:, (i % 2) * 512:(i % 2) * 512 + 512]

    def bank_bf(i):
        return bank(i).bitcast(BF16)

    cst = sb.alloc([128, 640], F32)
    ident_bf = sb.alloc([128, 128], BF16)
    mask2 = sb.alloc([128, 2, 128], BF16)
    ones_bf = sb.alloc([128, 128], BF16)
    persist_mark0 = sb.cur
    DTs = sb.alloc([128, NT, 32], F32)
    LAs = sb.alloc([128, NT, 32], F32)
    halo = sb.alloc([128, 32, 3], F32)
    ident_f = cst[:, 0:128]
    U_f = cst[:, 128:256]
    Ls_f = cst[:, 256:384]
    ones_f = cst[:, 384:512]
    Ge_f = cst[:, 512:640]
    S.dma(cst, consts_in, w=["cst"])
    S.op("dve", I("tensor_copy", out=ident_bf, in_=ident_f), r=["cst"], w=["ident"])
    S.op("dve", I("tensor_copy", out=mask2[:, 0, :], in_=Ge_f), r=["cst"], w=["mask2a"])
    S.op("dve", I("tensor_copy", out=mask2[:, 1, :], in_=U_f), r=["cst"], w=["mask2b"])
    S.op("dve", I("tensor_copy", out=ones_bf, in_=ones_f), r=["cst"], w=["onesbf"])
    persist_mark = sb.cur

    def bcast_load(dst, src_row, key):
        S.dma(dst, src_row.partition_broadcast(128), w=[key])

    def rstd_ops(ss, rstd, n, rkeys, wkey):
        S.op("dve", I("tensor_scalar", out=rstd, in0=ss, scalar1=1.0 / n, scalar2=EPS,
                      op0=ALU.mult, op1=ALU.add), r=rkeys, w=[wkey])
        S.op("act", I("activation", out=rstd, in_=rstd, func=AF.Ln), r=[wkey], w=[wkey])
        S.op("act", I("activation", out=rstd, in_=rstd, func=AF.Exp, scale=-0.5), r=[wkey], w=[wkey])

    C = dict(locals())
    for l in range(NL):
        x_src = x_in if l == 0 else xmid[l - 1]
        x_dst = out if l == NL - 1 else xmid[l]
        S.barrier()
        sb.cur = persist_mark
        gpre = sb.alloc([128, D], F32)
        convw = sb.alloc([128, 32, 4], F32)
        convb = sb.alloc([128, 32], F32)
        dtb = sb.alloc([128, 32], F32)
        abc = sb.alloc([128, 32], F32)
        bcast_load(gpre, P["norm_pre"][l:l + 1, :], "gpre")
        S.dma(convw, P["conv_w"][l].rearrange("p (b k) -> p b k", k=4), w=["convw"])
        S.dma(convb, P["conv_b"][l], w=["convb"])
        bcast_load(dtb, P["dt_bias"][l:l + 1, :], "dtb")
        bcast_load(abc, P["a_log"][l:l + 1, :], "abc0")
        S.op("act", I("activation", out=abc, in_=abc, func=AF.Exp), r=["abc0"], w=["abc0"])
        S.op("act", I("mul", out=abc, in_=abc, mul=-1.0), r=["abc0"], w=["abc"])
        S.op("pool", I("memset", ap=halo, constant=0.0), w=["halo"])
        projmark = sb.cur
        for hf in range(NHALF):
            sb.cur = projmark
            t0h = hf * HALF
            hT = sb.alloc([128, 8, HALF], BF16)
            nmark = sb.cur
            xin = [sb.alloc([128, D], F32) for _ in range(2)]
            xn = [sb.alloc([128, D], BF16) for _ in range(2)]
            junk = sb.alloc([128, D], BF16)
            ss = [sb.alloc([128, 1], F32) for _ in range(2)]
            rs = [sb.alloc([128, 1], F32) for _ in range(2)]
            for i in range(HALF // 128):
                s_ = i % 2
                tok = t0h + i * 128
                S.dma(xin[s_], x_src[tok:tok + 128, :], w=[("xin", s_)])
                S.op("act", I("activation", out=junk, in_=xin[s_], func=AF.Square,
                              accum_out=ss[s_][:, 0:1]),
                     r=[("xin", s_)], w=[("ss", s_), "junk"])
                rstd_ops(ss[s_], rs[s_], D, [("ss", s_)], ("rs", s_))
                S.op("dve", I("scalar_tensor_tensor", out=xn[s_], in0=xin[s_], scalar=rs[s_][:, 0:1],
                              in1=gpre, op0=ALU.mult, op1=ALU.mult),
                     r=[("xin", s_), ("rs", s_), "gpre"], w=[("xn", s_)])
                pb = 6 + s_
                S.op("pe", [I("transpose", out=bank_bf(pb)[:, k * 128:(k + 1) * 128],
                              in_=xn[s_][:, k * 128:(k + 1) * 128], identity=ident_bf)
                            for k in range(8)],
                     r=[("xn", s_), "ident"], w=[("bank", pb)])
                S.op("act", I("copy", out=hT[:, :, i * 128:(i + 1) * 128],
                              in_=bank_bf(pb).rearrange("p (k t) -> p k t", k=8)),
                     r=[("bank", pb)], w=[("hT", i // 4)])
            S.barrier()
            sb.cur = nmark
            C.update(locals())
            phaseP(C)
        S.barrier()
        sb.cur = persist_mark
        C.update(locals())
        for _ in phaseA(C):
            pass
        S.barrier()
        sb.cur = persist_mark
        for _ in phaseS(C):
            pass
        S.barrier()
        sb.cur = persist_mark
        phaseF(C)
    S.emit(nc)
    return nc


def phaseP(C):
    S, sb, l, hT = C["S"], C["sb"], C["l"], C["hT"]
    NTT, t0h, hf, NHALF, HALF = C["NTT"], C["t0h"], C["hf"], C["NHALF"], C["HALF"]
    bank, bank_bf, ident_bf = C["bank"], C["bank_bf"], C["ident_bf"]
    convw, convb, halo, dtb, abc = C["convw"], C["convb"], C["halo"], C["dtb"], C["abc"]
    DTs, LAs = C["DTs"], C["LAs"]
    W = C["P"]["w_in"][l]
    wst = [sb.alloc([128, 8, 512], F32) for _ in range(2)]
    wbf = [sb.alloc([128, 8, 512], BF16) for _ in range(2)]
    stage = [sb.alloc([128, 4, 512], BF16) for _ in range(2)]
    U = [[sb.alloc([128, 515], F32) for _ in range(2)] for _ in range(4)]
    acc = [sb.alloc([128, 512], F32) for _ in range(4)]
    xc = [sb.alloc([128, 4, 512], BF16) for _ in range(2)]
    stT = [sb.alloc([128, 4, 512], BF16) for _ in range(2)]
    wdt = sb.alloc([128, 8, 32], F32)
    wdtb = sb.alloc([128, 8, 32], BF16)
    dtmp = sb.alloc([128, 16, 32], F32)

    blocks = []
    for j in range(4):
        blocks.append((OFF_ZSSD + j * 512, "tm", C["Z1"], j * 512, AF.Silu))
    for j in range(8):
        blocks.append((OFF_XBC + j * 512, "xbc", None, j * 4, None))
    for g in range(3):
        for j in range(2):
            blocks.append((OFF_QKV + (0 * 3 + g) * 1024 + j * 512, "fm", C["QT"][g], j * 512, None))
        for j in range(2):
            blocks.append((OFF_QKV + (1 * 3 + g) * 1024 + j * 512, "fm", C["KT"][g], j * 512, None))
        for j in range(2):
            blocks.append((OFF_QKV + (2 * 3 + g) * 1024 + j * 512, "tm", C["V"][g], j * 512, AF.Copy))
    for j in range(2):
        blocks.append((OFF_ZATT + j * 512, "tm", C["ZA"], j * 512, AF.Silu))
    for j in range(2):
        blocks.append((OFF_QMEM + j * 512, "fm", C["QMT"], j * 512, None))
    for j in range(2):
        blocks.append((OFF_ZMEM + j * 512, "tm", C["ZM"], j * 512, AF.Silu))
    for j in range(6):
        blocks.append((OFF_GATE + j * 512, "tm", C["G"], j * 512, AF.Sigmoid))

    S.dma(wdt, W[:, OFF_DT:OFF_DT + 32].rearrange("(kc p) n -> p kc n", p=128), w=["wdt"])
    S.op("dve", I("tensor_copy", out=wdtb, in_=wdt), r=["wdt"], w=["wdtb"])
    ntile = HALF // 128
    grp = min(16, ntile)
    for g0 in range(0, ntile, grp):
        bk = 0
        for i in range(g0, g0 + grp):
            S.op("pe", [I("matmul", out=bank(bk)[:, (i - g0) * 32:(i - g0) * 32 + 32],
                          lhsT=hT[:, kc, i * 128:(i + 1) * 128], rhs=wdtb[:, kc, :],
                          start=(kc == 0), stop=(kc == 7)) for kc in range(8)],
                 r=[("hT", i // 4), "wdtb"], w=[("bank", bk)])
        c0 = t0h // 128 + g0
        S.op("dve", I("tensor_tensor", out=dtmp[:, 0:grp, :],
                      in0=bank(bk)[:, 0:grp * 32].rearrange("p (n e) -> p n e", e=32),
                      in1=dtb.unsqueeze(1).to_broadcast([128, grp, 32]), op=ALU.add),
             r=[("bank", bk), "dtb"], w=["dtmp"])
        S.op("act", I("activation", out=dtmp[:, 0:grp, :], in_=dtmp[:, 0:grp, :], func=AF.Exp),
             r=["dtmp"], w=["dtmp"])
        S.op("act", I("activation", out=DTs[:, c0:c0 + grp, :], in_=dtmp[:, 0:grp, :], func=AF.Ln,
                      bias=1.0, scale=1.0),
             r=["dtmp"], w=[("DTs", c0)])
        S.op("dve", I("tensor_tensor", out=LAs[:, c0:c0 + grp, :], in0=DTs[:, c0:c0 + grp, :],
                      in1=abc.unsqueeze(1).to_broadcast([128, grp, 32]), op=ALU.mult),
             r=[("DTs", c0), "abc"], w=[("LAs", c0)])

    def load_w(bi):
        c0 = blocks[bi][0]
        sl = bi % 2
        S.dma(wst[sl], W[:, c0:c0 + 512].rearrange("(kc p) n -> p kc n", p=128), w=[("wst", sl)])
        S.op("pool", I("tensor_copy", out=wbf[sl], in_=wst[sl]), r=[("wst", sl)], w=[("wbf", sl)])

    load_w(0)
    bkrr = [1]
    cnt = [0]
    for bi, (c0, kind, dst, d0, func) in enumerate(blocks):
        if bi + 1 < len(blocks):
            load_w(bi + 1)
        sl = bi % 2
        for tt in range(NTT):
            tok0 = t0h + tt * 512
            st = cnt[0] % 2
            cnt[0] += 1
            for sub in range(4):
                bk = bkrr[0]
                bkrr[0] = bkrr[0] % 5 + 1
                if kind == "tm":
                    mm = [I("matmul", out=bank(bk), lhsT=hT[:, kc, tt * 512 + sub * 128:tt * 512 + sub * 128 + 128],
                            rhs=wbf[sl][:, kc, :], start=(kc == 0), stop=(kc == 7)) for kc in range(8)]
                else:
                    mm = [I("matmul", out=bank(bk), lhsT=wbf[sl][:, kc, sub * 128:(sub + 1) * 128],
                            rhs=hT[:, kc, tt * 512:(tt + 1) * 512], start=(kc == 0), stop=(kc == 7))
                          for kc in range(8)]
                S.op("pe", mm, r=[("wbf", sl), ("hT", tt)], w=[("bank", bk)])
                if kind == "tm":
                    S.op("act", I("activation", out=stage[st][:, sub, :], in_=bank(bk), func=func),
                         r=[("bank", bk)], w=[("stage", st, sub)])
                elif kind == "fm":
                    eng = "act" if sub % 2 == 0 else "dve"
                    S.op(eng, I("copy" if eng == "act" else "tensor_copy", out=stage[st][:, sub, :], in_=bank(bk)),
                         r=[("bank", bk)], w=[("stage", st, sub)])
                else:
                    chb = d0 + sub
                    par = tt % 2
                    Uc, Up = U[sub][par], U[sub][1 - par]
                    S.op("act", I("copy", out=Uc[:, 3:515], in_=bank(bk)),
                         r=[("bank", bk)], w=[("U", sub, par, "m")])
                    S.op("act", I("activation", out=acc[sub], in_=bank(bk), func=AF.Identity,
                                  scale=convw[:, chb, 3:4], bias=convb[:, chb:chb + 1]),
                         r=[("bank", bk), "convw", "convb"], w=[("acc", sub)])
                    if tt == 0:
                        S.op("pool", I("tensor_copy", out=Uc[:, 0:3], in_=halo[:, chb, :]),
                             r=[("halo", chb)], w=[("U", sub, par, "h")])
                    else:
                        S.op("pool", I("tensor_copy", out=Uc[:, 0:3], in_=Up[:, 512:515]),
                             r=[("U", sub, 1 - par, "m")], w=[("U", sub, par, "h")])
                    if tt == NTT - 1 and hf + 1 < NHALF:
                        S.op("pool", I("tensor_copy", out=halo[:, chb, :], in_=Uc[:, 512:515]),
                             r=[("U", sub, par, "m")], w=[("halo", chb)])
            if kind == "xbc":
                par = tt % 2
                for kk in (2, 1, 0):
                    for sub in range(4):
                        chb = d0 + sub
                        Uc = U[sub][par]
                        ukeys = [("U", sub, par, "m"), ("U", sub, par, "h"), "convw", "convb"]
                        if kk == 3:
                            S.op("dve", I("tensor_scalar", out=acc[sub], in0=Uc[:, 3:515], scalar1=convw[:, chb, 3:4],
                                          scalar2=convb[:, chb:chb + 1], op0=ALU.mult, op1=ALU.add),
                                 r=ukeys, w=[("acc", sub)])
                        else:
                            S.op("dve", I("scalar_tensor_tensor", out=acc[sub], in0=Uc[:, kk:kk + 512],
                                          scalar=convw[:, chb, kk:kk + 1], in1=acc[sub], op0=ALU.mult, op1=ALU.add),
                                 r=ukeys + [("acc", sub)], w=[("acc", sub)])
                for sub in range(4):
                    S.op("act", I("activation", out=xc[st][:, sub, :], in_=acc[sub], func=AF.Silu),
                         r=[("acc", sub)], w=[("xc", st, sub)])
                if d0 < 24:
                    for sub in range(4):
                        tb = 6 + (sub % 2)
                        S.op("pe", [I("transpose", out=bank_bf(tb)[:, j * 128:(j + 1) * 128],
                                      in_=xc[st][:, sub, j * 128:(j + 1) * 128], identity=ident_bf)
                                    for j in range(4)],
                             r=[("xc", st, sub), "ident"], w=[("bank", tb)])
                        eng = "act" if sub % 2 == 0 else "pool_no"
                        eng = "act" if sub % 2 == 0 else "dve"
                        S.op(eng, I("copy" if eng == "act" else "tensor_copy",
                                    out=stT[st][:, :, sub * 128:(sub + 1) * 128],
                                    in_=bank_bf(tb)[:, 0:512].rearrange("p (j c) -> p j c", j=4)),
                             r=[("bank", tb)], w=[("stT", st, sub)])
            if kind == "tm":
                S.dma(dst[tok0:tok0 + 512, d0:d0 + 512].rearrange("(s p) c -> p s c", p=128), stage[st],
                      r=[("stage", st, s_) for s_ in range(4)])
            elif kind == "fm":
                S.dma(dst[d0:d0 + 512, tok0:tok0 + 512].rearrange("(s p) t -> p s t", p=128), stage[st],
                      r=[("stage", st, s_) for s_ in range(4)])
            else:
                chb0 = d0
                if chb0 < 16:
                    S.dma(C["XS"][tok0:tok0 + 512, chb0 * 128:chb0 * 128 + 512].rearrange("(j p) c -> p j c", p=128),
                          stT[st], r=[("stT", st, s_) for s_ in range(4)])
                elif chb0 < 24:
                    cc = (chb0 - 16) * 128
                    S.dma(C["BTOK"][tok0:tok0 + 512, cc:cc + 512].rearrange("(j p) c -> p j c", p=128),
                          stT[st], r=[("stT", st, s_) for s_ in range(4)])
                if chb0 >= 16:
                    r0 = (chb0 - 16) * 128
                    S.dma(C["BCT"][r0:r0 + 512, tok0:tok0 + 512].rearrange("(s p) t -> p s t", p=128),
                          xc[st], r=[("xc", st, s_) for s_ in range(4)])


def phaseA(C):
    S, sb, T = C["S"], C["sb"], C["T"]
    bank, PS, mask2, ones_bf = C["bank"], C["PS"], C["mask2"], C["ones_bf"]
    NBLK = T // 128
    q2 = [sb.alloc([128, T], BF16) for _ in range(2)]
    k2 = [sb.alloc([128, T], BF16) for _ in range(2)]
    v2 = [sb.alloc([128, NBLK, 128], BF16) for _ in range(2)]
    PT = [sb.alloc([128, 2, 2, 2, 128], BF16) for _ in range(2)]
    ost = [sb.alloc([128, 8, 130], F32) for _ in range(2)]
    mask4 = sb.alloc([128, 2, 256], BF16)
    for j in range(2):
        S.op("pool", I("tensor_copy", out=mask4[:, j, :], in_=mask2.rearrange("p t q -> p (t q)")),
             r=["mask2a", "mask2b"], w=[("mask4", j)])
    mask4f = mask4.rearrange("p j c -> p (j c)")
    combo = 0
    ostc = 0
    for g in range(3):
        d = DIL[g]
        nb = T // d // 128
        OB = min(nb, 8)
        for hp in range(8):
            sl = (g * 8 + hp) % 2
            rows = slice(hp * 128, (hp + 1) * 128)
            S.dma(q2[sl], C["QT"][g][rows, :], w=[("q2", sl)])
            S.dma(k2[sl], C["KT"][g][rows, :], w=[("k2", sl)])
            vsrc = C["V"][g][:, rows].rearrange("(b i r) c -> r i b c", i=128, r=d)
            for r in range(d):
                for c0 in range(0, nb, 8):
                    c1 = min(nb, c0 + 8)
                    S.dma(v2[sl][:, r * nb + c0:r * nb + c1, :], vsrc[r][:, c0:c1, :], w=[("v2", sl, r, c0)])
            qS = q2[sl].rearrange("p (m r) -> p r m", r=d)
            kS = k2[sl].rearrange("p (m r) -> p r m", r=d)
            odst = C["OA"][g][:, hp * 130:(hp + 1) * 130].rearrange("(b i r) c -> r i b c", i=128, r=d)
            for r in range(d):
                for b in range(0, nb, 2):
                    x = combo % 2
                    combo += 1
                    mm = []
                    for hh in range(2):
                        pr = slice(hh * 64, (hh + 1) * 64)
                        bk = bank(2 * x + hh)
                        for j in range(2):
                            bb = b + j
                            qb = qS[pr, r, bb * 128:(bb + 1) * 128]
                            if bb > 0:
                                mm.append(I("matmul", out=bk[:, j * 256:j * 256 + 128],
                                            lhsT=kS[pr, r, (bb - 1) * 128:bb * 128], rhs=qb, start=True, stop=True))
                            mm.append(I("matmul", out=bk[:, j * 256 + 128:j * 256 + 256],
                                        lhsT=kS[pr, r, bb * 128:(bb + 1) * 128], rhs=qb, start=True, stop=True))
                    bkeys = [("bank", 2 * x), ("bank", 2 * x + 1)]
                    S.op("pe", mm, r=[("q2", sl), ("k2", sl)], w=bkeys)
                    lo = 0 if b > 0 else 128
                    pin = PS[x].rearrange("p (h c) -> p h c", h=2)[:, :, lo:512]
                    pout = PT[x].rearrange("p h j t q -> p h (j t q)")[:, :, lo:512]
                    S.op("act", I("activation", out=pout, in_=pin, func=AF.Exp, scale=0.125),
                         r=bkeys, w=[("PT", x)])
                    mk = mask4f[:, lo:512].unsqueeze(1).to_broadcast([128, 2, 512 - lo])
                    S.op("dve" if combo % 2 == 0 else "pool",
                         I("tensor_tensor", out=pout, in0=pout, in1=mk, op=ALU.mult),
                         r=[("PT", x), ("mask4", 0), ("mask4", 1)], w=[("PT", x)])
                    ob = combo % 4
                    mm = []
                    for hh in range(2):
                        hc = slice(hh * 64, (hh + 1) * 64)
                        for j in range(2):
                            bb = b + j
                            blk = r * nb + bb
                            o_ = bank(4 + ob)[:, j * 130 + hh * 65:j * 130 + hh * 65 + 64]
                            dn = bank(4 + ob)[:, j * 130 + hh * 65 + 64:j * 130 + hh * 65 + 65]
                            if bb > 0:
                                mm.append(I("matmul", out=o_, lhsT=PT[x][:, hh, j, 0, :], rhs=v2[sl][:, blk - 1, hc],
                                            start=True, stop=False))
                            mm.append(I("matmul", out=o_, lhsT=PT[x][:, hh, j, 1, :], rhs=v2[sl][:, blk, hc],
                                        start=(bb == 0), stop=True))
                            if bb > 0:
                                mm.append(I("matmul", out=dn, lhsT=PT[x][:, hh, j, 0, :], rhs=ones_bf[:, 0:1],
                                            start=True, stop=False))
                            mm.append(I("matmul", out=dn, lhsT=PT[x][:, hh, j, 1, :], rhs=ones_bf[:, 0:1],
                                        start=(bb == 0), stop=True))
                    vkeys = [("v2", sl, r, c0) for c0 in range(0, nb, 8)]
                    S.op("pe", mm, r=[("PT", x), "onesbf"] + vkeys, w=[("bank", 4 + ob)])
                    os_ = ostc % 2
                    eng = "act" if combo % 2 == 0 else "dve"
                    S.op(eng, I("copy" if eng == "act" else "tensor_copy",
                                out=ost[os_][:, b % OB:b % OB + 2, :].rearrange("p j c -> p (j c)"),
                                in_=bank(4 + ob)[:, 0:260]),
                         r=[("bank", 4 + ob)], w=[("ost", os_, b % OB)])
                    if (b + 1) % OB == OB - 1:
                        b0 = b + 1 - (OB - 1)
                        S.dma(odst[r][:, b0:b0 + OB, :], ost[os_][:, 0:OB, :],
                              r=[("ost", os_, j) for j in range(0, OB, 2)])
                        ostc += 1
            yield


def phaseS(C):
    S, sb, T, l, NT = C["S"], C["sb"], C["T"], C["l"], C["NT"]
    bank, LAs, DTs = C["bank"], C["LAs"], C["DTs"]
    U_f, Ls_f, ones_f = C["U_f"], C["Ls_f"], C["ones_f"]
    H = sb.alloc([128, 2048], F32)
    Hbf = sb.alloc([128, 2048], BF16)
    dbc = sb.alloc([128, 32], F32)
    xs_t = [sb.alloc([128, 2048], BF16) for _ in range(2)]
    b_t = [sb.alloc([128, 1024], BF16) for _ in range(2)]
    bc4 = [sb.alloc([128, 16, 256], BF16) for _ in range(2)]
    ex = [sb.alloc([128, 96], F32) for _ in range(2)]
    xds = [sb.alloc([128, 32, 64], BF16) for _ in range(2)]
    xdt = [sb.alloc([128, 32, 64], BF16) for _ in range(2)]
    dx = [sb.alloc([128, 32, 64], BF16) for _ in range(2)]
    ybf = [sb.alloc([128, 2048], BF16) for _ in range(2)]
    cbm = [sb.alloc([128, 128], F32) for _ in range(2)]
    lseg = [sb.alloc([128, 4, 128], F32) for _ in range(2)]
    dec = [sb.alloc([128, 4, 128], F32) for _ in range(2)]
    MT = [sb.alloc([128, 4, 128], BF16) for _ in range(2)]
    tt_ = [sb.alloc([128, 4, 64], F32) for _ in range(2)]
    S.dma(dbc, C["P"]["d_skip"][l:l + 1, :].partition_broadcast(128), w=["dbc"])
    S.op("pool", I("memset", ap=H, constant=0.0), w=[("H", g) for g in range(8)])
    S.op("pool", I("memset", ap=Hbf, constant=0.0), w=[("Hbf", g) for g in range(8)])

    def loads(c):
        s = c % 2
        S.dma(xs_t[s], C["XS"][c * 128:(c + 1) * 128, :], w=[("xs_t", s)])
        S.dma(b_t[s], C["BTOK"][c * 128:(c + 1) * 128, :], w=[("b_t", s)])
        if c % 2 == 0:
            s4 = (c // 2) % 2
            S.dma(bc4[s4], C["BCT"][:, c * 128:c * 128 + 256].rearrange("(j p) t -> p j t", p=128),
                  w=[("bc4", s4)])

    loads(0)
    xi = 0
    for c in range(NT):
        if c + 1 < NT:
            loads(c + 1)
        s = c % 2
        s4 = (c // 2) % 2
        la = LAs[:, c, :]
        S.op("pe", [I("matmul", out=bank(0)[:, 0:32], lhsT=U_f, rhs=la, start=True, stop=True),
                    I("matmul", out=bank(0)[:, 32:64], lhsT=Ls_f, rhs=la, start=True, stop=True),
                    I("matmul", out=bank(0)[:, 64:96], lhsT=ones_f, rhs=la, start=True, stop=True)],
             r=["cst", ("LAs", c)], w=[("bank", 0)])
        S.op("act", I("activation", out=ex[s], in_=bank(0)[:, 0:96], func=AF.Exp),
             r=[("bank", 0)], w=[("ex", s)])
        S.op("pool", I("tensor_tensor", out=xdt[s], in0=xs_t[s].rearrange("p (h e) -> p h e", e=64),
                       in1=DTs[:, c, :].unsqueeze(2).to_broadcast([128, 32, 64]), op=ALU.mult),
             r=[("xs_t", s), ("DTs", c)], w=[("xdt", s)])
        S.op("dve", I("tensor_tensor", out=xds[s], in0=xdt[s],
                      in1=ex[s][:, 32:64].unsqueeze(2).to_broadcast([128, 32, 64]), op=ALU.mult),
             r=[("xdt", s), ("ex", s)], w=[("xds", s)])
        S.op("pool", I("tensor_tensor", out=dx[s], in0=xs_t[s].rearrange("p (h e) -> p h e", e=64),
                       in1=dbc.unsqueeze(2).to_broadcast([128, 32, 64]), op=ALU.mult),
             r=[("xs_t", s), "dbc"], w=[("dx", s)])
        tk = slice((c % 2) * 128, (c % 2 + 1) * 128)
        for g in range(8):
            x = xi % 2
            xi += 1
            BT = bc4[s4][:, g, tk]
            CT = bc4[s4][:, 8 + g, tk]
            hs = slice(g * 4, (g + 1) * 4)
            cs = slice(g * 256, (g + 1) * 256)
            cbp = bank(1)[:, 0:128]
            S.op("pe", I("matmul", out=cbp, lhsT=BT, rhs=CT, start=True, stop=True),
                 r=[("bc4", s4)], w=[("bank", 1)])
            S.op("dve", I("tensor_tensor", out=cbm[x], in0=cbp, in1=U_f, op=ALU.mult),
                 r=[("bank", 1), "cst"], w=[("cbm", x)])
            S.op("pool", I("tensor_tensor", out=lseg[x], in0=Ls_f.unsqueeze(1).to_broadcast([128, 4, 128]),
                           in1=la[:, hs].unsqueeze(2).to_broadcast([128, 4, 128]), op=ALU.mult),
                 r=["cst", ("LAs", c)], w=[("lseg", x)])
            S.op("pe", [I("matmul", out=bank(2 + x)[:, e * 128:(e + 1) * 128], lhsT=lseg[x][:, e, :], rhs=U_f,
                          start=True, stop=True) for e in range(4)],
                 r=[("lseg", x), "cst"], w=[("bank", 2 + x)])
            S.op("act", I("activation", out=dec[x], in_=bank(2 + x).rearrange("p (e l) -> p e l", e=4),
                          func=AF.Exp),
                 r=[("bank", 2 + x)], w=[("dec", x)])
            S.op("dve", I("tensor_tensor", out=MT[x], in0=dec[x],
                          in1=cbm[x].unsqueeze(1).to_broadcast([128, 4, 128]), op=ALU.mult),
                 r=[("dec", x), ("cbm", x)], w=[("MT", x)])
            mm = [I("matmul", out=bank(4 + x)[:, e * 64:(e + 1) * 64], lhsT=MT[x][:, e, :],
                    rhs=xdt[s][:, g * 4 + e, :], start=True, stop=True)
                  for e in range(4)]
            mm.append(I("matmul", out=bank(4 + x)[:, 256:512], lhsT=CT, rhs=Hbf[:, cs], start=True, stop=True))
            S.op("pe", mm, r=[("MT", x), ("xdt", s), ("bc4", s4), ("Hbf", g)], w=[("bank", 4 + x)])
            S.op("dve", I("tensor_tensor", out=tt_[x],
                          in0=bank(4 + x)[:, 256:512].rearrange("p (e q) -> p e q", e=4),
                          in1=ex[s][:, hs].unsqueeze(2).to_broadcast([128, 4, 64]), op=ALU.mult),
                 r=[("bank", 4 + x), ("ex", s)], w=[("tt", x)])
            S.op("pool", I("tensor_tensor", out=tt_[x], in0=tt_[x], in1=dx[s][:, hs, :], op=ALU.add),
                 r=[("tt", x), ("dx", s)], w=[("tt", x)])
            S.op("dve", I("tensor_tensor", out=ybf[s][:, cs].rearrange("p (e q) -> p e q", e=4),
                          in0=bank(4 + x)[:, 0:256].rearrange("p (e q) -> p e q", e=4), in1=tt_[x], op=ALU.add),
                 r=[("bank", 4 + x), ("tt", x)], w=[("ybf", s, g)])
            php = bank(6 + x)[:, 0:256]
            S.op("pe", I("matmul", out=php, lhsT=b_t[s][:, g * 128:(g + 1) * 128],
                         rhs=xds[s][:, hs, :].rearrange("p h e -> p (h e)"), start=True, stop=True),
                 r=[("b_t", s), ("xds", s)], w=[("bank", 6 + x)])
            Hg = H[:, cs].rearrange("p (e q) -> p e q", e=4)
            S.op("pool", I("tensor_tensor", out=Hg, in0=Hg,
                           in1=ex[s][:, 64 + g * 4:64 + (g + 1) * 4].unsqueeze(2).to_broadcast([128, 4, 64]),
                           op=ALU.mult),
                 r=[("H", g), ("ex", s)], w=[("H", g)])
            S.op("dve", I("tensor_tensor", out=H[:, cs], in0=H[:, cs], in1=php, op=ALU.add),
                 r=[("H", g), ("bank", 6 + x)], w=[("H", g)])
            S.op("act", I("copy", out=Hbf[:, cs], in_=H[:, cs]), r=[("H", g)], w=[("Hbf", g)])
            yield
        S.dma(C["YS"][c * 128:(c + 1) * 128, :], ybf[s], r=[("ybf", s, g) for g in range(8)])


def phaseF(C):
    S, sb, T, l, NT = C["S"], C["sb"], C["T"], C["l"], C["NT"]
    bank, bank_bf, PS, ident_bf, rstd_ops = C["bank"], C["bank_bf"], C["PS"], C["ident_bf"], C["rstd_ops"]
    P = C["P"]
    x_src, x_dst = C["x_src"], C["x_dst"]
    sb.cur = C["persist_mark0"]
    wso = sb.alloc([128, 16, 1024], BF16)
    wao = sb.alloc([128, 8, 1024], BF16)
    wmo = sb.alloc([128, 8, 1024], BF16)
    wo = sb.alloc([128, 8, 1024], BF16)
    ssdn = sb.alloc([128, 2048], F32)
    npost = sb.alloc([128, 1024], F32)
    KmT = sb.alloc([128, 8, 256], BF16)
    Vm1 = sb.alloc([128, 2, 4, 257], BF16)
    fmark = sb.cur
    wtmp = [sb.alloc([128, 4, 1024], F32) for _ in range(2)]
    wi = 0
    for (dstw, src, nkc) in ((wso, P["w_ssd_out"][l], 16), (wao, P["w_attn_out"][l], 8),
                             (wmo, P["w_mem_out"][l], 8), (wo, P["w_out"][l], 8)):
        srcv = src.rearrange("(kc p) n -> p kc n", p=128)
        for k0 in range(0, nkc, 4):
            s = wi % 2
            wi += 1
            S.dma(wtmp[s], srcv[:, k0:k0 + 4, :], w=[("wtmp", s)])
            S.op("pool" if wi % 2 else "dve", I("tensor_copy", out=dstw[:, k0:k0 + 4, :], in_=wtmp[s]),
                 r=[("wtmp", s)], w=[("wres", wi)])
    S.dma(ssdn, P["ssd_norm"][l:l + 1, :].partition_broadcast(128), w=["ssdn"])
    S.dma(npost, P["norm_post"][l:l + 1, :].partition_broadcast(128), w=["npost"])
    mnorm = sb.alloc([128, 1024], F32)
    S.dma(mnorm, P["mem_norm"][l:l + 1, :].partition_broadcast(128), w=["mnorm"])
    memT = sb.alloc([128, 8, 256], BF16)
    mx = sb.alloc([128, 1024], F32)
    mxn = sb.alloc([128, 1024], BF16)
    mss = sb.alloc([128, 1], F32)
    mrs = sb.alloc([128, 1], F32)
    for mb in range(2):
        S.dma(mx, C["mem_in"][mb * 128:(mb + 1) * 128, :], w=["mx"])
        S.op("act", I("activation", out=mxn, in_=mx, func=AF.Square, accum_out=mss[:, 0:1]),
             r=["mx"], w=["mss", "mxn"])
        rstd_ops(mss, mrs, D, ["mss"], "mrs")
        S.op("dve", I("scalar_tensor_tensor", out=mxn, in0=mx, scalar=mrs[:, 0:1], in1=mnorm,
                      op0=ALU.mult, op1=ALU.mult), r=["mx", "mrs", "mnorm"], w=["mxn"])
        S.op("pe", [I("transpose", out=bank_bf(4)[:, k * 128:(k + 1) * 128], in_=mxn[:, k * 128:(k + 1) * 128],
                      identity=ident_bf) for k in range(8)], r=["mxn", "ident"], w=[("bank", 4)])
        S.op("act", I("copy", out=memT[:, :, mb * 128:(mb + 1) * 128],
                      in_=bank_bf(4).rearrange("p (k t) -> p k t", k=8)), r=[("bank", 4)], w=[("memT", mb)])
    wkv = sb.alloc([128, 8, 512], BF16)
    wkvf = sb.alloc([128, 8, 512], F32)
    S.op("pool", I("memset", ap=Vm1, constant=1.0), w=["Vm1"])
    for cb in range(4):
        S.dma(wkvf, P["w_mem_kv"][l][:, cb * 512:(cb + 1) * 512].rearrange("(kc p) n -> p kc n", p=128),
              w=["wkvf"])
        S.op("dve", I("tensor_copy", out=wkv, in_=wkvf), r=["wkvf"], w=["wkv"])
        if cb < 2:
            for sub in range(4):
                j = cb * 4 + sub
                S.op("pe", [I("matmul", out=bank(5)[:, 0:256], lhsT=wkv[:, kc, sub * 128:(sub + 1) * 128],
                              rhs=memT[:, kc, :], start=(kc == 0), stop=(kc == 7)) for kc in range(8)],
                     r=["wkv", ("memT", 0), ("memT", 1)], w=[("bank", 5)])
                S.op("act", I("copy", out=KmT[:, j, :], in_=bank(5)[:, 0:256]), r=[("bank", 5)], w=[("KmT", j)])
        else:
            for mb in range(2):
                S.op("pe", [I("matmul", out=bank(5), lhsT=memT[:, kc, mb * 128:(mb + 1) * 128],
                              rhs=wkv[:, kc, :], start=(kc == 0), stop=(kc == 7)) for kc in range(8)],
                     r=["wkv", ("memT", 0), ("memT", 1)], w=[("bank", 5)])
                h0 = (cb - 2) * 2
                S.op("act", I("copy", out=Vm1[:, mb, h0:h0 + 2, 0:256],
                              in_=bank(5).rearrange("p (h e) -> p h e", h=2)),
                     r=[("bank", 5), "Vm1"], w=[("Vm1w", cb, mb)])
    S.barrier()
    sb.cur = fmark
    ys = [sb.alloc([128, 2048], BF16) for _ in range(2)]
    z1 = [sb.alloc([128, 2048], BF16) for _ in range(2)]
    oa = sb.alloc([128, 3, 1040], F32)
    za = [sb.alloc([128, 1024], BF16) for _ in range(2)]
    qmt = [sb.alloc([128, 8, 128], BF16) for _ in range(2)]
    zm = [sb.alloc([128, 1024], BF16) for _ in range(2)]
    gt = [sb.alloc([128, 3072], BF16) for _ in range(2)]
    xr = [sb.alloc([128, 1024], F32) for _ in range(2)]
    t1 = sb.alloc([128, 2048], F32)
    abfA = sb.alloc([128, 1024], BF16)
    abfS = sb.alloc([128, 2048], BF16)
    abfM = sb.alloc([128, 1024], BF16)
    actTA = sb.alloc([128, 8, 128], BF16)
    actTS = sb.alloc([128, 16, 128], BF16)
    actTM = sb.alloc([128, 8, 128], BF16)
    merged = sb.alloc([128, 1024], F32)
    num = sb.alloc([128, 1040], F32)
    ob_ = sb.alloc([128, 1024], F32)
    tmp = sb.alloc([128, 1024], F32)
    PmT = [sb.alloc([128, 4, 128], BF16) for _ in range(2)]
    ssg = sb.alloc([128, 8], F32)
    rs8 = sb.alloc([128, 8], F32)
    rden = sb.alloc([128, 16], F32)
    rdm = sb.alloc([128, 4], F32)
    ssf = sb.alloc([128, 1], F32)
    rsf = sb.alloc([128, 1], F32)
    om = t1[:, 0:1024]
    xo = t1[:, 1024:2048]

    def loads(i):
        s = i % 2
        tk = slice(i * 128, (i + 1) * 128)
        S.dma(ys[s], C["YS"][tk, :], w=[("ys", s)])
        S.dma(z1[s], C["Z1"][tk, :], w=[("z1", s)])
        S.dma(za[s], C["ZA"][tk, :], w=[("za", s)])
        S.dma(qmt[s], C["QMT"][:, tk].rearrange("(j p) t -> p j t", p=128), w=[("qmt", s)])
        S.dma(zm[s], C["ZM"][tk, :], w=[("zm", s)])
        S.dma(gt[s], C["G"][tk, :], w=[("gt", s)])
        S.dma(xr[s], x_src[tk, :], w=[("xr", s)])

    def transposes(src, skey, dstT, dkey, nk, banks):
        for b0 in range(0, nk, 8):
            bk = banks[b0 // 8]
            S.op("pe", [I("transpose", out=bank_bf(bk)[:, k * 128:(k + 1) * 128],
                          in_=src[:, (b0 + k) * 128:(b0 + k + 1) * 128], identity=ident_bf) for k in range(8)],
                 r=[skey, "ident"], w=[("bank", bk)])
            S.op("act", I("copy", out=dstT[:, b0:b0 + 8, :], in_=bank_bf(bk).rearrange("p (k t) -> p k t", k=8)),
                 r=[("bank", bk)], w=[(dkey, b0 // 8)])

    def outproj(wt, srcT, skey, nk, pso, okey):
        for hh in range(2):
            S.op("pe", [I("matmul", out=pso[:, hh * 512:(hh + 1) * 512], lhsT=srcT[:, kc, :],
                          rhs=wt[:, kc, hh * 512:(hh + 1) * 512], start=(kc == 0), stop=(kc == nk - 1))
                        for kc in range(nk)],
                 r=[(skey, j) for j in range((nk + 7) // 8)], w=[("bank", okey + hh)])

    loads(0)
    for i in range(NT):
        s = i % 2
        tk = slice(i * 128, (i + 1) * 128)
        for g in range(3):
            S.dma(oa[:, g, :], C["OA"][g][tk, :], w=[("oa", g)])
        if i + 1 < NT:
            loads(i + 1)
        S.op("pool", I("tensor_tensor", out=num, in0=oa[:, 0, :], in1=oa[:, 1, :], op=ALU.add),
             r=[("oa", 0), ("oa", 1)], w=["num"])
        S.op("pool", I("tensor_tensor", out=num, in0=num, in1=oa[:, 2, :], op=ALU.add),
             r=[("oa", 2), "num"], w=["num"])
        numv = num.rearrange("p (h e) -> p h e", e=65)
        S.op("dve", I("tensor_tensor", out=t1, in0=ys[s], in1=z1[s], op=ALU.mult),
             r=[("ys", s), ("z1", s)], w=["t1a", "t1b"])
        S.op("act", [I("activation", out=abfS[:, 0:256], in_=t1[:, g * 256:(g + 1) * 256], func=AF.Square,
                       accum_out=ssg[:, g:g + 1]) for g in range(8)], r=["t1a", "t1b"], w=["ssg", "abfS"])
        S.op("dve", I("reciprocal", out=rden, in_=numv[:, :, 64]), r=["num"], w=["rden"])
        rstd_ops(ssg, rs8, 256, ["ssg"], "rs8")
        S.op("dve", I("tensor_tensor", out=ob_.rearrange("p (h e) -> p h e", e=64), in0=numv[:, :, 0:64],
                      in1=rden.unsqueeze(2).to_broadcast([128, 16, 64]), op=ALU.mult),
             r=["num", "rden"], w=["ob"])
        S.op("pool", I("tensor_tensor", out=abfA, in0=ob_, in1=za[s], op=ALU.mult),
             r=["ob", ("za", s)], w=["abfA"])
        transposes(abfA, "abfA", actTA, "actTA", 8, [4])
        for hp in range(2):
            mm = []
            for hh in range(2):
                h = hp * 2 + hh
                for mb in range(2):
                    for ec in range(2):
                        mm.append(I("matmul", out=bank(5)[:, (hh * 2 + mb) * 128:(hh * 2 + mb + 1) * 128],
                                    lhsT=KmT[:, h * 2 + ec, mb * 128:(mb + 1) * 128], rhs=qmt[s][:, h * 2 + ec, :],
                                    start=(ec == 0), stop=(ec == 1)))
            S.op("pe", mm, r=[("qmt", s)], w=[("bank", 5)])
            S.op("act", I("activation", out=PmT[hp], in_=bank(5).rearrange("p (j t) -> p j t", j=4),
                          func=AF.Exp, scale=1.0 / 16.0), r=[("bank", 5)], w=[("PmT", hp)])
        outproj(wao, actTA, "actTA", 8, PS[3], 6)
        S.op("dve", I("tensor_tensor", out=t1.rearrange("p (g e) -> p g e", g=8),
                      in0=t1.rearrange("p (g e) -> p g e", g=8),
                      in1=rs8.unsqueeze(2).to_broadcast([128, 8, 256]), op=ALU.mult),
             r=["t1a", "t1b", "rs8"], w=["t1a", "t1b"])
        S.op("pool", I("tensor_tensor", out=abfS, in0=t1, in1=ssdn, op=ALU.mult),
             r=["t1a", "t1b", "ssdn"], w=["abfS"])
        S.op("dve", I("tensor_tensor", out=merged, in0=PS[3], in1=gt[s][:, 1024:2048], op=ALU.mult),
             r=[("bank", 6), ("bank", 7), ("gt", s)], w=["merged"])
        transposes(abfS, "abfS", actTS, "actTS", 16, [0, 1])
        outproj(wso, actTS, "actTS", 16, PS[1], 2)
        S.op("dve", I("tensor_tensor", out=tmp, in0=PS[1], in1=gt[s][:, 0:1024], op=ALU.mult),
             r=[("bank", 2), ("bank", 3), ("gt", s)], w=["tmp"])
        S.op("pool", I("tensor_tensor", out=merged, in0=merged, in1=tmp, op=ALU.add),
             r=["tmp", "merged"], w=["merged"])
        for hp in range(2):
            for hh in range(2):
                h = hp * 2 + hh
                S.op("pe", [I("matmul", out=bank(hh)[:, 0:257], lhsT=PmT[hp][:, hh * 2 + mb, :],
                              rhs=Vm1[:, mb, h, :], start=(mb == 0), stop=(mb == 1)) for mb in range(2)],
                     r=[("PmT", hp)], w=[("bank", hh)])
                S.op("dve", I("reciprocal", out=rdm[:, h:h + 1], in_=bank(hh)[:, 256:257]),
                     r=[("bank", hh)], w=[("rdm", h)])
                S.op("dve", I("tensor_scalar", out=om[:, h * 256:(h + 1) * 256], in0=bank(hh)[:, 0:256],
                              scalar1=rdm[:, h:h + 1], scalar2=None, op0=ALU.mult),
                     r=[("bank", hh), ("rdm", h)], w=["t1a"])
        S.op("pool", I("tensor_tensor", out=abfM, in0=om, in1=zm[s], op=ALU.mult),
             r=["t1a", ("zm", s)], w=["abfM"])
        transposes(abfM, "abfM", actTM, "actTM", 8, [4])
        outproj(wmo, actTM, "actTM", 8, PS[1], 2)
        S.op("dve", I("tensor_tensor", out=tmp, in0=PS[1], in1=gt[s][:, 2048:3072], op=ALU.mult),
             r=[("bank", 2), ("bank", 3), ("gt", s)], w=["tmp"])
        S.op("pool", I("tensor_tensor", out=merged, in0=merged, in1=tmp, op=ALU.add),
             r=["tmp", "merged"], w=["merged"])
        S.op("act", I("copy", out=abfA, in_=merged), r=["merged"], w=["abfA"])
        transposes(abfA, "abfA", actTA, "actTA", 8, [4])
        outproj(wo, actTA, "actTA", 8, PS[3], 6)
        S.op("act", I("activation", out=abfA, in_=PS[3], func=AF.Square, accum_out=ssf[:, 0:1]),
             r=[("bank", 6), ("bank", 7)], w=["ssf", "abfA"])
        rstd_ops(ssf, rsf, D, ["ssf"], "rsf")
        S.op("dve", I("scalar_tensor_tensor", out=xo, in0=PS[3], scalar=rsf[:, 0:1], in1=npost,
                      op0=ALU.mult, op1=ALU.mult),
             r=[("bank", 6), ("bank", 7), "rsf", "npost"], w=["t1b"])
        S.op("pool", I("tensor_tensor", out=xo, in0=xo, in1=xr[s], op=ALU.add), r=["t1b", ("xr", s)], w=["t1b"])
        S.dma(x_dst[tk, :], xo, r=["t1b"])


WNAMES = ["norm_pre", "norm_post", "w_in", "conv_w", "conv_b", "dt_bias", "a_log", "d_skip", "ssd_norm",
          "w_ssd_out", "w_attn_out", "mem_norm", "w_mem_kv", "w_mem_out", "w_out"]
_PROG = {}
FUSED = True


def _get_prog(T, NL):
    key = (T, NL)
    if key not in _PROG:
        _PROG[key] = build_program(T, NL)
    return _PROG[key]


def prep_weights(inputs):
    w = {k: np.ascontiguousarray(np.asarray(inputs[k], dtype=np.float32)) for k in WNAMES}
    nl = w["conv_w"].shape[0]
    cw = w["conv_w"].reshape(nl, 4, 32, 128).transpose(0, 3, 2, 1)
    w["conv_w"] = np.ascontiguousarray(cw.reshape(nl, 128, 128))
    cb = w["conv_b"].reshape(nl, 32, 128).transpose(0, 2, 1)
    w["conv_b"] = np.ascontiguousarray(cb)
    return w


def kernel(**inputs):
    x = np.ascontiguousarray(np.asarray(inputs["x"], dtype=np.float32))
    mem = np.ascontiguousarray(np.asarray(inputs["mem"], dtype=np.float32))
    B, T, _ = x.shape
    consts = make_consts()
    w = prep_weights(inputs)
    depth = w["w_in"].shape[0]
    if FUSED:
        nc = _get_prog(T, depth)
        in_maps = []
        for b in range(B):
            m = {"x": x[b], "mem": mem[b], "consts": consts}
            m.update(w)
            in_maps.append(m)
        res = run_bass_kernel_spmd(nc, in_maps, core_ids=list(range(B)))
        return np.stack([np.asarray(r["out"]) for r in res.results], axis=0).astype(np.float32)
    cur = [x[b] for b in range(B)]
    nc = _get_prog(T, 1)
    for l in range(depth):
        in_maps = []
        for b in range(B):
            m = {"x": cur[b], "mem": mem[b], "consts": consts}
            m.update({k: w[k][l:l + 1] for k in WNAMES})
            in_maps.append(m)
        res = run_bass_kernel_spmd(nc, in_maps, core_ids=list(range(B)))
        cur = [np.ascontiguousarray(np.asarray(r["out"], dtype=np.float32)) for r in res.results]
    return np.stack(cur, axis=0).astype(np.float32)
```

```python
import numpy as np
import concourse.bass as bass
import concourse.mybir as mybir
from concourse.bass_utils import run_bass_kernel_spmd

F32 = mybir.dt.float32
BF16 = mybir.dt.bfloat16
AF = mybir.ActivationFunctionType
ALU = mybir.AluOpType

D = 1024
DEPTH = 2
EPS = 1e-6
D_INNER = 2048
NH = 32
NG = 8
DS = 128
MEM_LEN = 256
OFF_ZSSD = 0
OFF_XBC = 2048
OFF_DT = 6144
OFF_QKV = 6176
OFF_ZATT = OFF_QKV + 9216
OFF_QMEM = OFF_ZATT + 1024
OFF_ZMEM = OFF_QMEM + 1024
OFF_GATE = OFF_ZMEM + 1024
N_IN = OFF_GATE + 3072
DIL = (1, 4, 16)
N_DMA_SEMS = 24


class Sched:
    ENGS = ("pe", "act", "dve", "pool", "sp")

    def __init__(self):
        self.ops = []

    def op(self, eng, instrs, r=(), w=()):
        if isinstance(instrs, tuple):
            instrs = [instrs]
        instrs = list(instrs)

        def fn(e, instrs=instrs):
            ins = None
            for m, kw in instrs:
                ins = getattr(e, m)(**kw)
            return ins
        r, w = list(r), list(w)
        for k in r:
            if isinstance(k, tuple) and k and k[0] == "bank" and k not in w:
                w.append(k)
        self.ops.append(dict(eng=eng, fn=fn, r=tuple(r), w=tuple(w), dma=False))

    def dma(self, out, in_, r=(), w=(), eng="sp"):
        def fn(e, out=out, in_=in_):
            return e.dma_start(out=out, in_=in_)
        self.ops.append(dict(eng=eng, fn=fn, r=tuple(r), w=tuple(w), dma=True))

    def barrier(self):
        self.ops.append(dict(barrier=True))

    def emit(self, nc):
        ops = self.ops
        n = len(ops)
        last_w = {}
        readers = {}
        deps = [None] * n
        dma_sem_of = [None] * n
        dma_rr = 0
        dma_last_use = [None] * N_DMA_SEMS
        bar_pending = {}
        last_op_of = {}
        outstanding = set()
        for i, o in enumerate(ops):
            if o.get("barrier"):
                allprev = set(outstanding)
                for e in self.ENGS:
                    bar_pending[e] = bar_pending.get(e, set()) | allprev
                outstanding = set()
                last_w.clear()
                readers.clear()
                continue
            d = set()
            for k in o["r"]:
                if k in last_w:
                    d.add(last_w[k])
            for k in o["w"]:
                if k in last_w:
                    d.add(last_w[k])
                for rr in readers.get(k, ()):
                    d.add(rr)
            if o["eng"] in bar_pending and bar_pending[o["eng"]]:
                d |= bar_pending[o["eng"]]
                bar_pending[o["eng"]] = set()
            if o["dma"]:
                j = dma_rr % N_DMA_SEMS
                dma_rr += 1
                dma_sem_of[i] = j
                if dma_last_use[j] is not None:
                    d.add(dma_last_use[j])
                dma_last_use[j] = i
            d.discard(i)
            deps[i] = d
            for k in o["r"]:
                readers.setdefault(k, []).append(i)
            for k in o["w"]:
                last_w[k] = i
                readers[k] = []
            if o["dma"]:
                outstanding.add(i)
            else:
                prev = last_op_of.get(o["eng"])
                if prev is not None:
                    outstanding.discard(prev)
                last_op_of[o["eng"]] = i
                outstanding.add(i)
        self.final_wait = set(outstanding) | bar_pending.get("sp", set())
        needed = [False] * n
        for i, o in enumerate(ops):
            if o.get("barrier"):
                continue
            for dd in deps[i]:
                if ops[dd]["eng"] == "pe" and o["eng"] == "pe" and not ops[dd]["dma"]:
                    continue
                needed[dd] = True
        for dd in self.final_wait:
            needed[dd] = True
        cnt = {e: 0 for e in self.ENGS}
        dcnt = [0] * N_DMA_SEMS
        event = [None] * n
        for i, o in enumerate(ops):
            if o.get("barrier"):
                continue
            if o["dma"]:
                j = dma_sem_of[i]
                dcnt[j] += 16
                event[i] = (("d", j), dcnt[j])
            elif needed[i]:
                cnt[o["eng"]] += 1
                event[i] = (("e", o["eng"]), cnt[o["eng"]])
        self.stats = (dict(cnt), max(dcnt), n)
        import contextlib
        with contextlib.ExitStack() as st:
            esem = {e: st.enter_context(nc.semaphore("s_" + e)) for e in self.ENGS}
            dsem = [st.enter_context(nc.semaphore("d_%d" % j)) for j in range(N_DMA_SEMS)]
            block = st.enter_context(nc.Block())

            def sem_of(key):
                return esem[key[1]] if key[0] == "e" else dsem[key[1]]

            def run(engname, eng):
                known = {}
                for i, o in enumerate(ops):
                    if o.get("barrier") or o["eng"] != engname:
                        continue
                    waits = {}
                    for dd in deps[i]:
                        if ops[dd]["eng"] == "pe" and engname == "pe" and not ops[dd]["dma"]:
                            continue
                        key, val = event[dd]
                        if known.get(key, 0) >= val:
                            continue
                        waits[key] = max(waits.get(key, 0), val)
                    for key, val in waits.items():
                        eng.wait_ge(sem_of(key), val)
                        known[key] = val
                    ins = o["fn"](eng)
                    if event[i] is not None:
                        key, val = event[i]
                        ins.then_inc(sem_of(key), 16 if key[0] == "d" else 1)
                if engname == "sp":
                    waits = {}
                    for dd in self.final_wait:
                        key, val = event[dd]
                        if known.get(key, 0) >= val:
                            continue
                        waits[key] = max(waits.get(key, 0), val)
                    for key, val in waits.items():
                        eng.wait_ge(sem_of(key), val)

            @block.tensor
            def _(e):
                run("pe", e)

            @block.scalar
            def _(e):
                run("act", e)

            @block.vector
            def _(e):
                run("dve", e)

            @block.gpsimd
            def _(e):
                run("pool", e)

            @block.sync
            def _(e):
                run("sp", e)


SB_BASE = 16512
SB_LIMIT = 229376 - 256


class SBAlloc:
    def __init__(self, nc):
        self.nc = nc
        self.cur = SB_BASE
        self.n = 0

    def alloc(self, shape, dt):
        nb = 1
        for s in shape[1:]:
            nb *= s
        nb *= 4 if dt == F32 else 2
        off = self.cur
        self.cur += (nb + 63) // 64 * 64
        assert self.cur <= SB_LIMIT, ("SBUF overflow", self.cur)
        self.n += 1
        return self.nc.alloc_sbuf_tensor_at("sb%d" % self.n, list(shape), dt, offset=off).ap()


def I(m, **kw):
    return (m, kw)


def make_consts():
    p = np.arange(128)[:, None]
    j = np.arange(128)[None, :]
    ident = (p == j)
    U = (p <= j)
    Ls = (p > j)
    ones = np.ones((128, 128), bool)
    Ge = (p >= j)
    return np.concatenate([ident, U, Ls, ones, Ge], axis=1).astype(np.float32)


def build_program(T, NL, dbg=()):
    nc = bass.Bass("TRN2", target_bir_lowering=False)
    S = Sched()
    NT = T // 128
    HALF = min(T, 4096)
    NHALF = T // HALF
    NTT = HALF // 512

    def dram(name, shape, dt, kind="Internal"):
        if name in dbg:
            kind = "ExternalOutput"
        return nc.dram_tensor(name, list(shape), dt, kind=kind).ap()

    x_in = dram("x", [T, D], F32, "ExternalInput")
    mem_in = dram("mem", [MEM_LEN, D], F32, "ExternalInput")
    consts_in = dram("consts", [128, 640], F32, "ExternalInput")
    P = {}
    for name, shp in [("norm_pre", [NL, D]), ("norm_post", [NL, D]), ("w_in", [NL, D, N_IN]),
                      ("conv_w", [NL, 128, 128]), ("conv_b", [NL, 128, 32]), ("dt_bias", [NL, NH]),
                      ("a_log", [NL, NH]), ("d_skip", [NL, NH]), ("ssd_norm", [NL, D_INNER]),
                      ("w_ssd_out", [NL, D_INNER, D]), ("w_attn_out", [NL, D, D]),
                      ("mem_norm", [NL, D]), ("w_mem_kv", [NL, D, 2 * D]),
                      ("w_mem_out", [NL, D, D]), ("w_out", [NL, D, D])]:
        P[name] = dram(name, shp, F32, "ExternalInput")
    out = dram("out", [T, D], F32, "ExternalOutput")
    xmid = [dram("xmid%d" % i, [T, D], F32) for i in range(NL - 1)]
    XS = dram("XS", [T, 2048], BF16)
    BTOK = dram("BTOK", [T, 1024], BF16)
    BCT = dram("BCT", [2048, T], BF16)
    Z1 = dram("Z1", [T, 2048], BF16)
    QT = [dram("QT%d" % g, [1024, T], BF16) for g in range(3)]
    KT = [dram("KT%d" % g, [1024, T], BF16) for g in range(3)]
    V = [dram("V%d" % g, [T, 1024], BF16) for g in range(3)]
    ZA = dram("ZA", [T, 1024], BF16)
    QMT = dram("QMT", [1024, T], BF16)
    ZM = dram("ZM", [T, 1024], BF16)
    G = dram("G", [T, 3072], BF16)
    YS = dram("YS", [T, 2048], BF16)
    OA = [dram("OA%d" % g, [T, 1040], F32) for g in range(3)]

    sb = SBAlloc(nc)
    PS = [nc.alloc_psum_tensor("ps%d" % i, [128, 1024], F32).ap() for i in range(4)]

    def bank(i):
        return PS[i // 2][:, (i % 2) * 512:(i % 2) * 512 + 512]

    def bank_bf(i):
        return bank(i).bitcast(BF16)

    cst = sb.alloc([128, 640], F32)
    ident_bf = sb.alloc([128, 128], BF16)
    mask2 = sb.alloc([128, 2, 128], BF16)
    ones_bf = sb.alloc([128, 128], BF16)
    persist_mark0 = sb.cur
    DTs = sb.alloc([128, NT, 32], F32)
    LAs = sb.alloc([128, NT, 32], F32)
    halo = sb.alloc([128, 32, 3], F32)
    ident_f = cst[:, 0:128]
    U_f = cst[:, 128:256]
    Ls_f = cst[:, 256:384]
    ones_f = cst[:, 384:512]
    Ge_f = cst[:, 512:640]
    S.dma(cst, consts_in, w=["cst"])
    S.op("dve", I("tensor_copy", out=ident_bf, in_=ident_f), r=["cst"], w=["ident"])
    S.op("dve", I("tensor_copy", out=mask2[:, 0, :], in_=Ge_f), r=["cst"], w=["mask2a"])
    S.op("dve", I("tensor_copy", out=mask2[:, 1, :], in_=U_f), r=["cst"], w=["mask2b"])
    S.op("dve", I("tensor_copy", out=ones_bf, in_=ones_f), r=["cst"], w=["onesbf"])
    persist_mark = sb.cur

    def bcast_load(dst, src_row, key):
        S.dma(dst, src_row.partition_broadcast(128), w=[key])

    def rstd_ops(ss, rstd, n, rkeys, wkey):
        S.op("dve", I("tensor_scalar", out=rstd, in0=ss, scalar1=1.0 / n, scalar2=EPS,
                      op0=ALU.mult, op1=ALU.add), r=rkeys, w=[wkey])
        S.op("act", I("activation", out=rstd, in_=rstd, func=AF.Ln), r=[wkey], w=[wkey])
        S.op("act", I("activation", out=rstd, in_=rstd, func=AF.Exp, scale=-0.5), r=[wkey], w=[wkey])

    C = dict(locals())
    for l in range(NL):
        x_src = x_in if l == 0 else xmid[l - 1]
        x_dst = out if l == NL - 1 else xmid[l]
        S.barrier()
        sb.cur = persist_mark
        gpre = sb.alloc([128, D], F32)
        convw = sb.alloc([128, 32, 4], F32)
        convb = sb.alloc([128, 32], F32)
        dtb = sb.alloc([128, 32], F32)
        abc = sb.alloc([128, 32], F32)
        bcast_load(gpre, P["norm_pre"][l:l + 1, :], "gpre")
        S.dma(convw, P["conv_w"][l].rearrange("p (b k) -> p b k", k=4), w=["convw"])
        S.dma(convb, P["conv_b"][l], w=["convb"])
        bcast_load(dtb, P["dt_bias"][l:l + 1, :], "dtb")
        bcast_load(abc, P["a_log"][l:l + 1, :], "abc0")
        S.op("act", I("activation", out=abc, in_=abc, func=AF.Exp), r=["abc0"], w=["abc0"])
        S.op("act", I("mul", out=abc, in_=abc, mul=-1.0), r=["abc0"], w=["abc"])
        S.op("pool", I("memset", ap=halo, constant=0.0), w=["halo"])
        projmark = sb.cur
        for hf in range(NHALF):
            sb.cur = projmark
            t0h = hf * HALF
            hT = sb.alloc([128, 8, HALF], BF16)
            nmark = sb.cur
            xin = [sb.alloc([128, D], F32) for _ in range(2)]
            xn = [sb.alloc([128, D], BF16) for _ in range(2)]
            junk = sb.alloc([128, D], BF16)
            ss = [sb.alloc([128, 1], F32) for _ in range(2)]
            rs = [sb.alloc([128, 1], F32) for _ in range(2)]
            for i in range(HALF // 128):
                s_ = i % 2
                tok = t0h + i * 128
                S.dma(xin[s_], x_src[tok:tok + 128, :], w=[("xin", s_)])
                S.op("act", I("activation", out=junk, in_=xin[s_], func=AF.Square,
                              accum_out=ss[s_][:, 0:1]),
                     r=[("xin", s_)], w=[("ss", s_), "junk"])
                rstd_ops(ss[s_], rs[s_], D, [("ss", s_)], ("rs", s_))
                S.op("dve", I("scalar_tensor_tensor", out=xn[s_], in0=xin[s_], scalar=rs[s_][:, 0:1],
                              in1=gpre, op0=ALU.mult, op1=ALU.mult),
                     r=[("xin", s_), ("rs", s_), "gpre"], w=[("xn", s_)])
                pb = 6 + s_
                S.op("pe", [I("transpose", out=bank_bf(pb)[:, k * 128:(k + 1) * 128],
                              in_=xn[s_][:, k * 128:(k + 1) * 128], identity=ident_bf)
                            for k in range(8)],
                     r=[("xn", s_), "ident"], w=[("bank", pb)])
                S.op("act", I("copy", out=hT[:, :, i * 128:(i + 1) * 128],
                              in_=bank_bf(pb).rearrange("p (k t) -> p k t", k=8)),
                     r=[("bank", pb)], w=[("hT", i // 4)])
            S.barrier()
            sb.cur = nmark
            C.update(locals())
            phaseP(C)
        S.barrier()
        sb.cur = persist_mark
        C.update(locals())
        for _ in phaseA(C):
            pass
        S.barrier()
        sb.cur = persist_mark
        for _ in phaseS(C):
            pass
        S.barrier()
        sb.cur = persist_mark
        phaseF(C)
    S.emit(nc)
    return nc


def phaseP(C):
    S, sb, l, hT = C["S"], C["sb"], C["l"], C["hT"]
    NTT, t0h, hf, NHALF, HALF = C["NTT"], C["t0h"], C["hf"], C["NHALF"], C["HALF"]
    bank, bank_bf, ident_bf = C["bank"], C["bank_bf"], C["ident_bf"]
    convw, convb, halo, dtb, abc = C["convw"], C["convb"], C["halo"], C["dtb"], C["abc"]
    DTs, LAs = C["DTs"], C["LAs"]
    W = C["P"]["w_in"][l]
    wst = [sb.alloc([128, 8, 512], F32) for _ in range(2)]
    wbf = [sb.alloc([128, 8, 512], BF16) for _ in range(2)]
    stage = [sb.alloc([128, 4, 512], BF16) for _ in range(2)]
    U = [[sb.alloc([128, 515], F32) for _ in range(2)] for _ in range(4)]
    acc = [sb.alloc([128, 512], F32) for _ in range(4)]
    xc = [sb.alloc([128, 4, 512], BF16) for _ in range(2)]
    stT = [sb.alloc([128, 4, 512], BF16) for _ in range(2)]
    wdt = sb.alloc([128, 8, 32], F32)
    wdtb = sb.alloc([128, 8, 32], BF16)
    dtmp = sb.alloc([128, 16, 32], F32)

    blocks = []
    for j in range(4):
        blocks.append((OFF_ZSSD + j * 512, "tm", C["Z1"], j * 512, AF.Silu))
    for j in range(8):
        blocks.append((OFF_XBC + j * 512, "xbc", None, j * 4, None))
    for g in range(3):
        for j in range(2):
            blocks.append((OFF_QKV + (0 * 3 + g) * 1024 + j * 512, "fm", C["QT"][g], j * 512, None))
        for j in range(2):
            blocks.append((OFF_QKV + (1 * 3 + g) * 1024 + j * 512, "fm", C["KT"][g], j * 512, None))
        for j in range(2):
            blocks.append((OFF_QKV + (2 * 3 + g) * 1024 + j * 512, "tm", C["V"][g], j * 512, AF.Copy))
    for j in range(2):
        blocks.append((OFF_ZATT + j * 512, "tm", C["ZA"], j * 512, AF.Silu))
    for j in range(2):
        blocks.append((OFF_QMEM + j * 512, "fm", C["QMT"], j * 512, None))
    for j in range(2):
        blocks.append((OFF_ZMEM + j * 512, "tm", C["ZM"], j * 512, AF.Silu))
    for j in range(6):
        blocks.append((OFF_GATE + j * 512, "tm", C["G"], j * 512, AF.Sigmoid))

    S.dma(wdt, W[:, OFF_DT:OFF_DT + 32].rearrange("(kc p) n -> p kc n", p=128), w=["wdt"])
    S.op("dve", I("tensor_copy", out=wdtb, in_=wdt), r=["wdt"], w=["wdtb"])
    ntile = HALF // 128
    grp = min(16, ntile)
    for g0 in range(0, ntile, grp):
        bk = 0
        for i in range(g0, g0 + grp):
            S.op("pe", [I("matmul", out=bank(bk)[:, (i - g0) * 32:(i - g0) * 32 + 32],
                          lhsT=hT[:, kc, i * 128:(i + 1) * 128], rhs=wdtb[:, kc, :],
                          start=(kc == 0), stop=(kc == 7)) for kc in range(8)],
                 r=[("hT", i // 4), "wdtb"], w=[("bank", bk)])
        c0 = t0h // 128 + g0
        S.op("dve", I("tensor_tensor", out=dtmp[:, 0:grp, :],
                      in0=bank(bk)[:, 0:grp * 32].rearrange("p (n e) -> p n e", e=32),
                      in1=dtb.unsqueeze(1).to_broadcast([128, grp, 32]), op=ALU.add),
             r=[("bank", bk), "dtb"], w=["dtmp"])
        S.op("act", I("activation", out=dtmp[:, 0:grp, :], in_=dtmp[:, 0:grp, :], func=AF.Exp),
             r=["dtmp"], w=["dtmp"])
        S.op("act", I("activation", out=DTs[:, c0:c0 + grp, :], in_=dtmp[:, 0:grp, :], func=AF.Ln,
                      bias=1.0, scale=1.0),
             r=["dtmp"], w=[("DTs", c0)])
        S.op("dve", I("tensor_tensor", out=LAs[:, c0:c0 + grp, :], in0=DTs[:, c0:c0 + grp, :],
                      in1=abc.unsqueeze(1).to_broadcast([128, grp, 32]), op=ALU.mult),
             r=[("DTs", c0), "abc"], w=[("LAs", c0)])

    def load_w(bi):
        c0 = blocks[bi][0]
        sl = bi % 2
        S.dma(wst[sl], W[:, c0:c0 + 512].rearrange("(kc p) n -> p kc n", p=128), w=[("wst", sl)])
        S.op("pool", I("tensor_copy", out=wbf[sl], in_=wst[sl]), r=[("wst", sl)], w=[("wbf", sl)])

    load_w(0)
    bkrr = [1]
    cnt = [0]
    for bi, (c0, kind, dst, d0, func) in enumerate(blocks):
        if bi + 1 < len(blocks):
            load_w(bi + 1)
        sl = bi % 2
        for tt in range(NTT):
            tok0 = t0h + tt * 512
            st = cnt[0] % 2
            cnt[0] += 1
            for sub in range(4):
                bk = bkrr[0]
                bkrr[0] = bkrr[0] % 5 + 1
                if kind == "tm":
                    mm = [I("matmul", out=bank(bk), lhsT=hT[:, kc, tt * 512 + sub * 128:tt * 512 + sub * 128 + 128],
                            rhs=wbf[sl][:, kc, :], start=(kc == 0), stop=(kc == 7)) for kc in range(8)]
                else:
                    mm = [I("matmul", out=bank(bk), lhsT=wbf[sl][:, kc, sub * 128:(sub + 1) * 128],
                            rhs=hT[:, kc, tt * 512:(tt + 1) * 512], start=(kc == 0), stop=(kc == 7))
                          for kc in range(8)]
                S.op("pe", mm, r=[("wbf", sl), ("hT", tt)], w=[("bank", bk)])
                if kind == "tm":
                    S.op("act", I("activation", out=stage[st][:, sub, :], in_=bank(bk), func=func),
                         r=[("bank", bk)], w=[("stage", st, sub)])
                elif kind == "fm":
                    eng = "act" if sub % 2 == 0 else "dve"
                    S.op(eng, I("copy" if eng == "act" else "tensor_copy", out=stage[st][:, sub, :], in_=bank(bk)),
                         r=[("bank", bk)], w=[("stage", st, sub)])
                else:
                    chb = d0 + sub
                    par = tt % 2
                    Uc, Up = U[sub][par], U[sub][1 - par]
                    S.op("act", I("copy", out=Uc[:, 3:515], in_=bank(bk)),
                         r=[("bank", bk)], w=[("U", sub, par, "m")])
                    S.op("act", I("activation", out=acc[sub], in_=bank(bk), func=AF.Identity,
                                  scale=convw[:, chb, 3:4], bias=convb[:, chb:chb + 1]),
                         r=[("bank", bk), "convw", "convb"], w=[("acc", sub)])
                    if tt == 0:
                        S.op("pool", I("tensor_copy", out=Uc[:, 0:3], in_=halo[:, chb, :]),
                             r=[("halo", chb)], w=[("U", sub, par, "h")])
                    else:
                        S.op("pool", I("tensor_copy", out=Uc[:, 0:3], in_=Up[:, 512:515]),
                             r=[("U", sub, 1 - par, "m")], w=[("U", sub, par, "h")])
                    if tt == NTT - 1 and hf + 1 < NHALF:
                        S.op("pool", I("tensor_copy", out=halo[:, chb, :], in_=Uc[:, 512:515]),
                             r=[("U", sub, par, "m")], w=[("halo", chb)])
            if kind == "xbc":
                par = tt % 2
                for kk in (2, 1, 0):
                    for sub in range(4):
                        chb = d0 + sub
                        Uc = U[sub][par]
                        ukeys = [("U", sub, par, "m"), ("U", sub, par, "h"), "convw", "convb"]
                        if kk == 3:
                            S.op("dve", I("tensor_scalar", out=acc[sub], in0=Uc[:, 3:515], scalar1=convw[:, chb, 3:4],
                                          scalar2=convb[:, chb:chb + 1], op0=ALU.mult, op1=ALU.add),
                                 r=ukeys, w=[("acc", sub)])
                        else:
                            S.op("dve", I("scalar_tensor_tensor", out=acc[sub], in0=Uc[:, kk:kk + 512],
                                          scalar=convw[:, chb, kk:kk + 1], in1=acc[sub], op0=ALU.mult, op1=ALU.add),
                                 r=ukeys + [("acc", sub)], w=[("acc", sub)])
                for sub in range(4):
                    S.op("act", I("activation", out=xc[st][:, sub, :], in_=acc[sub], func=AF.Silu),
                         r=[("acc", sub)], w=[("xc", st, sub)])
                if d0 < 24:
                    for sub in range(4):
                        tb = 6 + (sub % 2)
                        S.op("pe", [I("transpose", out=bank_bf(tb)[:, j * 128:(j + 1) * 128],
                                      in_=xc[st][:, sub, j * 128:(j + 1) * 128], identity=ident_bf)
                                    for j in range(4)],
                             r=[("xc", st, sub), "ident"], w=[("bank", tb)])
                        eng = "act" if sub % 2 == 0 else "pool_no"
                        eng = "act" if sub % 2 == 0 else "dve"
                        S.op(eng, I("copy" if eng == "act" else "tensor_copy",
                                    out=stT[st][:, :, sub * 128:(sub + 1) * 128],
                                    in_=bank_bf(tb)[:, 0:512].rearrange("p (j c) -> p j c", j=4)),
                             r=[("bank", tb)], w=[("stT", st, sub)])
            if kind == "tm":
                S.dma(dst[tok0:tok0 + 512, d0:d0 + 512].rearrange("(s p) c -> p s c", p=128), stage[st],
                      r=[("stage", st, s_) for s_ in range(4)])
            elif kind == "fm":
                S.dma(dst[d0:d0 + 512, tok0:tok0 + 512].rearrange("(s p) t -> p s t", p=128), stage[st],
                      r=[("stage", st, s_) for s_ in range(4)])
            else:
                chb0 = d0
                if chb0 < 16:
                    S.dma(C["XS"][tok0:tok0 + 512, chb0 * 128:chb0 * 128 + 512].rearrange("(j p) c -> p j c", p=128),
                          stT[st], r=[("stT", st, s_) for s_ in range(4)])
                elif chb0 < 24:
                    cc = (chb0 - 16) * 128
                    S.dma(C["BTOK"][tok0:tok0 + 512, cc:cc + 512].rearrange("(j p) c -> p j c", p=128),
                          stT[st], r=[("stT", st, s_) for s_ in range(4)])
                if chb0 >= 16:
                    r0 = (chb0 - 16) * 128
                    S.dma(C["BCT"][r0:r0 + 512, tok0:tok0 + 512].rearrange("(s p) t -> p s t", p=128),
                          xc[st], r=[("xc", st, s_) for s_ in range(4)])


def phaseA(C):
    S, sb, T = C["S"], C["sb"], C["T"]
    bank, PS, mask2, ones_bf = C["bank"], C["PS"], C["mask2"], C["ones_bf"]
    NBLK = T // 128
    q2 = [sb.alloc([128, T], BF16) for _ in range(2)]
    k2 = [sb.alloc([128, T], BF16) for _ in range(2)]
    v2 = [sb.alloc([128, NBLK, 128], BF16) for _ in range(2)]
    PT = [sb.alloc([128, 2, 2, 2, 128], BF16) for _ in range(2)]
    ost = [sb.alloc([128, 8, 130], F32) for _ in range(2)]
    mask4 = sb.alloc([128, 2, 256], BF16)
    for j in range(2):
        S.op("pool", I("tensor_copy", out=mask4[:, j, :], in_=mask2.rearrange("p t q -> p (t q)")),
             r=["mask2a", "mask2b"], w=[("mask4", j)])
    mask4f = mask4.rearrange("p j c -> p (j c)")
    pairs = [(g, hp) for g in range(3) for hp in range(8)]
    units = []
    for pi, (g, hp) in enumerate(pairs):
        d = DIL[g]
        nb = T // d // 128
        first = True
        for r in range(d):
            for b in range(0, nb, 2):
                units.append(dict(pi=pi, g=g, hp=hp, sl=pi % 2, d=d, nb=nb, OB=min(nb, 8), r=r, b=b, first=first))
                first = False
    state = dict(ostc=0)

    def loads(pi):
        g, hp = pairs[pi]
        d = DIL[g]
        nb = T // d // 128
        sl = pi % 2
        rows = slice(hp * 128, (hp + 1) * 128)
        S.dma(q2[sl], C["QT"][g][rows, :], w=[("q2", sl)])
        S.dma(k2[sl], C["KT"][g][rows, :], w=[("k2", sl)])
        vsrc = C["V"][g][:, rows].rearrange("(b i r) c -> r i b c", i=128, r=d)
        for r in range(d):
            for c0 in range(0, nb, 8):
                c1 = min(nb, c0 + 8)
                S.dma(v2[sl][:, r * nb + c0:r * nb + c1, :], vsrc[r][:, c0:c1, :], w=[("v2", sl, r, c0)])

    def front(i):
        u = units[i]
        sl, d, r, b = u["sl"], u["d"], u["r"], u["b"]
        x = i % 2
        qS = q2[sl].rearrange("p (m r) -> p r m", r=d)
        kS = k2[sl].rearrange("p (m r) -> p r m", r=d)
        mm = []
        for hh in range(2):
            pr = slice(hh * 64, (hh + 1) * 64)
            bk = bank(2 * x + hh)
            for j in range(2):
                bb = b + j
                qb = qS[pr, r, bb * 128:(bb + 1) * 128]
                if bb > 0:
                    mm.append(I("matmul", out=bk[:, j * 256:j * 256 + 128],
                                lhsT=kS[pr, r, (bb - 1) * 128:bb * 128], rhs=qb, start=True, stop=True))
                mm.append(I("matmul", out=bk[:, j * 256 + 128:j * 256 + 256],
                            lhsT=kS[pr, r, bb * 128:(bb + 1) * 128], rhs=qb, start=True, stop=True))
        bkeys = [("bank", 2 * x), ("bank", 2 * x + 1)]
        S.op("pe", mm, r=[("q2", sl), ("k2", sl)], w=bkeys)
        lo = 0 if b > 0 else 128
        pin = PS[x].rearrange("p (h c) -> p h c", h=2)[:, :, lo:512]
        pout = PT[x].rearrange("p h j t q -> p h (j t q)")[:, :, lo:512]
        S.op("act", I("activation", out=pout, in_=pin, func=AF.Exp, scale=0.125), r=bkeys, w=[("PT", x)])
        mk = mask4f[:, lo:512].unsqueeze(1).to_broadcast([128, 2, 512 - lo])
        S.op("dve" if i % 2 == 0 else "pool", I("tensor_tensor", out=pout, in0=pout, in1=mk, op=ALU.mult),
             r=[("PT", x), ("mask4", 0), ("mask4", 1)], w=[("PT", x)])

    def back(i):
        u = units[i]
        g, hp, sl, d, nb, OB, r, b = u["g"], u["hp"], u["sl"], u["d"], u["nb"], u["OB"], u["r"], u["b"]
        x = i % 2
        ob = i % 4
        mm = []
        for hh in range(2):
            hc = slice(hh * 64, (hh + 1) * 64)
            for j in range(2):
                bb = b + j
                blk = r * nb + bb
                o_ = bank(4 + ob)[:, j * 130 + hh * 65:j * 130 + hh * 65 + 64]
                dn = bank(4 + ob)[:, j * 130 + hh * 65 + 64:j * 130 + hh * 65 + 65]
                if bb > 0:
                    mm.append(I("matmul", out=o_, lhsT=PT[x][:, hh, j, 0, :], rhs=v2[sl][:, blk - 1, hc],
                                start=True, stop=False))
                mm.append(I("matmul", out=o_, lhsT=PT[x][:, hh, j, 1, :], rhs=v2[sl][:, blk, hc],
                            start=(bb == 0), stop=True))
                if bb > 0:
                    mm.append(I("matmul", out=dn, lhsT=PT[x][:, hh, j, 0, :], rhs=ones_bf[:, 0:1],
                                start=True, stop=False))
                mm.append(I("matmul", out=dn, lhsT=PT[x][:, hh, j, 1, :], rhs=ones_bf[:, 0:1],
                            start=(bb == 0), stop=True))
        vkeys = [("v2", sl, r, c0) for c0 in range(0, nb, 8)]
        S.op("pe", mm, r=[("PT", x), "onesbf"] + vkeys, w=[("bank", 4 + ob)])
        os_ = state["ostc"] % 2
        eng = "act" if i % 2 == 0 else "dve"
        S.op(eng, I("copy" if eng == "act" else "tensor_copy",
                    out=ost[os_][:, b % OB:b % OB + 2, :].rearrange("p j c -> p (j c)"),
                    in_=bank(4 + ob)[:, 0:260]),
             r=[("bank", 4 + ob)], w=[("ost", os_, b % OB)])
        if (b + 1) % OB == OB - 1:
            b0 = b + 1 - (OB - 1)
            odst = C["OA"][g][:, hp * 130:(hp + 1) * 130].rearrange("(b i r) c -> r i b c", i=128, r=d)
            S.dma(odst[r][:, b0:b0 + OB, :], ost[os_][:, 0:OB, :], r=[("ost", os_, j) for j in range(0, OB, 2)])
            state["ostc"] += 1

    n = len(units)
    loads(0)
    front(0)
    for i in range(n):
        if i + 1 < n:
            front(i + 1)
        back(i)
        if units[i]["first"] and units[i]["pi"] + 1 < len(pairs):
            loads(units[i]["pi"] + 1)
    yield


def phaseS(C):
    S, sb, T, l, NT = C["S"], C["sb"], C["T"], C["l"], C["NT"]
    bank, LAs, DTs = C["bank"], C["LAs"], C["DTs"]
    U_f, Ls_f, ones_f = C["U_f"], C["Ls_f"], C["ones_f"]
    H = sb.alloc([128, 2048], F32)
    Hbf = sb.alloc([128, 2048], BF16)
    dbc = sb.alloc([128, 32], F32)
    xs_t = [sb.alloc([128, 2048], BF16) for _ in range(2)]
    b_t = [sb.alloc([128, 1024], BF16) for _ in range(2)]
    bc4 = [sb.alloc([128, 16, 256], BF16) for _ in range(2)]
    ex = [sb.alloc([128, 96], F32) for _ in range(2)]
    xds = [sb.alloc([128, 32, 64], BF16) for _ in range(2)]
    xdt = [sb.alloc([128, 32, 64], BF16) for _ in range(2)]
    dx = [sb.alloc([128, 32, 64], BF16) for _ in range(2)]
    ybf = [sb.alloc([128, 2048], BF16) for _ in range(2)]
    cbm = [sb.alloc([128, 128], F32) for _ in range(2)]
    lseg = [sb.alloc([128, 4, 128], F32) for _ in range(2)]
    dec = [sb.alloc([128, 4, 128], F32) for _ in range(2)]
    MT = [sb.alloc([128, 4, 128], BF16) for _ in range(2)]
    tt_ = [sb.alloc([128, 4, 64], F32) for _ in range(2)]
    S.dma(dbc, C["P"]["d_skip"][l:l + 1, :].partition_broadcast(128), w=["dbc"])
    S.op("pool", I("memset", ap=H, constant=0.0), w=[("H", g) for g in range(8)])
    S.op("pool", I("memset", ap=Hbf, constant=0.0), w=[("Hbf", g) for g in range(8)])

    def loads(c):
        s = c % 2
        S.dma(xs_t[s], C["XS"][c * 128:(c + 1) * 128, :], w=[("xs_t", s)])
        S.dma(b_t[s], C["BTOK"][c * 128:(c + 1) * 128, :], w=[("b_t", s)])
        if c % 2 == 0:
            s4 = (c // 2) % 2
            S.dma(bc4[s4], C["BCT"][:, c * 128:c * 128 + 256].rearrange("(j p) t -> p j t", p=128),
                  w=[("bc4", s4)])

    def pre(c):
        s = c % 2
        la = LAs[:, c, :]
        S.op("pe", [I("matmul", out=bank(0)[:, 0:32], lhsT=U_f, rhs=la, start=True, stop=True),
                    I("matmul", out=bank(0)[:, 32:64], lhsT=Ls_f, rhs=la, start=True, stop=True),
                    I("matmul", out=bank(0)[:, 64:96], lhsT=ones_f, rhs=la, start=True, stop=True)],
             r=["cst", ("LAs", c)], w=[("bank", 0)])
        S.op("act", I("activation", out=ex[s], in_=bank(0)[:, 0:96], func=AF.Exp),
             r=[("bank", 0)], w=[("ex", s)])
        S.op("pool", I("tensor_tensor", out=xdt[s], in0=xs_t[s].rearrange("p (h e) -> p h e", e=64),
                       in1=DTs[:, c, :].unsqueeze(2).to_broadcast([128, 32, 64]), op=ALU.mult),
             r=[("xs_t", s), ("DTs", c)], w=[("xdt", s)])
        S.op("dve", I("tensor_tensor", out=xds[s], in0=xdt[s],
                      in1=ex[s][:, 32:64].unsqueeze(2).to_broadcast([128, 32, 64]), op=ALU.mult),
             r=[("xdt", s), ("ex", s)], w=[("xds", s)])
        S.op("pool", I("tensor_tensor", out=dx[s], in0=xs_t[s].rearrange("p (h e) -> p h e", e=64),
                       in1=dbc.unsqueeze(2).to_broadcast([128, 32, 64]), op=ALU.mult),
             r=[("xs_t", s), "dbc"], w=[("dx", s)])

    def front(c, g):
        x = (c * 8 + g) % 2
        s4 = (c // 2) % 2
        la = LAs[:, c, :]
        tk = slice((c % 2) * 128, (c % 2 + 1) * 128)
        BT = bc4[s4][:, g, tk]
        CT = bc4[s4][:, 8 + g, tk]
        hs = slice(g * 4, (g + 1) * 4)
        cbp = bank(1)[:, 0:128]
        S.op("pe", I("matmul", out=cbp, lhsT=BT, rhs=CT, start=True, stop=True),
             r=[("bc4", s4)], w=[("bank", 1)])
        S.op("dve", I("tensor_tensor", out=cbm[x], in0=cbp, in1=U_f, op=ALU.mult),
             r=[("bank", 1), "cst"], w=[("cbm", x)])
        S.op("pool", I("tensor_tensor", out=lseg[x], in0=Ls_f.unsqueeze(1).to_broadcast([128, 4, 128]),
                       in1=la[:, hs].unsqueeze(2).to_broadcast([128, 4, 128]), op=ALU.mult),
             r=["cst", ("LAs", c)], w=[("lseg", x)])
        S.op("pe", [I("matmul", out=bank(2 + x)[:, e * 128:(e + 1) * 128], lhsT=lseg[x][:, e, :], rhs=U_f,
                      start=True, stop=True) for e in range(4)],
             r=[("lseg", x), "cst"], w=[("bank", 2 + x)])
        S.op("act", I("activation", out=dec[x], in_=bank(2 + x).rearrange("p (e l) -> p e l", e=4),
                      func=AF.Exp),
             r=[("bank", 2 + x)], w=[("dec", x)])
        S.op("dve", I("tensor_tensor", out=MT[x], in0=dec[x],
                      in1=cbm[x].unsqueeze(1).to_broadcast([128, 4, 128]), op=ALU.mult),
             r=[("dec", x), ("cbm", x)], w=[("MT", x)])

    def back(c, g):
        x = (c * 8 + g) % 2
        s = c % 2
        s4 = (c // 2) % 2
        tk = slice((c % 2) * 128, (c % 2 + 1) * 128)
        CT = bc4[s4][:, 8 + g, tk]
        hs = slice(g * 4, (g + 1) * 4)
        cs = slice(g * 256, (g + 1) * 256)
        mm = [I("matmul", out=bank(4 + x)[:, e * 64:(e + 1) * 64], lhsT=MT[x][:, e, :],
                rhs=xdt[s][:, g * 4 + e, :], start=True, stop=True)
              for e in range(4)]
        mm.append(I("matmul", out=bank(4 + x)[:, 256:512], lhsT=CT, rhs=Hbf[:, cs], start=True, stop=True))
        S.op("pe", mm, r=[("MT", x), ("xdt", s), ("bc4", s4), ("Hbf", g)], w=[("bank", 4 + x)])
        S.op("dve", I("tensor_tensor", out=tt_[x],
                      in0=bank(4 + x)[:, 256:512].rearrange("p (e q) -> p e q", e=4),
                      in1=ex[s][:, hs].unsqueeze(2).to_broadcast([128, 4, 64]), op=ALU.mult),
             r=[("bank", 4 + x), ("ex", s)], w=[("tt", x)])
        S.op("pool", I("tensor_tensor", out=tt_[x], in0=tt_[x], in1=dx[s][:, hs, :], op=ALU.add),
             r=[("tt", x), ("dx", s)], w=[("tt", x)])
        S.op("dve", I("tensor_tensor", out=ybf[s][:, cs].rearrange("p (e q) -> p e q", e=4),
                      in0=bank(4 + x)[:, 0:256].rearrange("p (e q) -> p e q", e=4), in1=tt_[x], op=ALU.add),
             r=[("bank", 4 + x), ("tt", x)], w=[("ybf", s, g)])
        php = bank(6 + x)[:, 0:256]
        S.op("pe", I("matmul", out=php, lhsT=b_t[s][:, g * 128:(g + 1) * 128],
                     rhs=xds[s][:, hs, :].rearrange("p h e -> p (h e)"), start=True, stop=True),
             r=[("b_t", s), ("xds", s)], w=[("bank", 6 + x)])
        Hg = H[:, cs].rearrange("p (e q) -> p e q", e=4)
        S.op("pool", I("tensor_tensor", out=Hg, in0=Hg,
                       in1=ex[s][:, 64 + g * 4:64 + (g + 1) * 4].unsqueeze(2).to_broadcast([128, 4, 64]),
                       op=ALU.mult),
             r=[("H", g), ("ex", s)], w=[("H", g)])
        S.op("dve", I("tensor_tensor", out=H[:, cs], in0=H[:, cs], in1=php, op=ALU.add),
             r=[("H", g), ("bank", 6 + x)], w=[("H", g)])
        S.op("act", I("copy", out=Hbf[:, cs], in_=H[:, cs]), r=[("H", g)], w=[("Hbf", g)])

    loads(0)
    if NT > 1:
        loads(1)
    pre(0)
    front(0, 0)
    for c in range(NT):
        for g in range(8):
            if g < 7:
                front(c, g + 1)
            elif c + 1 < NT:
                pre(c + 1)
                front(c + 1, 0)
            back(c, g)
        S.dma(C["YS"][c * 128:(c + 1) * 128, :], ybf[c % 2], r=[("ybf", c % 2, g) for g in range(8)])
        if c + 2 < NT:
            loads(c + 2)
    yield


def phaseF(C):
    S, sb, T, l, NT = C["S"], C["sb"], C["T"], C["l"], C["NT"]
    bank, bank_bf, PS, ident_bf, rstd_ops = C["bank"], C["bank_bf"], C["PS"], C["ident_bf"], C["rstd_ops"]
    P = C["P"]
    x_src, x_dst = C["x_src"], C["x_dst"]
    sb.cur = C["persist_mark0"]
    wso = sb.alloc([128, 16, 1024], BF16)
    wao = sb.alloc([128, 8, 1024], BF16)
    wmo = sb.alloc([128, 8, 1024], BF16)
    wo = sb.alloc([128, 8, 1024], BF16)
    ssdn = sb.alloc([128, 2048], F32)
    npost = sb.alloc([128, 1024], F32)
    KmT = sb.alloc([128, 8, 256], BF16)
    Vm1 = sb.alloc([128, 2, 4, 257], BF16)
    fmark = sb.cur
    wtmp = [sb.alloc([128, 4, 1024], F32) for _ in range(2)]
    wi = 0
    for (dstw, src, nkc) in ((wso, P["w_ssd_out"][l], 16), (wao, P["w_attn_out"][l], 8),
                             (wmo, P["w_mem_out"][l], 8), (wo, P["w_out"][l], 8)):
        srcv = src.rearrange("(kc p) n -> p kc n", p=128)
        for k0 in range(0, nkc, 4):
            s = wi % 2
            wi += 1
            S.dma(wtmp[s], srcv[:, k0:k0 + 4, :], w=[("wtmp", s)])
            S.op("pool" if wi % 2 else "dve", I("tensor_copy", out=dstw[:, k0:k0 + 4, :], in_=wtmp[s]),
                 r=[("wtmp", s)], w=[("wres", wi)])
    S.dma(ssdn, P["ssd_norm"][l:l + 1, :].partition_broadcast(128), w=["ssdn"])
    S.dma(npost, P["norm_post"][l:l + 1, :].partition_broadcast(128), w=["npost"])
    mnorm = sb.alloc([128, 1024], F32)
    S.dma(mnorm, P["mem_norm"][l:l + 1, :].partition_broadcast(128), w=["mnorm"])
    memT = sb.alloc([128, 8, 256], BF16)
    mx = sb.alloc([128, 1024], F32)
    mxn = sb.alloc([128, 1024], BF16)
    mss = sb.alloc([128, 1], F32)
    mrs = sb.alloc([128, 1], F32)
    for mb in range(2):
        S.dma(mx, C["mem_in"][mb * 128:(mb + 1) * 128, :], w=["mx"])
        S.op("act", I("activation", out=mxn, in_=mx, func=AF.Square, accum_out=mss[:, 0:1]),
             r=["mx"], w=["mss", "mxn"])
        rstd_ops(mss, mrs, D, ["mss"], "mrs")
        S.op("dve", I("scalar_tensor_tensor", out=mxn, in0=mx, scalar=mrs[:, 0:1], in1=mnorm,
                      op0=ALU.mult, op1=ALU.mult), r=["mx", "mrs", "mnorm"], w=["mxn"])
        S.op("pe", [I("transpose", out=bank_bf(4)[:, k * 128:(k + 1) * 128], in_=mxn[:, k * 128:(k + 1) * 128],
                      identity=ident_bf) for k in range(8)], r=["mxn", "ident"], w=[("bank", 4)])
        S.op("act", I("copy", out=memT[:, :, mb * 128:(mb + 1) * 128],
                      in_=bank_bf(4).rearrange("p (k t) -> p k t", k=8)), r=[("bank", 4)], w=[("memT", mb)])
    wkv = sb.alloc([128, 8, 512], BF16)
    wkvf = sb.alloc([128, 8, 512], F32)
    S.op("pool", I("memset", ap=Vm1, constant=1.0), w=["Vm1"])
    for cb in range(4):
        S.dma(wkvf, P["w_mem_kv"][l][:, cb * 512:(cb + 1) * 512].rearrange("(kc p) n -> p kc n", p=128),
              w=["wkvf"])
        S.op("dve", I("tensor_copy", out=wkv, in_=wkvf), r=["wkvf"], w=["wkv"])
        if cb < 2:
            for sub in range(4):
                j = cb * 4 + sub
                S.op("pe", [I("matmul", out=bank(5)[:, 0:256], lhsT=wkv[:, kc, sub * 128:(sub + 1) * 128],
                              rhs=memT[:, kc, :], start=(kc == 0), stop=(kc == 7)) for kc in range(8)],
                     r=["wkv", ("memT", 0), ("memT", 1)], w=[("bank", 5)])
                S.op("act", I("copy", out=KmT[:, j, :], in_=bank(5)[:, 0:256]), r=[("bank", 5)], w=[("KmT", j)])
        else:
            for mb in range(2):
                S.op("pe", [I("matmul", out=bank(5), lhsT=memT[:, kc, mb * 128:(mb + 1) * 128],
                              rhs=wkv[:, kc, :], start=(kc == 0), stop=(kc == 7)) for kc in range(8)],
                     r=["wkv", ("memT", 0), ("memT", 1)], w=[("bank", 5)])
                h0 = (cb - 2) * 2
                S.op("act", I("copy", out=Vm1[:, mb, h0:h0 + 2, 0:256],
                              in_=bank(5).rearrange("p (h e) -> p h e", h=2)),
                     r=[("bank", 5), "Vm1"], w=[("Vm1w", cb, mb)])
    S.barrier()
    sb.cur = fmark
    ys = [sb.alloc([128, 2048], BF16) for _ in range(2)]
    z1 = [sb.alloc([128, 2048], BF16) for _ in range(2)]
    oa = sb.alloc([128, 3, 1040], F32)
    za = [sb.alloc([128, 1024], BF16) for _ in range(2)]
    qmt = [sb.alloc([128, 8, 128], BF16) for _ in range(2)]
    zm = [sb.alloc([128, 1024], BF16) for _ in range(2)]
    gt = [sb.alloc([128, 3072], BF16) for _ in range(2)]
    xr = [sb.alloc([128, 1024], F32) for _ in range(2)]
    t1 = sb.alloc([128, 2048], F32)
    abfA = sb.alloc([128, 1024], BF16)
    abfS = sb.alloc([128, 2048], BF16)
    abfM = sb.alloc([128, 1024], BF16)
    actTA = sb.alloc([128, 8, 128], BF16)
    actTS = sb.alloc([128, 16, 128], BF16)
    actTM = sb.alloc([128, 8, 128], BF16)
    merged = sb.alloc([128, 1024], F32)
    num = sb.alloc([128, 1040], F32)
    ob_ = sb.alloc([128, 1024], F32)
    tmp = sb.alloc([128, 1024], F32)
    PmT = [sb.alloc([128, 4, 128], BF16) for _ in range(2)]
    ssg = sb.alloc([128, 8], F32)
    rs8 = sb.alloc([128, 8], F32)
    rden = sb.alloc([128, 16], F32)
    rdm = sb.alloc([128, 4], F32)
    ssf = sb.alloc([128, 1], F32)
    rsf = sb.alloc([128, 1], F32)
    om = t1[:, 0:1024]
    xo = t1[:, 1024:2048]

    def loads(i):
        s = i % 2
        tk = slice(i * 128, (i + 1) * 128)
        S.dma(ys[s], C["YS"][tk, :], w=[("ys", s)])
        S.dma(z1[s], C["Z1"][tk, :], w=[("z1", s)])
        S.dma(za[s], C["ZA"][tk, :], w=[("za", s)])
        S.dma(qmt[s], C["QMT"][:, tk].rearrange("(j p) t -> p j t", p=128), w=[("qmt", s)])
        S.dma(zm[s], C["ZM"][tk, :], w=[("zm", s)])
        S.dma(gt[s], C["G"][tk, :], w=[("gt", s)])
        S.dma(xr[s], x_src[tk, :], w=[("xr", s)])

    def transposes(src, skey, dstT, dkey, nk, banks):
        for b0 in range(0, nk, 8):
            bk = banks[b0 // 8]
            S.op("pe", [I("transpose", out=bank_bf(bk)[:, k * 128:(k + 1) * 128],
                          in_=src[:, (b0 + k) * 128:(b0 + k + 1) * 128], identity=ident_bf) for k in range(8)],
                 r=[skey, "ident"], w=[("bank", bk)])
            S.op("act", I("copy", out=dstT[:, b0:b0 + 8, :], in_=bank_bf(bk).rearrange("p (k t) -> p k t", k=8)),
                 r=[("bank", bk)], w=[(dkey, b0 // 8)])

    def outproj(wt, srcT, skey, nk, pso, okey):
        for hh in range(2):
            S.op("pe", [I("matmul", out=pso[:, hh * 512:(hh + 1) * 512], lhsT=srcT[:, kc, :],
                          rhs=wt[:, kc, hh * 512:(hh + 1) * 512], start=(kc == 0), stop=(kc == nk - 1))
                        for kc in range(nk)],
                 r=[(skey, j) for j in range((nk + 7) // 8)], w=[("bank", okey + hh)])

    loads(0)
    for i in range(NT):
        s = i % 2
        tk = slice(i * 128, (i + 1) * 128)
        for g in range(3):
            S.dma(oa[:, g, :], C["OA"][g][tk, :], w=[("oa", g)])
        if i + 1 < NT:
            loads(i + 1)
        S.op("pool", I("tensor_tensor", out=num, in0=oa[:, 0, :], in1=oa[:, 1, :], op=ALU.add),
             r=[("oa", 0), ("oa", 1)], w=["num"])
        S.op("pool", I("tensor_tensor", out=num, in0=num, in1=oa[:, 2, :], op=ALU.add),
             r=[("oa", 2), "num"], w=["num"])
        numv = num.rearrange("p (h e) -> p h e", e=65)
        S.op("dve", I("tensor_tensor", out=t1, in0=ys[s], in1=z1[s], op=ALU.mult),
             r=[("ys", s), ("z1", s)], w=["t1a", "t1b"])
        S.op("act", [I("activation", out=abfS[:, 0:256], in_=t1[:, g * 256:(g + 1) * 256], func=AF.Square,
                       accum_out=ssg[:, g:g + 1]) for g in range(8)], r=["t1a", "t1b"], w=["ssg", "abfS"])
        S.op("dve", I("reciprocal", out=rden, in_=numv[:, :, 64]), r=["num"], w=["rden"])
        rstd_ops(ssg, rs8, 256, ["ssg"], "rs8")
        S.op("dve", I("tensor_tensor", out=ob_.rearrange("p (h e) -> p h e", e=64), in0=numv[:, :, 0:64],
                      in1=rden.unsqueeze(2).to_broadcast([128, 16, 64]), op=ALU.mult),
             r=["num", "rden"], w=["ob"])
        S.op("pool", I("tensor_tensor", out=abfA, in0=ob_, in1=za[s], op=ALU.mult),
             r=["ob", ("za", s)], w=["abfA"])
        transposes(abfA, "abfA", actTA, "actTA", 8, [4])
        for hp in range(2):
            mm = []
            for hh in range(2):
                h = hp * 2 + hh
                for mb in range(2):
                    for ec in range(2):
                        mm.append(I("matmul", out=bank(5)[:, (hh * 2 + mb) * 128:(hh * 2 + mb + 1) * 128],
                                    lhsT=KmT[:, h * 2 + ec, mb * 128:(mb + 1) * 128], rhs=qmt[s][:, h * 2 + ec, :],
                                    start=(ec == 0), stop=(ec == 1)))
            S.op("pe", mm, r=[("qmt", s)], w=[("bank", 5)])
            S.op("act", I("activation", out=PmT[hp], in_=bank(5).rearrange("p (j t) -> p j t", j=4),
                          func=AF.Exp, scale=1.0 / 16.0), r=[("bank", 5)], w=[("PmT", hp)])
        outproj(wao, actTA, "actTA", 8, PS[3], 6)
        S.op("dve", I("tensor_tensor", out=t1.rearrange("p (g e) -> p g e", g=8),
                      in0=t1.rearrange("p (g e) -> p g e", g=8),
                      in1=rs8.unsqueeze(2).to_broadcast([128, 8, 256]), op=ALU.mult),
             r=["t1a", "t1b", "rs8"], w=["t1a", "t1b"])
        S.op("pool", I("tensor_tensor", out=abfS, in0=t1, in1=ssdn, op=ALU.mult),
             r=["t1a", "t1b", "ssdn"], w=["abfS"])
        S.op("dve", I("tensor_tensor", out=merged, in0=PS[3], in1=gt[s][:, 1024:2048], op=ALU.mult),
             r=[("bank", 6), ("bank", 7), ("gt", s)], w=["merged"])
        transposes(abfS, "abfS", actTS, "actTS", 16, [0, 1])
        outproj(wso, actTS, "actTS", 16, PS[1], 2)
        S.op("dve", I("tensor_tensor", out=tmp, in0=PS[1], in1=gt[s][:, 0:1024], op=ALU.mult),
             r=[("bank", 2), ("bank", 3), ("gt", s)], w=["tmp"])
        S.op("pool", I("tensor_tensor", out=merged, in0=merged, in1=tmp, op=ALU.add),
             r=["tmp", "merged"], w=["merged"])
        for hp in range(2):
            for hh in range(2):
                h = hp * 2 + hh
                S.op("pe", [I("matmul", out=bank(hh)[:, 0:257], lhsT=PmT[hp][:, hh * 2 + mb, :],
                              rhs=Vm1[:, mb, h, :], start=(mb == 0), stop=(mb == 1)) for mb in range(2)],
                     r=[("PmT", hp)], w=[("bank", hh)])
                S.op("dve", I("reciprocal", out=rdm[:, h:h + 1], in_=bank(hh)[:, 256:257]),
                     r=[("bank", hh)], w=[("rdm", h)])
                S.op("dve", I("tensor_scalar", out=om[:, h * 256:(h + 1) * 256], in0=bank(hh)[:, 0:256],
                              scalar1=rdm[:, h:h + 1], scalar2=None, op0=ALU.mult),
                     r=[("bank", hh), ("rdm", h)], w=["t1a"])
        S.op("pool", I("tensor_tensor", out=abfM, in0=om, in1=zm[s], op=ALU.mult),
             r=["t1a", ("zm", s)], w=["abfM"])
        transposes(abfM, "abfM", actTM, "actTM", 8, [4])
        outproj(wmo, actTM, "actTM", 8, PS[1], 2)
        S.op("dve", I("tensor_tensor", out=tmp, in0=PS[1], in1=gt[s][:, 2048:3072], op=ALU.mult),
             r=[("bank", 2), ("bank", 3), ("gt", s)], w=["tmp"])
        S.op("pool", I("tensor_tensor", out=merged, in0=merged, in1=tmp, op=ALU.add),
             r=["tmp", "merged"], w=["merged"])
        S.op("act", I("copy", out=abfA, in_=merged), r=["merged"], w=["abfA"])
        transposes(abfA, "abfA", actTA, "actTA", 8, [4])
        outproj(wo, actTA, "actTA", 8, PS[3], 6)
        S.op("act", I("activation", out=abfA, in_=PS[3], func=AF.Square, accum_out=ssf[:, 0:1]),
             r=[("bank", 6), ("bank", 7)], w=["ssf", "abfA"])
        rstd_ops(ssf, rsf, D, ["ssf"], "rsf")
        S.op("dve", I("scalar_tensor_tensor", out=xo, in0=PS[3], scalar=rsf[:, 0:1], in1=npost,
                      op0=ALU.mult, op1=ALU.mult),
             r=[("bank", 6), ("bank", 7), "rsf", "npost"], w=["t1b"])
        S.op("pool", I("tensor_tensor", out=xo, in0=xo, in1=xr[s], op=ALU.add), r=["t1b", ("xr", s)], w=["t1b"])
        S.dma(x_dst[tk, :], xo, r=["t1b"])


WNAMES = ["norm_pre", "norm_post", "w_in", "conv_w", "conv_b", "dt_bias", "a_log", "d_skip", "ssd_norm",
          "w_ssd_out", "w_attn_out", "mem_norm", "w_mem_kv", "w_mem_out", "w_out"]
_PROG = {}
FUSED = True


def _get_prog(T, NL):
    key = (T, NL)
    if key not in _PROG:
        _PROG[key] = build_program(T, NL)
    return _PROG[key]


def prep_weights(inputs):
    w = {k: np.ascontiguousarray(np.asarray(inputs[k], dtype=np.float32)) for k in WNAMES}
    nl = w["conv_w"].shape[0]
    cw = w["conv_w"].reshape(nl, 4, 32, 128).transpose(0, 3, 2, 1)
    w["conv_w"] = np.ascontiguousarray(cw.reshape(nl, 128, 128))
    cb = w["conv_b"].reshape(nl, 32, 128).transpose(0, 2, 1)
    w["conv_b"] = np.ascontiguousarray(cb)
    return w


def kernel(**inputs):
    x = np.ascontiguousarray(np.asarray(inputs["x"], dtype=np.float32))
    mem = np.ascontiguousarray(np.asarray(inputs["mem"], dtype=np.float32))
    B, T, _ = x.shape
    consts = make_consts()
    w = prep_weights(inputs)
    depth = w["w_in"].shape[0]
    if FUSED:
        nc = _get_prog(T, depth)
        in_maps = []
        for b in range(B):
            m = {"x": x[b], "mem": mem[b], "consts": consts}
            m.update(w)
            in_maps.append(m)
        res = run_bass_kernel_spmd(nc, in_maps, core_ids=list(range(B)))
        return np.stack([np.asarray(r["out"]) for r in res.results], axis=0).astype(np.float32)
    cur = [x[b] for b in range(B)]
    nc = _get_prog(T, 1)
    for l in range(depth):
        in_maps = []
        for b in range(B):
            m = {"x": cur[b], "mem": mem[b], "consts": consts}
            m.update({k: w[k][l:l + 1] for k in WNAMES})
            in_maps.append(m)
        res = run_bass_kernel_spmd(nc, in_maps, core_ids=list(range(B)))
        cur = [np.ascontiguousarray(np.asarray(r["out"], dtype=np.float32)) for r in res.results]
    return np.stack(cur, axis=0).astype(np.float32)
```

```python
import numpy as np
import concourse.bass as bass
import concourse.mybir as mybir
from concourse.bass_utils import run_bass_kernel_spmd

F32 = mybir.dt.float32
BF16 = mybir.dt.bfloat16
AF = mybir.ActivationFunctionType
ALU = mybir.AluOpType

D = 1024
DEPTH = 2
EPS = 1e-6
D_INNER = 2048
NH = 32
NG = 8
DS = 128
MEM_LEN = 256
OFF_ZSSD = 0
OFF_XBC = 2048
OFF_DT = 6144
OFF_QKV = 6176
OFF_ZATT = OFF_QKV + 9216
OFF_QMEM = OFF_ZATT + 1024
OFF_ZMEM = OFF_QMEM + 1024
OFF_GATE = OFF_ZMEM + 1024
N_IN = OFF_GATE + 3072
DIL = (1, 4, 16)
N_DMA_SEMS = 24


class Sched:
    ENGS = ("pe", "act", "dve", "pool", "sp")

    def __init__(self):
        self.ops = []

    def op(self, eng, instrs, r=(), w=()):
        if isinstance(instrs, tuple):
            instrs = [instrs]
        instrs = list(instrs)

        def fn(e, instrs=instrs):
            ins = None
            for m, kw in instrs:
                ins = getattr(e, m)(**kw)
            return ins
        r, w = list(r), list(w)
        for k in r:
            if isinstance(k, tuple) and k and k[0] == "bank" and k not in w:
                w.append(k)
        self.ops.append(dict(eng=eng, fn=fn, r=tuple(r), w=tuple(w), dma=False))

    def dma(self, out, in_, r=(), w=(), eng="sp"):
        def fn(e, out=out, in_=in_):
            return e.dma_start(out=out, in_=in_)
        self.ops.append(dict(eng=eng, fn=fn, r=tuple(r), w=tuple(w), dma=True))

    def barrier(self):
        self.ops.append(dict(barrier=True))

    def emit(self, nc):
        ops = self.ops
        n = len(ops)
        last_w = {}
        readers = {}
        deps = [None] * n
        dma_sem_of = [None] * n
        dma_rr = 0
        dma_last_use = [None] * N_DMA_SEMS
        bar_pending = {}
        last_op_of = {}
        outstanding = set()
        for i, o in enumerate(ops):
            if o.get("barrier"):
                allprev = set(outstanding)
                for e in self.ENGS:
                    bar_pending[e] = bar_pending.get(e, set()) | allprev
                outstanding = set()
                last_w.clear()
                readers.clear()
                continue
            d = set()
            for k in o["r"]:
                if k in last_w:
                    d.add(last_w[k])
            for k in o["w"]:
                if k in last_w:
                    d.add(last_w[k])
                for rr in readers.get(k, ()):
                    d.add(rr)
            if o["eng"] in bar_pending and bar_pending[o["eng"]]:
                d |= bar_pending[o["eng"]]
                bar_pending[o["eng"]] = set()
            if o["dma"]:
                j = dma_rr % N_DMA_SEMS
                dma_rr += 1
                dma_sem_of[i] = j
                if dma_last_use[j] is not None:
                    d.add(dma_last_use[j])
                dma_last_use[j] = i
            d.discard(i)
            deps[i] = d
            for k in o["r"]:
                readers.setdefault(k, []).append(i)
            for k in o["w"]:
                last_w[k] = i
                readers[k] = []
            if o["dma"]:
                outstanding.add(i)
            else:
                prev = last_op_of.get(o["eng"])
                if prev is not None:
                    outstanding.discard(prev)
                last_op_of[o["eng"]] = i
                outstanding.add(i)
        self.final_wait = set(outstanding) | bar_pending.get("sp", set())
        needed = [False] * n
        for i, o in enumerate(ops):
            if o.get("barrier"):
                continue
            for dd in deps[i]:
                if ops[dd]["eng"] == "pe" and o["eng"] == "pe" and not ops[dd]["dma"]:
                    continue
                needed[dd] = True
        for dd in self.final_wait:
            needed[dd] = True
        cnt = {e: 0 for e in self.ENGS}
        dcnt = [0] * N_DMA_SEMS
        event = [None] * n
        for i, o in enumerate(ops):
            if o.get("barrier"):
                continue
            if o["dma"]:
                j = dma_sem_of[i]
                dcnt[j] += 16
                event[i] = (("d", j), dcnt[j])
            elif needed[i]:
                cnt[o["eng"]] += 1
                event[i] = (("e", o["eng"]), cnt[o["eng"]])
        self.stats = (dict(cnt), max(dcnt), n)
        import contextlib
        with contextlib.ExitStack() as st:
            esem = {e: st.enter_context(nc.semaphore("s_" + e)) for e in self.ENGS}
            dsem = [st.enter_context(nc.semaphore("d_%d" % j)) for j in range(N_DMA_SEMS)]
            block = st.enter_context(nc.Block())

            def sem_of(key):
                return esem[key[1]] if key[0] == "e" else dsem[key[1]]

            def run(engname, eng):
                known = {}
                for i, o in enumerate(ops):
                    if o.get("barrier") or o["eng"] != engname:
                        continue
                    waits = {}
                    for dd in deps[i]:
                        if ops[dd]["eng"] == "pe" and engname == "pe" and not ops[dd]["dma"]:
                            continue
                        key, val = event[dd]
                        if known.get(key, 0) >= val:
                            continue
                        waits[key] = max(waits.get(key, 0), val)
                    for key, val in waits.items():
                        eng.wait_ge(sem_of(key), val)
                        known[key] = val
                    ins = o["fn"](eng)
                    if event[i] is not None:
                        key, val = event[i]
                        ins.then_inc(sem_of(key), 16 if key[0] == "d" else 1)
                if engname == "sp":
                    waits = {}
                    for dd in self.final_wait:
                        key, val = event[dd]
                        if known.get(key, 0) >= val:
                            continue
                        waits[key] = max(waits.get(key, 0), val)
                    for key, val in waits.items():
                        eng.wait_ge(sem_of(key), val)

            @block.tensor
            def _(e):
                run("pe", e)

            @block.scalar
            def _(e):
                run("act", e)

            @block.vector
            def _(e):
                run("dve", e)

            @block.gpsimd
            def _(e):
                run("pool", e)

            @block.sync
            def _(e):
                run("sp", e)


SB_BASE = 16512
SB_LIMIT = 229376 - 256


class SBAlloc:
    def __init__(self, nc):
        self.nc = nc
        self.cur = SB_BASE
        self.n = 0

    def alloc(self, shape, dt):
        nb = 1
        for s in shape[1:]:
            nb *= s
        nb *= 4 if dt == F32 else 2
        off = self.cur
        self.cur += (nb + 63) // 64 * 64
        assert self.cur <= SB_LIMIT, ("SBUF overflow", self.cur)
        self.n += 1
        return self.nc.alloc_sbuf_tensor_at("sb%d" % self.n, list(shape), dt, offset=off).ap()


def I(m, **kw):
    return (m, kw)


def make_consts():
    p = np.arange(128)[:, None]
    j = np.arange(128)[None, :]
    ident = (p == j)
    U = (p <= j)
    Ls = (p > j)
    ones = np.ones((128, 128), bool)
    Ge = (p >= j)
    return np.concatenate([ident, U, Ls, ones, Ge], axis=1).astype(np.float32)


def build_program(T, NL, dbg=()):
    nc = bass.Bass("TRN2", target_bir_lowering=False)
    S = Sched()
    NT = T // 128
    HALF = min(T, 4096)
    NHALF = T // HALF
    NTT = HALF // 512

    def dram(name, shape, dt, kind="Internal"):
        if name in dbg:
            kind = "ExternalOutput"
        return nc.dram_tensor(name, list(shape), dt, kind=kind).ap()

    x_in = dram("x", [T, D], F32, "ExternalInput")
    mem_in = dram("mem", [MEM_LEN, D], F32, "ExternalInput")
    consts_in = dram("consts", [128, 640], F32, "ExternalInput")
    P = {}
    for name, shp in [("norm_pre", [NL, D]), ("norm_post", [NL, D]), ("w_in", [NL, D, N_IN]),
                      ("conv_w", [NL, 128, 128]), ("conv_b", [NL, 128, 32]), ("dt_bias", [NL, NH]),
                      ("a_log", [NL, NH]), ("d_skip", [NL, NH]), ("ssd_norm", [NL, D_INNER]),
                      ("w_ssd_out", [NL, D_INNER, D]), ("w_attn_out", [NL, D, D]),
                      ("mem_norm", [NL, D]), ("w_mem_kv", [NL, D, 2 * D]),
                      ("w_mem_out", [NL, D, D]), ("w_out", [NL, D, D])]:
        P[name] = dram(name, shp, F32, "ExternalInput")
    out = dram("out", [T, D], F32, "ExternalOutput")
    xmid = [dram("xmid%d" % i, [T, D], F32) for i in range(NL - 1)]
    XS = dram("XS", [T, 2048], BF16)
    BTOK = dram("BTOK", [T, 1024], BF16)
    BCT = dram("BCT", [2048, T], BF16)
    Z1 = dram("Z1", [T, 2048], BF16)
    QT = [dram("QT%d" % g, [1024, T], BF16) for g in range(3)]
    KT = [dram("KT%d" % g, [1024, T], BF16) for g in range(3)]
    V = [dram("V%d" % g, [T, 1024], BF16) for g in range(3)]
    ZA = dram("ZA", [T, 1024], BF16)
    QMT = dram("QMT", [1024, T], BF16)
    ZM = dram("ZM", [T, 1024], BF16)
    G = dram("G", [T, 3072], BF16)
    YS = dram("YS", [T, 2048], BF16)
    OA = [dram("OA%d" % g, [T, 1040], F32) for g in range(3)]

    sb = SBAlloc(nc)
    PS = [nc.alloc_psum_tensor("ps%d" % i, [128, 1024], F32).ap() for i in range(4)]

    def bank(i):
        return PS[i // 2][:, (i % 2) * 512:(i % 2) * 512 + 512]

    def bank_bf(i):
        return bank(i).bitcast(BF16)

    cst = sb.alloc([128, 640], F32)
    ident_bf = sb.alloc([128, 128], BF16)
    mask2 = sb.alloc([128, 2, 128], BF16)
    ones_bf = sb.alloc([128, 128], BF16)
    persist_mark0 = sb.cur
    DTs = sb.alloc([128, NT, 32], F32)
    LAs = sb.alloc([128, NT, 32], F32)
    halo = sb.alloc([128, 32, 3], F32)
    ident_f = cst[:, 0:128]
    U_f = cst[:, 128:256]
    Ls_f = cst[:, 256:384]
    ones_f = cst[:, 384:512]
    Ge_f = cst[:, 512:640]
    S.dma(cst, consts_in, w=["cst"])
    S.op("dve", I("tensor_copy", out=ident_bf, in_=ident_f), r=["cst"], w=["ident"])
    S.op("dve", I("tensor_copy", out=mask2[:, 0, :], in_=Ge_f), r=["cst"], w=["mask2a"])
    S.op("dve", I("tensor_copy", out=mask2[:, 1, :], in_=U_f), r=["cst"], w=["mask2b"])
    S.op("dve", I("tensor_copy", out=ones_bf, in_=ones_f), r=["cst"], w=["onesbf"])
    persist_mark = sb.cur

    def bcast_load(dst, src_row, key):
        S.dma(dst, src_row.partition_broadcast(128), w=[key])

    def rstd_ops(ss, rstd, n, rkeys, wkey):
        S.op("dve", I("tensor_scalar", out=rstd, in0=ss, scalar1=1.0 / n, scalar2=EPS,
                      op0=ALU.mult, op1=ALU.add), r=rkeys, w=[wkey])
        S.op("act", I("activation", out=rstd, in_=rstd, func=AF.Ln), r=[wkey], w=[wkey])
        S.op("act", I("activation", out=rstd, in_=rstd, func=AF.Exp, scale=-0.5), r=[wkey], w=[wkey])

    C = dict(locals())
    for l in range(NL):
        x_src = x_in if l == 0 else xmid[l - 1]
        x_dst = out if l == NL - 1 else xmid[l]
        S.barrier()
        sb.cur = persist_mark
        gpre = sb.alloc([128, D], F32)
        convw = sb.alloc([128, 32, 4], F32)
        convb = sb.alloc([128, 32], F32)
        dtb = sb.alloc([128, 32], F32)
        abc = sb.alloc([128, 32], F32)
        bcast_load(gpre, P["norm_pre"][l:l + 1, :], "gpre")
        S.dma(convw, P["conv_w"][l].rearrange("p (b k) -> p b k", k=4), w=["convw"])
        S.dma(convb, P["conv_b"][l], w=["convb"])
        bcast_load(dtb, P["dt_bias"][l:l + 1, :], "dtb")
        bcast_load(abc, P["a_log"][l:l + 1, :], "abc0")
        S.op("act", I("activation", out=abc, in_=abc, func=AF.Exp), r=["abc0"], w=["abc0"])
        S.op("act", I("mul", out=abc, in_=abc, mul=-1.0), r=["abc0"], w=["abc"])
        S.op("pool", I("memset", ap=halo, constant=0.0), w=["halo"])
        projmark = sb.cur
        for hf in range(NHALF):
            sb.cur = projmark
            t0h = hf * HALF
            hT = sb.alloc([128, 8, HALF], BF16)
            nmark = sb.cur
            xin = [sb.alloc([128, D], F32) for _ in range(2)]
            xn = [sb.alloc([128, D], BF16) for _ in range(2)]
            junk = sb.alloc([128, D], BF16)
            ss = [sb.alloc([128, 1], F32) for _ in range(2)]
            rs = [sb.alloc([128, 1], F32) for _ in range(2)]
            for i in range(HALF // 128):
                s_ = i % 2
                tok = t0h + i * 128
                S.dma(xin[s_], x_src[tok:tok + 128, :], w=[("xin", s_)])
                S.op("act", I("activation", out=junk, in_=xin[s_], func=AF.Square,
                              accum_out=ss[s_][:, 0:1]),
                     r=[("xin", s_)], w=[("ss", s_), "junk"])
                rstd_ops(ss[s_], rs[s_], D, [("ss", s_)], ("rs", s_))
                S.op("dve", I("scalar_tensor_tensor", out=xn[s_], in0=xin[s_], scalar=rs[s_][:, 0:1],
                              in1=gpre, op0=ALU.mult, op1=ALU.mult),
                     r=[("xin", s_), ("rs", s_), "gpre"], w=[("xn", s_)])
                pb = 6 + s_
                S.op("pe", [I("transpose", out=bank_bf(pb)[:, k * 128:(k + 1) * 128],
                              in_=xn[s_][:, k * 128:(k + 1) * 128], identity=ident_bf)
                            for k in range(8)],
                     r=[("xn", s_), "ident"], w=[("bank", pb)])
                S.op("act", I("copy", out=hT[:, :, i * 128:(i + 1) * 128],
                              in_=bank_bf(pb).rearrange("p (k t) -> p k t", k=8)),
                     r=[("bank", pb)], w=[("hT", i // 4)])
            S.barrier()
            sb.cur = nmark
            C.update(locals())
            phaseP(C)
        S.barrier()
        sb.cur = persist_mark
        C.update(locals())
        for _ in phaseA(C):
            pass
        S.barrier()
        sb.cur = persist_mark
        for _ in phaseS(C):
            pass
        S.barrier()
        sb.cur = persist_mark
        phaseF(C)
    S.emit(nc)
    return nc


def phaseP(C):
    S, sb, l, hT = C["S"], C["sb"], C["l"], C["hT"]
    NTT, t0h, hf, NHALF, HALF = C["NTT"], C["t0h"], C["hf"], C["NHALF"], C["HALF"]
    bank, bank_bf, ident_bf = C["bank"], C["bank_bf"], C["ident_bf"]
    convw, convb, halo, dtb, abc = C["convw"], C["convb"], C["halo"], C["dtb"], C["abc"]
    DTs, LAs = C["DTs"], C["LAs"]
    W = C["P"]["w_in"][l]
    wst = [sb.alloc([128, 8, 512], F32) for _ in range(2)]
    wbf = [sb.alloc([128, 8, 512], BF16) for _ in range(2)]
    stage = [sb.alloc([128, 4, 512], BF16) for _ in range(2)]
    U = [[sb.alloc([128, 515], F32) for _ in range(2)] for _ in range(4)]
    acc = [[sb.alloc([128, 512], F32) for _ in range(4)] for _ in range(2)]
    xc = [sb.alloc([128, 4, 512], BF16) for _ in range(2)]
    stT = [sb.alloc([128, 4, 512], BF16) for _ in range(2)]
    wdt = sb.alloc([128, 8, 32], F32)
    wdtb = sb.alloc([128, 8, 32], BF16)
    dtmp = sb.alloc([128, 16, 32], F32)

    blocks = []
    for j in range(4):
        blocks.append((OFF_ZSSD + j * 512, "tm", C["Z1"], j * 512, AF.Silu))
    for j in range(8):
        blocks.append((OFF_XBC + j * 512, "xbc", None, j * 4, None))
    for g in range(3):
        for j in range(2):
            blocks.append((OFF_QKV + (0 * 3 + g) * 1024 + j * 512, "fm", C["QT"][g], j * 512, None))
        for j in range(2):
            blocks.append((OFF_QKV + (1 * 3 + g) * 1024 + j * 512, "fm", C["KT"][g], j * 512, None))
        for j in range(2):
            blocks.append((OFF_QKV + (2 * 3 + g) * 1024 + j * 512, "tm", C["V"][g], j * 512, AF.Copy))
    for j in range(2):
        blocks.append((OFF_ZATT + j * 512, "tm", C["ZA"], j * 512, AF.Silu))
    for j in range(2):
        blocks.append((OFF_QMEM + j * 512, "fm", C["QMT"], j * 512, None))
    for j in range(2):
        blocks.append((OFF_ZMEM + j * 512, "tm", C["ZM"], j * 512, AF.Silu))
    for j in range(6):
        blocks.append((OFF_GATE + j * 512, "tm", C["G"], j * 512, AF.Sigmoid))

    S.dma(wdt, W[:, OFF_DT:OFF_DT + 32].rearrange("(kc p) n -> p kc n", p=128), w=["wdt"])
    S.op("dve", I("tensor_copy", out=wdtb, in_=wdt), r=["wdt"], w=["wdtb"])
    ntile = HALF // 128
    grp = min(16, ntile)
    for g0 in range(0, ntile, grp):
        bk = 0
        for i in range(g0, g0 + grp):
            S.op("pe", [I("matmul", out=bank(bk)[:, (i - g0) * 32:(i - g0) * 32 + 32],
                          lhsT=hT[:, kc, i * 128:(i + 1) * 128], rhs=wdtb[:, kc, :],
                          start=(kc == 0), stop=(kc == 7)) for kc in range(8)],
                 r=[("hT", i // 4), "wdtb"], w=[("bank", bk)])
        c0 = t0h // 128 + g0
        S.op("dve", I("tensor_tensor", out=dtmp[:, 0:grp, :],
                      in0=bank(bk)[:, 0:grp * 32].rearrange("p (n e) -> p n e", e=32),
                      in1=dtb.unsqueeze(1).to_broadcast([128, grp, 32]), op=ALU.add),
             r=[("bank", bk), "dtb"], w=["dtmp"])
        S.op("act", I("activation", out=dtmp[:, 0:grp, :], in_=dtmp[:, 0:grp, :], func=AF.Exp),
             r=["dtmp"], w=["dtmp"])
        S.op("act", I("activation", out=DTs[:, c0:c0 + grp, :], in_=dtmp[:, 0:grp, :], func=AF.Ln,
                      bias=1.0, scale=1.0),
             r=["dtmp"], w=[("DTs", c0)])
        S.op("dve", I("tensor_tensor", out=LAs[:, c0:c0 + grp, :], in0=DTs[:, c0:c0 + grp, :],
                      in1=abc.unsqueeze(1).to_broadcast([128, grp, 32]), op=ALU.mult),
             r=[("DTs", c0), "abc"], w=[("LAs", c0)])

    def load_w(bi):
        c0 = blocks[bi][0]
        sl = bi % 2
        S.dma(wst[sl], W[:, c0:c0 + 512].rearrange("(kc p) n -> p kc n", p=128), w=[("wst", sl)])
        S.op("pool", I("tensor_copy", out=wbf[sl], in_=wst[sl]), r=[("wst", sl)], w=[("wbf", sl)])

    load_w(0)
    bkrr = [1]
    cnt = [0]
    for bi, (c0, kind, dst, d0, func) in enumerate(blocks):
        if bi + 1 < len(blocks):
            load_w(bi + 1)
        sl = bi % 2
        if kind == "xbc":
            def part1(tt, sl=sl, d0=d0):
                par = tt % 2
                for sub in range(4):
                    bk = bkrr[0]
                    bkrr[0] = bkrr[0] % 5 + 1
                    S.op("pe", [I("matmul", out=bank(bk), lhsT=wbf[sl][:, kc, sub * 128:(sub + 1) * 128],
                                  rhs=hT[:, kc, tt * 512:(tt + 1) * 512], start=(kc == 0), stop=(kc == 7))
                                for kc in range(8)],
                         r=[("wbf", sl), ("hT", tt)], w=[("bank", bk)])
                    chb = d0 + sub
                    Uc, Up = U[sub][par], U[sub][1 - par]
                    S.op("act", I("copy", out=Uc[:, 3:515], in_=bank(bk)),
                         r=[("bank", bk)], w=[("U", sub, par, "m")])
                    S.op("act", I("activation", out=acc[par][sub], in_=bank(bk), func=AF.Identity,
                                  scale=convw[:, chb, 3:4], bias=convb[:, chb:chb + 1]),
                         r=[("bank", bk), "convw", "convb"], w=[("acc", par, sub)])
                    if tt == 0:
                        S.op("pool", I("tensor_copy", out=Uc[:, 0:3], in_=halo[:, chb, :]),
                             r=[("halo", chb)], w=[("U", sub, par, "h")])
                    else:
                        S.op("pool", I("tensor_copy", out=Uc[:, 0:3], in_=Up[:, 512:515]),
                             r=[("U", sub, 1 - par, "m")], w=[("U", sub, par, "h")])
                    if tt == NTT - 1 and hf + 1 < NHALF:
                        S.op("pool", I("tensor_copy", out=halo[:, chb, :], in_=Uc[:, 512:515]),
                             r=[("U", sub, par, "m")], w=[("halo", chb)])

            def part2(tt, st, d0=d0):
                par = tt % 2
                tok0 = t0h + tt * 512
                for kk in (2, 1, 0):
                    for sub in range(4):
                        chb = d0 + sub
                        Uc = U[sub][par]
                        ukeys = [("U", sub, par, "m"), ("U", sub, par, "h"), "convw", "convb"]
                        S.op("dve", I("scalar_tensor_tensor", out=acc[par][sub], in0=Uc[:, kk:kk + 512],
                                      scalar=convw[:, chb, kk:kk + 1], in1=acc[par][sub], op0=ALU.mult, op1=ALU.add),
                             r=ukeys + [("acc", par, sub)], w=[("acc", par, sub)])
                for sub in range(4):
                    S.op("act", I("activation", out=xc[st][:, sub, :], in_=acc[par][sub], func=AF.Silu),
                         r=[("acc", par, sub)], w=[("xc", st, sub)])
                if d0 < 24:
                    for sub in range(4):
                        tb = 6 + (sub % 2)
                        S.op("pe", [I("transpose", out=bank_bf(tb)[:, j * 128:(j + 1) * 128],
                                      in_=xc[st][:, sub, j * 128:(j + 1) * 128], identity=ident_bf)
                                    for j in range(4)],
                             r=[("xc", st, sub), "ident"], w=[("bank", tb)])
                        eng = "act" if sub % 2 == 0 else "dve"
                        S.op(eng, I("copy" if eng == "act" else "tensor_copy",
                                    out=stT[st][:, :, sub * 128:(sub + 1) * 128],
                                    in_=bank_bf(tb)[:, 0:512].rearrange("p (j c) -> p j c", j=4)),
                             r=[("bank", tb)], w=[("stT", st, sub)])
                chb0 = d0
                if chb0 < 16:
                    S.dma(C["XS"][tok0:tok0 + 512, chb0 * 128:chb0 * 128 + 512].rearrange("(j p) c -> p j c", p=128),
                          stT[st], r=[("stT", st, s_) for s_ in range(4)])
                elif chb0 < 24:
                    cc = (chb0 - 16) * 128
                    S.dma(C["BTOK"][tok0:tok0 + 512, cc:cc + 512].rearrange("(j p) c -> p j c", p=128),
                          stT[st], r=[("stT", st, s_) for s_ in range(4)])
                if chb0 >= 16:
                    r0 = (chb0 - 16) * 128
                    S.dma(C["BCT"][r0:r0 + 512, tok0:tok0 + 512].rearrange("(s p) t -> p s t", p=128),
                          xc[st], r=[("xc", st, s_) for s_ in range(4)])

            part1(0)
            for tt in range(NTT):
                st = cnt[0] % 2
                cnt[0] += 1
                if tt + 1 < NTT:
                    part1(tt + 1)
                part2(tt, st)
            continue
        for tt in range(NTT):
            tok0 = t0h + tt * 512
            st = cnt[0] % 2
            cnt[0] += 1
            for sub in range(4):
                bk = bkrr[0]
                bkrr[0] = bkrr[0] % 5 + 1
                if kind == "tm":
                    mm = [I("matmul", out=bank(bk), lhsT=hT[:, kc, tt * 512 + sub * 128:tt * 512 + sub * 128 + 128],
                            rhs=wbf[sl][:, kc, :], start=(kc == 0), stop=(kc == 7)) for kc in range(8)]
                else:
                    mm = [I("matmul", out=bank(bk), lhsT=wbf[sl][:, kc, sub * 128:(sub + 1) * 128],
                            rhs=hT[:, kc, tt * 512:(tt + 1) * 512], start=(kc == 0), stop=(kc == 7))
                          for kc in range(8)]
                S.op("pe", mm, r=[("wbf", sl), ("hT", tt)], w=[("bank", bk)])
                if kind == "tm":
                    S.op("act", I("activation", out=stage[st][:, sub, :], in_=bank(bk), func=func),
                         r=[("bank", bk)], w=[("stage", st, sub)])
                elif kind == "fm":
                    eng = "act" if sub % 2 == 0 else "dve"
                    S.op(eng, I("copy" if eng == "act" else "tensor_copy", out=stage[st][:, sub, :], in_=bank(bk)),
                         r=[("bank", bk)], w=[("stage", st, sub)])
                else:
                    chb = d0 + sub
                    par = tt % 2
                    Uc, Up = U[sub][par], U[sub][1 - par]
                    S.op("act", I("copy", out=Uc[:, 3:515], in_=bank(bk)),
                         r=[("bank", bk)], w=[("U", sub, par, "m")])
                    S.op("act", I("activation", out=acc[sub], in_=bank(bk), func=AF.Identity,
                                  scale=convw[:, chb, 3:4], bias=convb[:, chb:chb + 1]),
                         r=[("bank", bk), "convw", "convb"], w=[("acc", sub)])
                    if tt == 0:
                        S.op("pool", I("tensor_copy", out=Uc[:, 0:3], in_=halo[:, chb, :]),
                             r=[("halo", chb)], w=[("U", sub, par, "h")])
                    else:
                        S.op("pool", I("tensor_copy", out=Uc[:, 0:3], in_=Up[:, 512:515]),
                             r=[("U", sub, 1 - par, "m")], w=[("U", sub, par, "h")])
                    if tt == NTT - 1 and hf + 1 < NHALF:
                        S.op("pool", I("tensor_copy", out=halo[:, chb, :], in_=Uc[:, 512:515]),
                             r=[("U", sub, par, "m")], w=[("halo", chb)])
            if kind == "xbc":
                par = tt % 2
                for kk in (2, 1, 0):
                    for sub in range(4):
                        chb = d0 + sub
                        Uc = U[sub][par]
                        ukeys = [("U", sub, par, "m"), ("U", sub, par, "h"), "convw", "convb"]
                        if kk == 3:
                            S.op("dve", I("tensor_scalar", out=acc[sub], in0=Uc[:, 3:515], scalar1=convw[:, chb, 3:4],
                                          scalar2=convb[:, chb:chb + 1], op0=ALU.mult, op1=ALU.add),
                                 r=ukeys, w=[("acc", sub)])
                        else:
                            S.op("dve", I("scalar_tensor_tensor", out=acc[sub], in0=Uc[:, kk:kk + 512],
                                          scalar=convw[:, chb, kk:kk + 1], in1=acc[sub], op0=ALU.mult, op1=ALU.add),
                                 r=ukeys + [("acc", sub)], w=[("acc", sub)])
                for sub in range(4):
                    S.op("act", I("activation", out=xc[st][:, sub, :], in_=acc[sub], func=AF.Silu),
                         r=[("acc", sub)], w=[("xc", st, sub)])
                if d0 < 24:
                    for sub in range(4):
                        tb = 6 + (sub % 2)
                        S.op("pe", [I("transpose", out=bank_bf(tb)[:, j * 128:(j + 1) * 128],
                                      in_=xc[st][:, sub, j * 128:(j + 1) * 128], identity=ident_bf)
                                    for j in range(4)],
                             r=[("xc", st, sub), "ident"], w=[("bank", tb)])
                        eng = "act" if sub % 2 == 0 else "pool_no"
                        eng = "act" if sub % 2 == 0 else "dve"
                        S.op(eng, I("copy" if eng == "act" else "tensor_copy",
                                    out=stT[st][:, :, sub * 128:(sub + 1) * 128],
                                    in_=bank_bf(tb)[:, 0:512].rearrange("p (j c) -> p j c", j=4)),
                             r=[("bank", tb)], w=[("stT", st, sub)])
            if kind == "tm":
                S.dma(dst[tok0:tok0 + 512, d0:d0 + 512].rearrange("(s p) c -> p s c", p=128), stage[st],
                      r=[("stage", st, s_) for s_ in range(4)])
            elif kind == "fm":
                S.dma(dst[d0:d0 + 512, tok0:tok0 + 512].rearrange("(s p) t -> p s t", p=128), stage[st],
                      r=[("stage", st, s_) for s_ in range(4)])
            else:
                chb0 = d0
                if chb0 < 16:
                    S.dma(C["XS"][tok0:tok0 + 512, chb0 * 128:chb0 * 128 + 512].rearrange("(j p) c -> p j c", p=128),
                          stT[st], r=[("stT", st, s_) for s_ in range(4)])
                elif chb0 < 24:
                    cc = (chb0 - 16) * 128
                    S.dma(C["BTOK"][tok0:tok0 + 512, cc:cc + 512].rearrange("(j p) c -> p j c", p=128),
                          stT[st], r=[("stT", st, s_) for s_ in range(4)])
                if chb0 >= 16:
                    r0 = (chb0 - 16) * 128
                    S.dma(C["BCT"][r0:r0 + 512, tok0:tok0 + 512].rearrange("(s p) t -> p s t", p=128),
                          xc[st], r=[("xc", st, s_) for s_ in range(4)])


def phaseA(C):
    S, sb, T = C["S"], C["sb"], C["T"]
    bank, PS, mask2, ones_bf = C["bank"], C["PS"], C["mask2"], C["ones_bf"]
    NBLK = T // 128
    q2 = [sb.alloc([128, T], BF16) for _ in range(2)]
    k2 = [sb.alloc([128, T], BF16) for _ in range(2)]
    v2 = [sb.alloc([128, NBLK, 128], BF16) for _ in range(2)]
    PT = [sb.alloc([128, 2, 2, 2, 128], BF16) for _ in range(2)]
    ost = [sb.alloc([128, 8, 130], F32) for _ in range(2)]
    mask4 = sb.alloc([128, 2, 256], BF16)
    for j in range(2):
        S.op("pool", I("tensor_copy", out=mask4[:, j, :], in_=mask2.rearrange("p t q -> p (t q)")),
             r=["mask2a", "mask2b"], w=[("mask4", j)])
    mask4f = mask4.rearrange("p j c -> p (j c)")
    pairs = [(g, hp) for g in range(3) for hp in range(8)]
    units = []
    for pi, (g, hp) in enumerate(pairs):
        d = DIL[g]
        nb = T // d // 128
        first = True
        for r in range(d):
            for b in range(0, nb, 2):
                units.append(dict(pi=pi, g=g, hp=hp, sl=pi % 2, d=d, nb=nb, OB=min(nb, 8), r=r, b=b, first=first))
                first = False
    state = dict(ostc=0)

    def loads(pi):
        g, hp = pairs[pi]
        d = DIL[g]
        nb = T // d // 128
        sl = pi % 2
        rows = slice(hp * 128, (hp + 1) * 128)
        S.dma(q2[sl], C["QT"][g][rows, :], w=[("q2", sl)])
        S.dma(k2[sl], C["KT"][g][rows, :], w=[("k2", sl)])
        vsrc = C["V"][g][:, rows].rearrange("(b i r) c -> r i b c", i=128, r=d)
        for r in range(d):
            for c0 in range(0, nb, 8):
                c1 = min(nb, c0 + 8)
                S.dma(v2[sl][:, r * nb + c0:r * nb + c1, :], vsrc[r][:, c0:c1, :], w=[("v2", sl, r, c0)])

    def front(i):
        u = units[i]
        sl, d, r, b = u["sl"], u["d"], u["r"], u["b"]
        x = i % 2
        qS = q2[sl].rearrange("p (m r) -> p r m", r=d)
        kS = k2[sl].rearrange("p (m r) -> p r m", r=d)
        mm = []
        for hh in range(2):
            pr = slice(hh * 64, (hh + 1) * 64)
            bk = bank(2 * x + hh)
            for j in range(2):
                bb = b + j
                qb = qS[pr, r, bb * 128:(bb + 1) * 128]
                if bb > 0:
                    mm.append(I("matmul", out=bk[:, j * 256:j * 256 + 128],
                                lhsT=kS[pr, r, (bb - 1) * 128:bb * 128], rhs=qb, start=True, stop=True))
                mm.append(I("matmul", out=bk[:, j * 256 + 128:j * 256 + 256],
                            lhsT=kS[pr, r, bb * 128:(bb + 1) * 128], rhs=qb, start=True, stop=True))
        bkeys = [("bank", 2 * x), ("bank", 2 * x + 1)]
        S.op("pe", mm, r=[("q2", sl), ("k2", sl)], w=bkeys)
        lo = 0 if b > 0 else 128
        pin = PS[x].rearrange("p (h c) -> p h c", h=2)[:, :, lo:512]
        pout = PT[x].rearrange("p h j t q -> p h (j t q)")[:, :, lo:512]
        S.op("act", I("activation", out=pout, in_=pin, func=AF.Exp, scale=0.125), r=bkeys, w=[("PT", x)])
        mk = mask4f[:, lo:512].unsqueeze(1).to_broadcast([128, 2, 512 - lo])
        S.op("dve" if i % 2 == 0 else "pool", I("tensor_tensor", out=pout, in0=pout, in1=mk, op=ALU.mult),
             r=[("PT", x), ("mask4", 0), ("mask4", 1)], w=[("PT", x)])

    def back(i):
        u = units[i]
        g, hp, sl, d, nb, OB, r, b = u["g"], u["hp"], u["sl"], u["d"], u["nb"], u["OB"], u["r"], u["b"]
        x = i % 2
        ob = i % 4
        mm = []
        for hh in range(2):
            hc = slice(hh * 64, (hh + 1) * 64)
            for j in range(2):
                bb = b + j
                blk = r * nb + bb
                o_ = bank(4 + ob)[:, j * 130 + hh * 65:j * 130 + hh * 65 + 64]
                dn = bank(4 + ob)[:, j * 130 + hh * 65 + 64:j * 130 + hh * 65 + 65]
                if bb > 0:
                    mm.append(I("matmul", out=o_, lhsT=PT[x][:, hh, j, 0, :], rhs=v2[sl][:, blk - 1, hc],
                                start=True, stop=False))
                mm.append(I("matmul", out=o_, lhsT=PT[x][:, hh, j, 1, :], rhs=v2[sl][:, blk, hc],
                            start=(bb == 0), stop=True))
                if bb > 0:
                    mm.append(I("matmul", out=dn, lhsT=PT[x][:, hh, j, 0, :], rhs=ones_bf[:, 0:1],
                                start=True, stop=False))
                mm.append(I("matmul", out=dn, lhsT=PT[x][:, hh, j, 1, :], rhs=ones_bf[:, 0:1],
                            start=(bb == 0), stop=True))
        vkeys = [("v2", sl, r, c0) for c0 in range(0, nb, 8)]
        S.op("pe", mm, r=[("PT", x), "onesbf"] + vkeys, w=[("bank", 4 + ob)])
        os_ = state["ostc"] % 2
        eng = "act" if i % 2 == 0 else "dve"
        S.op(eng, I("copy" if eng == "act" else "tensor_copy",
                    out=ost[os_][:, b % OB:b % OB + 2, :].rearrange("p j c -> p (j c)"),
                    in_=bank(4 + ob)[:, 0:260]),
             r=[("bank", 4 + ob)], w=[("ost", os_, b % OB)])
        if (b + 1) % OB == OB - 1:
            b0 = b + 1 - (OB - 1)
            odst = C["OA"][g][:, hp * 130:(hp + 1) * 130].rearrange("(b i r) c -> r i b c", i=128, r=d)
            S.dma(odst[r][:, b0:b0 + OB, :], ost[os_][:, 0:OB, :], r=[("ost", os_, j) for j in range(0, OB, 2)])
            state["ostc"] += 1

    n = len(units)
    loads(0)
    front(0)
    for i in range(n):
        if i + 1 < n:
            front(i + 1)
        back(i)
        if units[i]["first"] and units[i]["pi"] + 1 < len(pairs):
            loads(units[i]["pi"] + 1)
    yield


def phaseS(C):
    S, sb, T, l, NT = C["S"], C["sb"], C["T"], C["l"], C["NT"]
    bank, LAs, DTs = C["bank"], C["LAs"], C["DTs"]
    U_f, Ls_f, ones_f = C["U_f"], C["Ls_f"], C["ones_f"]
    H = sb.alloc([128, 2048], F32)
    Hbf = sb.alloc([128, 2048], BF16)
    dbc = sb.alloc([128, 32], F32)
    xs_t = [sb.alloc([128, 2048], BF16) for _ in range(2)]
    b_t = [sb.alloc([128, 1024], BF16) for _ in range(2)]
    bc4 = [sb.alloc([128, 16, 256], BF16) for _ in range(2)]
    ex = [sb.alloc([128, 96], F32) for _ in range(2)]
    xds = [sb.alloc([128, 32, 64], BF16) for _ in range(2)]
    xdt = [sb.alloc([128, 32, 64], BF16) for _ in range(2)]
    dx = [sb.alloc([128, 32, 64], BF16) for _ in range(2)]
    ybf = [sb.alloc([128, 2048], BF16) for _ in range(2)]
    cbm = [sb.alloc([128, 128], F32) for _ in range(2)]
    lseg = [sb.alloc([128, 4, 128], F32) for _ in range(2)]
    dec = [sb.alloc([128, 4, 128], F32) for _ in range(2)]
    MT = [sb.alloc([128, 4, 128], BF16) for _ in range(2)]
    tt_ = [sb.alloc([128, 4, 64], F32) for _ in range(2)]
    S.dma(dbc, C["P"]["d_skip"][l:l + 1, :].partition_broadcast(128), w=["dbc"])
    S.op("pool", I("memset", ap=H, constant=0.0), w=[("H", g) for g in range(8)])
    S.op("pool", I("memset", ap=Hbf, constant=0.0), w=[("Hbf", g) for g in range(8)])

    def loads(c):
        s = c % 2
        S.dma(xs_t[s], C["XS"][c * 128:(c + 1) * 128, :], w=[("xs_t", s)])
        S.dma(b_t[s], C["BTOK"][c * 128:(c + 1) * 128, :], w=[("b_t", s)])
        if c % 2 == 0:
            s4 = (c // 2) % 2
            S.dma(bc4[s4], C["BCT"][:, c * 128:c * 128 + 256].rearrange("(j p) t -> p j t", p=128),
                  w=[("bc4", s4)])

    def pre(c):
        s = c % 2
        la = LAs[:, c, :]
        S.op("pe", [I("matmul", out=bank(0)[:, 0:32], lhsT=U_f, rhs=la, start=True, stop=True),
                    I("matmul", out=bank(0)[:, 32:64], lhsT=Ls_f, rhs=la, start=True, stop=True),
                    I("matmul", out=bank(0)[:, 64:96], lhsT=ones_f, rhs=la, start=True, stop=True)],
             r=["cst", ("LAs", c)], w=[("bank", 0)])
        S.op("act", I("activation", out=ex[s], in_=bank(0)[:, 0:96], func=AF.Exp),
             r=[("bank", 0)], w=[("ex", s)])
        S.op("pool", I("tensor_tensor", out=xdt[s], in0=xs_t[s].rearrange("p (h e) -> p h e", e=64),
                       in1=DTs[:, c, :].unsqueeze(2).to_broadcast([128, 32, 64]), op=ALU.mult),
             r=[("xs_t", s), ("DTs", c)], w=[("xdt", s)])
        S.op("dve", I("tensor_tensor", out=xds[s], in0=xdt[s],
                      in1=ex[s][:, 32:64].unsqueeze(2).to_broadcast([128, 32, 64]), op=ALU.mult),
             r=[("xdt", s), ("ex", s)], w=[("xds", s)])
        S.op("pool", I("tensor_tensor", out=dx[s], in0=xs_t[s].rearrange("p (h e) -> p h e", e=64),
                       in1=dbc.unsqueeze(2).to_broadcast([128, 32, 64]), op=ALU.mult),
             r=[("xs_t", s), "dbc"], w=[("dx", s)])

    def front(c, g):
        x = (c * 8 + g) % 2
        s4 = (c // 2) % 2
        la = LAs[:, c, :]
        tk = slice((c % 2) * 128, (c % 2 + 1) * 128)
        BT = bc4[s4][:, g, tk]
        CT = bc4[s4][:, 8 + g, tk]
        hs = slice(g * 4, (g + 1) * 4)
        cbp = bank(1)[:, 0:128]
        S.op("pe", I("matmul", out=cbp, lhsT=BT, rhs=CT, start=True, stop=True),
             r=[("bc4", s4)], w=[("bank", 1)])
        S.op("dve", I("tensor_tensor", out=cbm[x], in0=cbp, in1=U_f, op=ALU.mult),
             r=[("bank", 1), "cst"], w=[("cbm", x)])
        S.op("pool", I("tensor_tensor", out=lseg[x], in0=Ls_f.unsqueeze(1).to_broadcast([128, 4, 128]),
                       in1=la[:, hs].unsqueeze(2).to_broadcast([128, 4, 128]), op=ALU.mult),
             r=["cst", ("LAs", c)], w=[("lseg", x)])
        S.op("pe", [I("matmul", out=bank(2 + x)[:, e * 128:(e + 1) * 128], lhsT=lseg[x][:, e, :], rhs=U_f,
                      start=True, stop=True) for e in range(4)],
             r=[("lseg", x), "cst"], w=[("bank", 2 + x)])
        S.op("act", I("activation", out=dec[x], in_=bank(2 + x).rearrange("p (e l) -> p e l", e=4),
                      func=AF.Exp),
             r=[("bank", 2 + x)], w=[("dec", x)])
        S.op("dve", I("tensor_tensor", out=MT[x], in0=dec[x],
                      in1=cbm[x].unsqueeze(1).to_broadcast([128, 4, 128]), op=ALU.mult),
             r=[("dec", x), ("cbm", x)], w=[("MT", x)])

    def back(c, g):
        x = (c * 8 + g) % 2
        s = c % 2
        s4 = (c // 2) % 2
        tk = slice((c % 2) * 128, (c % 2 + 1) * 128)
        CT = bc4[s4][:, 8 + g, tk]
        hs = slice(g * 4, (g + 1) * 4)
        cs = slice(g * 256, (g + 1) * 256)
        mm = [I("matmul", out=bank(4 + x)[:, e * 64:(e + 1) * 64], lhsT=MT[x][:, e, :],
                rhs=xdt[s][:, g * 4 + e, :], start=True, stop=True)
              for e in range(4)]
        mm.append(I("matmul", out=bank(4 + x)[:, 256:512], lhsT=CT, rhs=Hbf[:, cs], start=True, stop=True))
        S.op("pe", mm, r=[("MT", x), ("xdt", s), ("bc4", s4), ("Hbf", g)], w=[("bank", 4 + x)])
        S.op("dve", I("tensor_tensor", out=tt_[x],
                      in0=bank(4 + x)[:, 256:512].rearrange("p (e q) -> p e q", e=4),
                      in1=ex[s][:, hs].unsqueeze(2).to_broadcast([128, 4, 64]), op=ALU.mult),
             r=[("bank", 4 + x), ("ex", s)], w=[("tt", x)])
        S.op("pool", I("tensor_tensor", out=tt_[x], in0=tt_[x], in1=dx[s][:, hs, :], op=ALU.add),
             r=[("tt", x), ("dx", s)], w=[("tt", x)])
        S.op("dve", I("tensor_tensor", out=ybf[s][:, cs].rearrange("p (e q) -> p e q", e=4),
                      in0=bank(4 + x)[:, 0:256].rearrange("p (e q) -> p e q", e=4), in1=tt_[x], op=ALU.add),
             r=[("bank", 4 + x), ("tt", x)], w=[("ybf", s, g)])
        php = bank(6 + x)[:, 0:256]
        S.op("pe", I("matmul", out=php, lhsT=b_t[s][:, g * 128:(g + 1) * 128],
                     rhs=xds[s][:, hs, :].rearrange("p h e -> p (h e)"), start=True, stop=True),
             r=[("b_t", s), ("xds", s)], w=[("bank", 6 + x)])
        Hg = H[:, cs].rearrange("p (e q) -> p e q", e=4)
        S.op("pool", I("tensor_tensor", out=Hg, in0=Hg,
                       in1=ex[s][:, 64 + g * 4:64 + (g + 1) * 4].unsqueeze(2).to_broadcast([128, 4, 64]),
                       op=ALU.mult),
             r=[("H", g), ("ex", s)], w=[("H", g)])
        S.op("dve", I("tensor_tensor", out=H[:, cs], in0=H[:, cs], in1=php, op=ALU.add),
             r=[("H", g), ("bank", 6 + x)], w=[("H", g)])
        S.op("act", I("copy", out=Hbf[:, cs], in_=H[:, cs]), r=[("H", g)], w=[("Hbf", g)])

    loads(0)
    if NT > 1:
        loads(1)
    pre(0)
    front(0, 0)
    for c in range(NT):
        for g in range(8):
            if g < 7:
                front(c, g + 1)
            elif c + 1 < NT:
                pre(c + 1)
                front(c + 1, 0)
            back(c, g)
        S.dma(C["YS"][c * 128:(c + 1) * 128, :], ybf[c % 2], r=[("ybf", c % 2, g) for g in range(8)])
        if c + 2 < NT:
            loads(c + 2)
    yield


def phaseF(C):
    S, sb, T, l, NT = C["S"], C["sb"], C["T"], C["l"], C["NT"]
    bank, bank_bf, PS, ident_bf, rstd_ops = C["bank"], C["bank_bf"], C["PS"], C["ident_bf"], C["rstd_ops"]
    P = C["P"]
    x_src, x_dst = C["x_src"], C["x_dst"]
    sb.cur = C["persist_mark0"]
    wso = sb.alloc([128, 16, 1024], BF16)
    wao = sb.alloc([128, 8, 1024], BF16)
    wmo = sb.alloc([128, 8, 1024], BF16)
    wo = sb.alloc([128, 8, 1024], BF16)
    ssdn = sb.alloc([128, 2048], F32)
    npost = sb.alloc([128, 1024], F32)
    KmT = sb.alloc([128, 8, 256], BF16)
    Vm1 = sb.alloc([128, 2, 4, 257], BF16)
    fmark = sb.cur
    wtmp = [sb.alloc([128, 4, 1024], F32) for _ in range(2)]
    wi = 0
    for (dstw, src, nkc) in ((wso, P["w_ssd_out"][l], 16), (wao, P["w_attn_out"][l], 8),
                             (wmo, P["w_mem_out"][l], 8), (wo, P["w_out"][l], 8)):
        srcv = src.rearrange("(kc p) n -> p kc n", p=128)
        for k0 in range(0, nkc, 4):
            s = wi % 2
            wi += 1
            S.dma(wtmp[s], srcv[:, k0:k0 + 4, :], w=[("wtmp", s)])
            S.op("pool" if wi % 2 else "dve", I("tensor_copy", out=dstw[:, k0:k0 + 4, :], in_=wtmp[s]),
                 r=[("wtmp", s)], w=[("wres", wi)])
    S.dma(ssdn, P["ssd_norm"][l:l + 1, :].partition_broadcast(128), w=["ssdn"])
    S.dma(npost, P["norm_post"][l:l + 1, :].partition_broadcast(128), w=["npost"])
    mnorm = sb.alloc([128, 1024], F32)
    S.dma(mnorm, P["mem_norm"][l:l + 1, :].partition_broadcast(128), w=["mnorm"])
    memT = sb.alloc([128, 8, 256], BF16)
    mx = sb.alloc([128, 1024], F32)
    mxn = sb.alloc([128, 1024], BF16)
    mss = sb.alloc([128, 1], F32)
    mrs = sb.alloc([128, 1], F32)
    for mb in range(2):
        S.dma(mx, C["mem_in"][mb * 128:(mb + 1) * 128, :], w=["mx"])
        S.op("act", I("activation", out=mxn, in_=mx, func=AF.Square, accum_out=mss[:, 0:1]),
             r=["mx"], w=["mss", "mxn"])
        rstd_ops(mss, mrs, D, ["mss"], "mrs")
        S.op("dve", I("scalar_tensor_tensor", out=mxn, in0=mx, scalar=mrs[:, 0:1], in1=mnorm,
                      op0=ALU.mult, op1=ALU.mult), r=["mx", "mrs", "mnorm"], w=["mxn"])
        S.op("pe", [I("transpose", out=bank_bf(4)[:, k * 128:(k + 1) * 128], in_=mxn[:, k * 128:(k + 1) * 128],
                      identity=ident_bf) for k in range(8)], r=["mxn", "ident"], w=[("bank", 4)])
        S.op("act", I("copy", out=memT[:, :, mb * 128:(mb + 1) * 128],
                      in_=bank_bf(4).rearrange("p (k t) -> p k t", k=8)), r=[("bank", 4)], w=[("memT", mb)])
    wkv = sb.alloc([128, 8, 512], BF16)
    wkvf = sb.alloc([128, 8, 512], F32)
    S.op("pool", I("memset", ap=Vm1, constant=1.0), w=["Vm1"])
    for cb in range(4):
        S.dma(wkvf, P["w_mem_kv"][l][:, cb * 512:(cb + 1) * 512].rearrange("(kc p) n -> p kc n", p=128),
              w=["wkvf"])
        S.op("dve", I("tensor_copy", out=wkv, in_=wkvf), r=["wkvf"], w=["wkv"])
        if cb < 2:
            for sub in range(4):
                j = cb * 4 + sub
                S.op("pe", [I("matmul", out=bank(5)[:, 0:256], lhsT=wkv[:, kc, sub * 128:(sub + 1) * 128],
                              rhs=memT[:, kc, :], start=(kc == 0), stop=(kc == 7)) for kc in range(8)],
                     r=["wkv", ("memT", 0), ("memT", 1)], w=[("bank", 5)])
                S.op("act", I("copy", out=KmT[:, j, :], in_=bank(5)[:, 0:256]), r=[("bank", 5)], w=[("KmT", j)])
        else:
            for mb in range(2):
                S.op("pe", [I("matmul", out=bank(5), lhsT=memT[:, kc, mb * 128:(mb + 1) * 128],
                              rhs=wkv[:, kc, :], start=(kc == 0), stop=(kc == 7)) for kc in range(8)],
                     r=["wkv", ("memT", 0), ("memT", 1)], w=[("bank", 5)])
                h0 = (cb - 2) * 2
                S.op("act", I("copy", out=Vm1[:, mb, h0:h0 + 2, 0:256],
                              in_=bank(5).rearrange("p (h e) -> p h e", h=2)),
                     r=[("bank", 5), "Vm1"], w=[("Vm1w", cb, mb)])
    S.barrier()
    sb.cur = fmark
    ys = [sb.alloc([128, 2048], BF16) for _ in range(2)]
    z1 = [sb.alloc([128, 2048], BF16) for _ in range(2)]
    oa = sb.alloc([128, 3, 1040], F32)
    za = [sb.alloc([128, 1024], BF16) for _ in range(2)]
    qmt = [sb.alloc([128, 8, 128], BF16) for _ in range(2)]
    zm = [sb.alloc([128, 1024], BF16) for _ in range(2)]
    gt = [sb.alloc([128, 3072], BF16) for _ in range(2)]
    xr = [sb.alloc([128, 1024], F32) for _ in range(2)]
    t1 = sb.alloc([128, 2048], F32)
    abfA = sb.alloc([128, 1024], BF16)
    abfS = sb.alloc([128, 2048], BF16)
    abfM = sb.alloc([128, 1024], BF16)
    actTA = sb.alloc([128, 8, 128], BF16)
    actTS = sb.alloc([128, 16, 128], BF16)
    actTM = sb.alloc([128, 8, 128], BF16)
    merged = sb.alloc([128, 1024], F32)
    num = sb.alloc([128, 1040], F32)
    ob_ = sb.alloc([128, 1024], F32)
    tmp = sb.alloc([128, 1024], F32)
    PmT = [sb.alloc([128, 4, 128], BF16) for _ in range(2)]
    ssg = sb.alloc([128, 8], F32)
    rs8 = sb.alloc([128, 8], F32)
    rden = sb.alloc([128, 16], F32)
    rdm = sb.alloc([128, 4], F32)
    ssf = sb.alloc([128, 1], F32)
    rsf = sb.alloc([128, 1], F32)
    om = t1[:, 0:1024]
    xo = t1[:, 1024:2048]

    def loads(i):
        s = i % 2
        tk = slice(i * 128, (i + 1) * 128)
        S.dma(ys[s], C["YS"][tk, :], w=[("ys", s)])
        S.dma(z1[s], C["Z1"][tk, :], w=[("z1", s)])
        S.dma(za[s], C["ZA"][tk, :], w=[("za", s)])
        S.dma(qmt[s], C["QMT"][:, tk].rearrange("(j p) t -> p j t", p=128), w=[("qmt", s)])
        S.dma(zm[s], C["ZM"][tk, :], w=[("zm", s)])
        S.dma(gt[s], C["G"][tk, :], w=[("gt", s)])
        S.dma(xr[s], x_src[tk, :], w=[("xr", s)])

    def transposes(src, skey, dstT, dkey, nk, banks):
        for b0 in range(0, nk, 8):
            bk = banks[b0 // 8]
            S.op("pe", [I("transpose", out=bank_bf(bk)[:, k * 128:(k + 1) * 128],
                          in_=src[:, (b0 + k) * 128:(b0 + k + 1) * 128], identity=ident_bf) for k in range(8)],
                 r=[skey, "ident"], w=[("bank", bk)])
            S.op("act", I("copy", out=dstT[:, b0:b0 + 8, :], in_=bank_bf(bk).rearrange("p (k t) -> p k t", k=8)),
                 r=[("bank", bk)], w=[(dkey, b0 // 8)])

    def outproj(wt, srcT, skey, nk, pso, okey):
        for hh in range(2):
            S.op("pe", [I("matmul", out=pso[:, hh * 512:(hh + 1) * 512], lhsT=srcT[:, kc, :],
                          rhs=wt[:, kc, hh * 512:(hh + 1) * 512], start=(kc == 0), stop=(kc == nk - 1))
                        for kc in range(nk)],
                 r=[(skey, j) for j in range((nk + 7) // 8)], w=[("bank", okey + hh)])

    loads(0)
    for i in range(NT):
        s = i % 2
        tk = slice(i * 128, (i + 1) * 128)
        if i == 0:
            for g in range(3):
                S.dma(oa[:, g, :], C["OA"][g][tk, :], w=[("oa", g)])
        if i + 1 < NT:
            loads(i + 1)
        S.op("pool", I("tensor_tensor", out=num, in0=oa[:, 0, :], in1=oa[:, 1, :], op=ALU.add),
             r=[("oa", 0), ("oa", 1)], w=["num"])
        S.op("pool", I("tensor_tensor", out=num, in0=num, in1=oa[:, 2, :], op=ALU.add),
             r=[("oa", 2), "num"], w=["num"])
        if i + 1 < NT:
            tkn = slice((i + 1) * 128, (i + 2) * 128)
            for g in range(3):
                S.dma(oa[:, g, :], C["OA"][g][tkn, :], w=[("oa", g)])
        numv = num.rearrange("p (h e) -> p h e", e=65)
        S.op("dve", I("tensor_tensor", out=t1, in0=ys[s], in1=z1[s], op=ALU.mult),
             r=[("ys", s), ("z1", s)], w=["t1a", "t1b"])
        S.op("act", [I("activation", out=abfS[:, 0:256], in_=t1[:, g * 256:(g + 1) * 256], func=AF.Square,
                       accum_out=ssg[:, g:g + 1]) for g in range(8)], r=["t1a", "t1b"], w=["ssg", "abfS"])
        S.op("dve", I("reciprocal", out=rden, in_=numv[:, :, 64]), r=["num"], w=["rden"])
        rstd_ops(ssg, rs8, 256, ["ssg"], "rs8")
        S.op("dve", I("tensor_tensor", out=ob_.rearrange("p (h e) -> p h e", e=64), in0=numv[:, :, 0:64],
                      in1=rden.unsqueeze(2).to_broadcast([128, 16, 64]), op=ALU.mult),
             r=["num", "rden"], w=["ob"])
        S.op("pool", I("tensor_tensor", out=abfA, in0=ob_, in1=za[s], op=ALU.mult),
             r=["ob", ("za", s)], w=["abfA"])
        transposes(abfA, "abfA", actTA, "actTA", 8, [4])
        for hp in range(2):
            mm = []
            for hh in range(2):
                h = hp * 2 + hh
                for mb in range(2):
                    for ec in range(2):
                        mm.append(I("matmul", out=bank(5)[:, (hh * 2 + mb) * 128:(hh * 2 + mb + 1) * 128],
                                    lhsT=KmT[:, h * 2 + ec, mb * 128:(mb + 1) * 128], rhs=qmt[s][:, h * 2 + ec, :],
                                    start=(ec == 0), stop=(ec == 1)))
            S.op("pe", mm, r=[("qmt", s)], w=[("bank", 5)])
            S.op("act", I("activation", out=PmT[hp], in_=bank(5).rearrange("p (j t) -> p j t", j=4),
                          func=AF.Exp, scale=1.0 / 16.0), r=[("bank", 5)], w=[("PmT", hp)])
        outproj(wao, actTA, "actTA", 8, PS[3], 6)
        S.op("dve", I("tensor_tensor", out=t1.rearrange("p (g e) -> p g e", g=8),
                      in0=t1.rearrange("p (g e) -> p g e", g=8),
                      in1=rs8.unsqueeze(2).to_broadcast([128, 8, 256]), op=ALU.mult),
             r=["t1a", "t1b", "rs8"], w=["t1a", "t1b"])
        S.op("pool", I("tensor_tensor", out=abfS, in0=t1, in1=ssdn, op=ALU.mult),
             r=["t1a", "t1b", "ssdn"], w=["abfS"])
        S.op("dve", I("tensor_tensor", out=merged, in0=PS[3], in1=gt[s][:, 1024:2048], op=ALU.mult),
             r=[("bank", 6), ("bank", 7), ("gt", s)], w=["merged"])
        transposes(abfS, "abfS", actTS, "actTS", 16, [0, 1])
        outproj(wso, actTS, "actTS", 16, PS[1], 2)
        S.op("dve", I("tensor_tensor", out=tmp, in0=PS[1], in1=gt[s][:, 0:1024], op=ALU.mult),
             r=[("bank", 2), ("bank", 3), ("gt", s)], w=["tmp"])
        S.op("pool", I("tensor_tensor", out=merged, in0=merged, in1=tmp, op=ALU.add),
             r=["tmp", "merged"], w=["merged"])
        for hp in range(2):
            for hh in range(2):
                h = hp * 2 + hh
                S.op("pe", [I("matmul", out=bank(hh)[:, 0:257], lhsT=PmT[hp][:, hh * 2 + mb, :],
                              rhs=Vm1[:, mb, h, :], start=(mb == 0), stop=(mb == 1)) for mb in range(2)],
                     r=[("PmT", hp)], w=[("bank", hh)])
                S.op("dve", I("reciprocal", out=rdm[:, h:h + 1], in_=bank(hh)[:, 256:257]),
                     r=[("bank", hh)], w=[("rdm", h)])
                S.op("dve", I("tensor_scalar", out=om[:, h * 256:(h + 1) * 256], in0=bank(hh)[:, 0:256],
                              scalar1=rdm[:, h:h + 1], scalar2=None, op0=ALU.mult),
                     r=[("bank", hh), ("rdm", h)], w=["t1a"])
        S.op("pool", I("tensor_tensor", out=abfM, in0=om, in1=zm[s], op=ALU.mult),
             r=["t1a", ("zm", s)], w=["abfM"])
        transposes(abfM, "abfM", actTM, "actTM", 8, [4])
        outproj(wmo, actTM, "actTM", 8, PS[1], 2)
        S.op("dve", I("tensor_tensor", out=tmp, in0=PS[1], in1=gt[s][:, 2048:3072], op=ALU.mult),
             r=[("bank", 2), ("bank", 3), ("gt", s)], w=["tmp"])
        S.op("pool", I("tensor_tensor", out=merged, in0=merged, in1=tmp, op=ALU.add),
             r=["tmp", "merged"], w=["merged"])
        S.op("act", I("copy", out=abfA, in_=merged), r=["merged"], w=["abfA"])
        transposes(abfA, "abfA", actTA, "actTA", 8, [4])
        outproj(wo, actTA, "actTA", 8, PS[3], 6)
        S.op("act", I("activation", out=abfA, in_=PS[3], func=AF.Square, accum_out=ssf[:, 0:1]),
             r=[("bank", 6), ("bank", 7)], w=["ssf", "abfA"])
        rstd_ops(ssf, rsf, D, ["ssf"], "rsf")
        S.op("dve", I("scalar_tensor_tensor", out=xo, in0=PS[3], scalar=rsf[:, 0:1], in1=npost,
                      op0=ALU.mult, op1=ALU.mult),
             r=[("bank", 6), ("bank", 7), "rsf", "npost"], w=["t1b"])
        S.op("pool", I("tensor_tensor", out=xo, in0=xo, in1=xr[s], op=ALU.add), r=["t1b", ("xr", s)], w=["t1b"])
        S.dma(x_dst[tk, :], xo, r=["t1b"])


WNAMES = ["norm_pre", "norm_post", "w_in", "conv_w", "conv_b", "dt_bias", "a_log", "d_skip", "ssd_norm",
          "w_ssd_out", "w_attn_out", "mem_norm", "w_mem_kv", "w_mem_out", "w_out"]
_PROG = {}
FUSED = True


def _get_prog(T, NL):
    key = (T, NL)
    if key not in _PROG:
        _PROG[key] = build_program(T, NL)
    return _PROG[key]


def prep_weights(inputs):
    w = {k: np.ascontiguousarray(np.asarray(inputs[k], dtype=np.float32)) for k in WNAMES}
    nl = w["conv_w"].shape[0]
    cw = w["conv_w"].reshape(nl, 4, 32, 128).transpose(0, 3, 2, 1)
    w["conv_w"] = np.ascontiguousarray(cw.reshape(nl, 128, 128))
    cb = w["conv_b"].reshape(nl, 32, 128).transpose(0, 2, 1)
    w["conv_b"] = np.ascontiguousarray(cb)
    return w


def kernel(**inputs):
    x = np.ascontiguousarray(np.asarray(inputs["x"], dtype=np.float32))
    mem = np.ascontiguousarray(np.asarray(inputs["mem"], dtype=np.float32))
    B, T, _ = x.shape
    consts = make_consts()
    w = prep_weights(inputs)
    depth = w["w_in"].shape[0]
    if FUSED:
        nc = _get_prog(T, depth)
        in_maps = []
        for b in range(B):
            m = {"x": x[b], "mem": mem[b], "consts": consts}
            m.update(w)
            in_maps.append(m)
        res = run_bass_kernel_spmd(nc, in_maps, core_ids=list(range(B)))
        return np.stack([np.asarray(r["out"]) for r in res.results], axis=0).astype(np.float32)
    cur = [x[b] for b in range(B)]
    nc = _get_prog(T, 1)
    for l in range(depth):
        in_maps = []
        for b in range(B):
            m = {"x": cur[b], "mem": mem[b], "consts": consts}
            m.update({k: w[k][l:l + 1] for k in WNAMES})
            in_maps.append(m)
        res = run_bass_kernel_spmd(nc, in_maps, core_ids=list(range(B)))
        cur = [np.ascontiguousarray(np.asarray(r["out"], dtype=np.float32)) for r in res.results]
    return np.stack(cur, axis=0).astype(np.float32)
```
